# Optimizing a Trainium2 kernel written in Bass

```python
import jax, jax.numpy as jnp
from jax import lax
import numpy as np

D_MODEL = 2048
BATCH = 2
SEQ = 4096
DEPTH = 1

ROPE_THETA = 500000.0
EPS = 1e-6
A_GROUPS = 8
A_GROUP_DIM = 128
A_WIDTH = A_GROUPS * A_GROUP_DIM
CHUNK = 128
B_HEADS = 8
B_KV_HEADS = 2
B_HEAD_DIM = 128
B_WIDTH = B_HEADS * B_HEAD_DIM
IDX_HEADS = 16
IDX_DIM = 64
TOPK_MAX = 256
Q_BLOCK = 128
M_HEADS = 4
M_HEAD_DIM = 256
M_WIDTH = M_HEADS * M_HEAD_DIM
MEM_LEN = 256

SPLIT_SIZES = (
    A_WIDTH, A_WIDTH, A_WIDTH,
    B_WIDTH, B_KV_HEADS * B_HEAD_DIM, B_KV_HEADS * B_HEAD_DIM,
    B_WIDTH,
    IDX_HEADS * IDX_DIM, IDX_DIM, IDX_HEADS,
    M_WIDTH, M_WIDTH,
    D_MODEL, D_MODEL, D_MODEL,
)
D_IN = sum(SPLIT_SIZES)
SPLIT_OFFSETS = tuple(int(o) for o in np.cumsum(SPLIT_SIZES)[:-1])

kernel_name = 'hybrid_gmlp_dsa_memxattn_gated_block'


def rms_norm(x, g):
    xf = x.astype(jnp.float32)
    y = xf * lax.rsqrt(jnp.mean(xf * xf, axis=-1, keepdims=True) + EPS)
    return (y * g.astype(jnp.float32)).astype(x.dtype)


def layer_norm(x, g, b):
    xf = x.astype(jnp.float32)
    mu = jnp.mean(xf, axis=-1, keepdims=True)
    var = jnp.mean(jnp.square(xf - mu), axis=-1, keepdims=True)
    y = (xf - mu) * lax.rsqrt(var + EPS)
    return (y * g.astype(jnp.float32) + b.astype(jnp.float32)).astype(x.dtype)


def partial_rope(x, pos):
    rot = x.shape[-1] // 4
    half = rot // 2
    inv_freq = ROPE_THETA ** (-jnp.arange(half, dtype=jnp.float32) / half)
    ang = pos.astype(jnp.float32)[..., None] * inv_freq
    cos = jnp.cos(ang)[:, :, None, :]
    sin = jnp.sin(ang)[:, :, None, :]
    x1 = x[..., :half].astype(jnp.float32)
    x2 = x[..., half:rot].astype(jnp.float32)
    r1 = (x1 * cos - x2 * sin).astype(x.dtype)
    r2 = (x2 * cos + x1 * sin).astype(x.dtype)
    return jnp.concatenate([r1, r2, x[..., rot:]], axis=-1)


def gmlp_branch(a_u, a_v, a_z, ln_g, ln_b, spatial_w, spatial_b):
    bsz, s, _ = a_u.shape
    u = jax.nn.gelu(a_u)
    v = layer_norm(jax.nn.gelu(a_v), ln_g, ln_b)
    vr = v.reshape(bsz, s // CHUNK, CHUNK, A_GROUPS, A_GROUP_DIM)
    mask = jnp.tril(jnp.ones((CHUNK, CHUNK), dtype=bool))
    ws = jnp.where(mask[None], spatial_w, jnp.zeros_like(spatial_w))
    sg = jnp.einsum('gts,bcsgd->bctgd', ws, vr) + spatial_b.T[None, None, :, :, None]
    return u * sg.reshape(bsz, s, A_WIDTH) * jax.nn.silu(a_z)


def dsa_branch(b_q, b_k, b_v, b_z, i_q, i_k, i_w, positions,
               q_norm_gain, k_norm_gain, idx_k_ln_gain, idx_k_ln_bias):
    bsz, s, _ = b_q.shape
    grp = B_HEADS // B_KV_HEADS
    q = partial_rope(rms_norm(b_q.reshape(bsz, s, B_HEADS, B_HEAD_DIM), q_norm_gain), positions)
    k = partial_rope(rms_norm(b_k.reshape(bsz, s, B_KV_HEADS, B_HEAD_DIM), k_norm_gain), positions)
    v = b_v.reshape(bsz, s, B_KV_HEADS, B_HEAD_DIM)
    kv = jnp.stack([k, v], axis=2)
    iq = partial_rope(i_q.reshape(bsz, s, IDX_HEADS, IDX_DIM), positions)
    ik = partial_rope(layer_norm(i_k, idx_k_ln_gain, idx_k_ln_bias)[:, :, None, :], positions)[:, :, 0]
    iw = i_w * (IDX_HEADS ** -0.5)
    topk = min(TOPK_MAX, s // 4)
    nb = s // Q_BLOCK
    key_pos = jnp.arange(s)

    def to_blocks(t):
        return t.reshape((bsz, nb, Q_BLOCK) + t.shape[2:]).swapaxes(0, 1)

    def block(args):
        qb, iqb, iwb, bid = args
        t_idx = bid * Q_BLOCK + jnp.arange(Q_BLOCK)
        rel = jax.nn.relu(jnp.einsum('bthd,bsd->bths', iqb, ik).astype(jnp.float32) * (IDX_DIM ** -0.5))
        score = jnp.einsum('bth,bths->bts', iwb.astype(jnp.float32), rel)
        causal = key_pos[None, :] <= t_idx[:, None]
        score = jnp.where(causal[None], score, -jnp.inf)
        _, sel = lax.top_k(score, topk)
        valid = sel <= t_idx[None, :, None]
        kv_sel = jax.vmap(lambda kvb, ib: kvb[ib])(kv, sel)
        k_sel = kv_sel[:, :, :, 0]
        v_sel = kv_sel[:, :, :, 1]
        qg = qb.reshape(bsz, Q_BLOCK, B_KV_HEADS, grp, B_HEAD_DIM)
        logits = jnp.einsum('btngd,btsnd->btngs', qg, k_sel).astype(jnp.float32) * (B_HEAD_DIM ** -0.5)
        logits = jnp.where(valid[:, :, None, None, :], logits, -jnp.inf)
        p = jax.nn.softmax(logits, axis=-1).astype(v_sel.dtype)
        o = jnp.einsum('btngs,btsnd->btngd', p, v_sel)
        return o.reshape(bsz, Q_BLOCK, B_WIDTH)

    out = lax.map(block, (to_blocks(q), to_blocks(iq), to_blocks(iw), jnp.arange(nb)))
    out = out.swapaxes(0, 1).reshape(bsz, s, B_WIDTH)
    return out * jax.nn.silu(b_z)


def memory_branch(m_q, m_z, mem, mem_norm_gain, w_mem_kv, mem_q_norm_gain, mem_k_norm_gain):
    bsz, s, _ = m_q.shape
    qm = rms_norm(m_q.reshape(bsz, s, M_HEADS, M_HEAD_DIM), mem_q_norm_gain)
    memn = rms_norm(mem, mem_norm_gain)
    kvm = (memn @ w_mem_kv).reshape(bsz, mem.shape[1], 2, M_HEADS, M_HEAD_DIM)
    km = rms_norm(kvm[:, :, 0], mem_k_norm_gain)
    vm = kvm[:, :, 1]
    logits = jnp.einsum('bshd,bmhd->bhsm', qm, km).astype(jnp.float32) * (M_HEAD_DIM ** -0.5)
    p = jax.nn.softmax(logits, axis=-1).astype(vm.dtype)
    o = jnp.einsum('bhsm,bmhd->bshd', p, vm).reshape(bsz, s, M_WIDTH)
    return o * jax.nn.silu(m_z)


def setup_inputs(seed: int = 0) -> dict:
    key = jax.random.key(seed)
    ks = jax.random.split(key, 24)
    f32 = jnp.float32
    nrm = lambda k, shape, scale: jax.random.normal(k, shape, f32) * scale
    L = DEPTH
    return {
        'x': jax.random.normal(ks[0], (BATCH, SEQ, D_MODEL), f32),
        'mem': jax.random.normal(ks[1], (BATCH, MEM_LEN, D_MODEL), f32),
        'positions': jnp.arange(SEQ, dtype=jnp.int32)[None, :]
                     + jax.random.randint(ks[2], (BATCH, 1), 0, 1024, dtype=jnp.int32),
        'norm_gain': 1.0 + nrm(ks[3], (L, D_MODEL), 0.01),
        'w_in': nrm(ks[4], (L, D_MODEL, D_IN), D_MODEL ** -0.5),
        'gmlp_ln_gain': 1.0 + nrm(ks[5], (L, A_WIDTH), 0.01),
        'gmlp_ln_bias': nrm(ks[6], (L, A_WIDTH), 0.01),
        'spatial_w': nrm(ks[7], (L, A_GROUPS, CHUNK, CHUNK), CHUNK ** -0.5),
        'spatial_b': 1.0 + nrm(ks[8], (L, A_GROUPS, CHUNK), 0.1),
        'w_branch_a': nrm(ks[9], (L, A_WIDTH, D_MODEL), A_WIDTH ** -0.5),
        'q_norm_gain': 1.0 + nrm(ks[10], (L, B_HEAD_DIM), 0.01),
        'k_norm_gain': 1.0 + nrm(ks[11], (L, B_HEAD_DIM), 0.01),
        'idx_k_ln_gain': 1.0 + nrm(ks[12], (L, IDX_DIM), 0.01),
        'idx_k_ln_bias': nrm(ks[13], (L, IDX_DIM), 0.01),
        'w_branch_b': nrm(ks[14], (L, B_WIDTH, D_MODEL), B_WIDTH ** -0.5),
        'mem_norm_gain': 1.0 + nrm(ks[15], (L, D_MODEL), 0.01),
        'w_mem_kv': nrm(ks[16], (L, D_MODEL, 2 * M_WIDTH), D_MODEL ** -0.5),
        'mem_q_norm_gain': 1.0 + nrm(ks[17], (L, M_HEAD_DIM), 0.01),
        'mem_k_norm_gain': 1.0 + nrm(ks[18], (L, M_HEAD_DIM), 0.01),
        'w_branch_m': nrm(ks[19], (L, M_WIDTH, D_MODEL), M_WIDTH ** -0.5),
        'w_out': nrm(ks[20], (L, D_MODEL, D_MODEL), D_MODEL ** -0.5),
    }


def reference(x, mem, positions, norm_gain, w_in, gmlp_ln_gain, gmlp_ln_bias, spatial_w,
              spatial_b, w_branch_a, q_norm_gain, k_norm_gain, idx_k_ln_gain, idx_k_ln_bias,
              w_branch_b, mem_norm_gain, w_mem_kv, mem_q_norm_gain, mem_k_norm_gain,
              w_branch_m, w_out):
    for l in range(DEPTH):
        h = rms_norm(x, norm_gain[l])
        proj = h @ w_in[l]
        (a_u, a_v, a_z, b_q, b_k, b_v, b_z, i_q, i_k, i_w,
         m_q, m_z, g_a, g_b, g_m) = jnp.split(proj, SPLIT_OFFSETS, axis=-1)
        y_a = gmlp_branch(a_u, a_v, a_z, gmlp_ln_gain[l], gmlp_ln_bias[l],
                          spatial_w[l], spatial_b[l]) @ w_branch_a[l]
        y_b = dsa_branch(b_q, b_k, b_v, b_z, i_q, i_k, i_w, positions,
                         q_norm_gain[l], k_norm_gain[l], idx_k_ln_gain[l],
                         idx_k_ln_bias[l]) @ w_branch_b[l]
        y_m = memory_branch(m_q, m_z, mem, mem_norm_gain[l], w_mem_kv[l],
                            mem_q_norm_gain[l], mem_k_norm_gain[l]) @ w_branch_m[l]
        merged = (jax.nn.sigmoid(g_a) * y_a + jax.nn.sigmoid(g_b) * y_b
                  + jax.nn.sigmoid(g_m) * y_m)
        x = x + merged @ w_out[l]
    return x
```

```python
import numpy as np
from contextlib import ExitStack
import concourse.bass as bass
import concourse.mybir as mybir
from concourse.bass_utils import run_bass_kernel_spmd

F32 = mybir.dt.float32
BF16 = mybir.dt.bfloat16
I32 = mybir.dt.int32
AF = mybir.ActivationFunctionType
ALU = mybir.AluOpType

D = 2048
SEQ = 4096
NB = 8
NOWN = NB * 128
EPS = 1e-6
NIT = 20
TOPK = 256
NEG = -1.0e30
PI = float(np.pi)

O_AU, O_AV, O_AZ = 0, 1024, 2048
O_BQ, O_BK, O_BV, O_BZ = 3072, 4096, 4352, 4608
O_IQ, O_IK, O_IW = 5632, 6656, 6720
O_MQ, O_MZ = 6736, 7760
O_GA, O_GB, O_GM = 8784, 10832, 12880
D_IN = 14928

C_GQ, C_GK, C_GMQ0, C_GMQ1, C_GMK0, C_GMK1, C_IKG, C_IKB, C_INVK, C_INVI, C_EPS, C_NPI = range(12)
NCOLS = 16


class T:
    __slots__ = ("w", "r", "name", "excl")

    def __init__(self, name="", excl=False):
        self.w = None
        self.r = []
        self.name = name
        self.excl = excl


class Sched:
    ENG = ['sync', 'scalar', 'vector', 'gpsimd', 'tensor']

    def __init__(self, nc, stack):
        self.nc = nc
        self.stack = stack
        self.streams = {e: [] for e in self.ENG}
        self.count = {e: 0 for e in self.ENG}
        self.sems = {}
        self.waited = {e: {} for e in self.ENG}
        self.dcount = {}
        for e in self.ENG:
            self.sems[e] = stack.enter_context(nc.semaphore("s_" + e))

    def _deps(self, eng, reads, writes):
        need = {}

        def add(tok, kind):
            if tok is None:
                return
            k, v = tok
            if k == eng and (eng == 'tensor' or kind == 'war'):
                return
            if need.get(k, 0) < v:
                need[k] = v
        for r in reads:
            add(r.w, 'raw')
        for w in writes:
            add(w.w, 'waw')
            for t in w.r:
                add(t, 'war')
        waits = []
        for k, v in need.items():
            if k in self.dcount:
                v = self.dcount[k]
            if self.waited[eng].get(k, 0) >= v:
                continue
            self.waited[eng][k] = v
            waits.append((k, v))
        return waits

    def _record(self, tok, reads, writes):
        for r in reads:
            r.r.append(tok)
            if len(r.r) > 64:
                mx = {}
                for k, v in r.r:
                    if mx.get(k, 0) < v:
                        mx[k] = v
                r.r = list(mx.items())
        for w in writes:
            w.w = tok
            w.r = []

    def op(self, eng, fn, reads=(), writes=(), inc=True):
        if any(r.excl for r in reads):
            writes = list(writes) + [r for r in reads if r.excl and r not in writes]
            reads = [r for r in reads if not r.excl]
        waits = self._deps(eng, reads, writes)
        tok = (eng, self.count[eng] + 1)
        if inc:
            self.count[eng] += 1
        self.streams[eng].append((waits, fn, eng if inc else None, 1))
        self._record(tok, reads, writes)
        return tok

    def dma(self, eng, fn, key, reads=(), writes=()):
        if key not in self.sems:
            self.sems[key] = self.stack.enter_context(self.nc.semaphore("d_" + key))
            self.dcount[key] = 0
        waits = self._deps(eng, reads, writes)
        self.dcount[key] += 16
        tok = (key, self.dcount[key])
        self.streams[eng].append((waits, fn, key, 16))
        self._record(tok, reads, writes)
        return tok

    def barrier(self):
        toks = [(e, self.count[e]) for e in self.ENG if self.count[e] > 0]
        toks += [(k, v) for k, v in self.dcount.items() if v > 0]
        for e in self.ENG:
            waits = []
            for k, v in toks:
                if k == e:
                    continue
                if self.waited[e].get(k, 0) >= v:
                    continue
                self.waited[e][k] = v
                waits.append((k, v))
            if waits:
                self.streams[e].append((waits, None, None, 0))

    def emit(self):
        nc = self.nc
        S = self
        with nc.Block() as block:
            def mk(ename):
                def body(e):
                    for waits, fn, inc, n in S.streams[ename]:
                        for k, v in waits:
                            e.wait_ge(S.sems[k], v)
                        if fn is not None:
                            ins = fn(e)
                            if inc is not None:
                                ins.then_inc(S.sems[inc], n)
                return body
            block.sync(mk('sync'))
            block.scalar(mk('scalar'))
            block.vector(mk('vector'))
            block.gpsimd(mk('gpsimd'))
            block.tensor(mk('tensor'))


class StopBuild(Exception):
    pass


class Builder:
    def chk(self, name):
        if self.stop == name:
            raise StopBuild()

    def __init__(self, dbg=(), stop=None):
        self.dbg = set(dbg)
        self.stop = stop
        self.nc = bass.Bass("TRN2", target_bir_lowering=False)
        self.dbg_outs = {}

    def alloc(self, nwords):
        off = self.top
        self.top += (nwords + 7) // 8 * 8
        assert self.top <= self.AW, f"arena overflow {self.top} > {self.AW}"
        return off

    def f32(self, nwords, shape=None):
        off = self.alloc(nwords)
        return self.f32_at(off, nwords, shape)

    def f32_at(self, off, nwords, shape=None):
        ap = self.arena[:, off:off + nwords]
        if shape is not None:
            ap = self._reshape(ap, shape)
        return ap

    def bf(self, nelem, shape=None):
        assert nelem % 2 == 0
        off = self.alloc(nelem // 2)
        return self.bf_at(off, nelem, shape)

    def bf_at(self, off, nelem, shape=None):
        ap = self.arena[:, off:off + nelem // 2].bitcast(BF16)
        if shape is not None:
            ap = self._reshape(ap, shape)
        return ap

    @staticmethod
    def _reshape(ap, shape):
        if len(shape) == 2:
            return ap.rearrange("p (a b) -> p a b", a=shape[0], b=shape[1])
        if len(shape) == 3:
            return ap.rearrange("p (a b c) -> p a b c", a=shape[0], b=shape[1], c=shape[2])
        raise ValueError

    def pbank(self):
        b = self.ppool[self.pidx % len(self.ppool)]
        self.pidx += 1
        return self.psum[:, b, :], self.pT[b]

    def pbank_bf(self):
        b = self.ppool[self.pidx % len(self.ppool)]
        self.pidx += 1
        return self.psum[:, b, :].bitcast(BF16), self.pT[b]

    def act(self, out, in_, func, reads, writes, bias=None, scale=None, accum=None):
        kw = {}
        if bias is not None:
            kw['bias'] = bias
        if scale is not None:
            kw['scale'] = scale
        if accum is not None:
            kw['accum_out'] = accum
        return self.S.op('scalar', lambda e: e.activation(out=out, in_=in_, func=func, **kw), reads, writes)

    def tt(self, out, a, b, op, reads, writes, eng='vector'):
        return self.S.op(eng, lambda e: e.tensor_tensor(out=out, in0=a, in1=b, op=op), reads, writes)

    def ts(self, out, a, s1, s2, op0, op1, reads, writes, eng='vector', accum=None):
        if op1 is None:
            return self.S.op(eng, lambda e: e.tensor_scalar(out=out, in0=a, scalar1=s1, scalar2=None, op0=op0), reads, writes)
        if accum is not None:
            return self.S.op(eng, lambda e: e.tensor_scalar(out=out, in0=a, scalar1=s1, scalar2=s2, op0=op0, op1=op1, accum_out=accum), reads, writes)
        return self.S.op(eng, lambda e: e.tensor_scalar(out=out, in0=a, scalar1=s1, scalar2=s2, op0=op0, op1=op1), reads, writes)

    def stt(self, out, a, s, b, op0, op1, reads, writes, eng='vector'):
        return self.S.op(eng, lambda e: e.scalar_tensor_tensor(out=out, in0=a, scalar=s, in1=b, op0=op0, op1=op1), reads, writes)

    def copy(self, out, in_, reads, writes, eng='vector'):
        if eng == 'scalar':
            return self.S.op('scalar', lambda e: e.activation(out=out, in_=in_, func=AF.Copy), reads, writes)
        return self.S.op(eng, lambda e: e.tensor_copy(out=out, in_=in_), reads, writes)

    def recip(self, out, in_, reads, writes):
        return self.S.op('vector', lambda e: e.reciprocal(out=out, in_=in_), reads, writes)

    def mm(self, out, lhsT, rhs, start, stop, reads, writes, inc=None):
        if inc is None:
            inc = stop
        return self.S.op('tensor', lambda e: e.matmul(out, lhsT=lhsT, rhs=rhs, start=start, stop=stop), reads, writes, inc=inc)

    def transpose(self, out, in_, reads, writes, inc=True):
        ident = self.ident
        return self.S.op('tensor', lambda e: e.transpose(out=out, in_=in_, identity=ident), list(reads) + [self.Tconst], writes, inc=inc)

    def dma(self, q, out, in_, key, reads=(), writes=()):
        return self.S.dma(q, lambda e: e.dma_start(out=out, in_=in_), key, reads, writes)

    def tap(self, name, ap, Tt, shape):
        if name not in self.dbg:
            return
        o = self.nc.dram_tensor("dbg_" + name, list(shape), F32 if ap.dtype == F32 else ap.dtype, kind="ExternalOutput").ap()
        self.dbg_outs[name] = o
        tok = self.dma('sync', o, ap, 'dbg', reads=[Tt])
        self.final_toks.append(tok)

    def wload(self, dram2d, col0, ncols, krows=2048):
        i = self.widx % len(self.wbufs)
        self.widx += 1
        buf, Tb = self.wbufs[i]
        kc = krows // 128
        view = buf[:, 0:kc, 0:ncols]
        src = dram2d.rearrange("(c p) n -> p c n", p=128)[:, :, col0:col0 + ncols]
        self.dma('gpsimd', view, src, f'w{i}', writes=[Tb])
        return view, Tb

    def projT(self, w, Tw, wc0, nout, rhs3, Trhs, out_ps, Tps, kc=16):
        for k in range(kc):
            self.mm(out_ps, w[:, k, wc0:wc0 + nout], rhs3[:, k, :], k == 0, k == kc - 1, [Tw, Trhs], [Tps])

    def make_hT(self, xrows, gbc, Tg, dst3, Tdst, xi):
        xt, Txt = self.xbufs[xi % 2]
        self.dma('sync', xt, xrows, f'x{xi % 2}', writes=[Txt])
        ssq, Tss = self.small[self.sidx % len(self.small)]
        self.sidx += 1
        self.act(self.junk, xt, AF.Square, [Txt], [self.Tjunk, Tss], accum=ssq[:, 0:1])
        self.act(ssq[:, 1:2], ssq[:, 0:1], AF.Sqrt, [Tss, self.Tconst], [Tss], bias=self.cols[:, C_EPS:C_EPS + 1], scale=1.0 / D)
        self.recip(ssq[:, 2:3], ssq[:, 1:2], [Tss], [Tss])
        xs, Txs = self.xsbufs[xi % 2]
        self.stt(xs, xt, ssq[:, 2:3], gbc, ALU.mult, ALU.mult, [Txt, Tss, Tg], [Txs])
        for half in range(2):
            pb, Tp = self.pbank_bf()
            for c in range(8):
                kc = half * 8 + c
                self.transpose(pb[:, c * 128:(c + 1) * 128], xs[:, kc * 128:(kc + 1) * 128], [Txs], [Tp], inc=(c == 7))
            self.copy(dst3[:, half * 8:(half + 1) * 8, :], pb.rearrange("p (c t) -> p c t", c=8), [Tp], [Tdst],
                      eng=('scalar' if half == 0 else 'vector'))

    def hT_stream(self, tiles, gbc, Tg, banks):
        n = len(tiles)
        cols = self.cols

        def S0(t):
            xt, Txt = self.xbufs[t % 3]
            self.dma('sync', xt, tiles[t][0], f'x{t % 3}', writes=[Txt])

        def S1(t):
            xt, Txt = self.xbufs[t % 3]
            ssq, Tss = self.small[t % 4]
            xs, Txs = self.xsbufs[t % 2]
            self.act(xs, xt, AF.Square, [Txt], [Txs, Tss], accum=ssq[:, 0:1])
            self.act(ssq[:, 1:2], ssq[:, 0:1], AF.Ln, [Tss, self.Tconst], [Tss], bias=cols[:, C_EPS:C_EPS + 1], scale=1.0 / D)
            self.act(ssq[:, 2:3], ssq[:, 1:2], AF.Exp, [Tss], [Tss], scale=-0.5)

        def S2(t):
            xt, Txt = self.xbufs[t % 3]
            ssq, Tss = self.small[t % 4]
            xs, Txs = self.xsbufs[t % 2]
            self.stt(xs, xt, ssq[:, 2:3], gbc, ALU.mult, ALU.mult, [Txt, Tss, Tg], [Txs])
            for half in range(2):
                pbk, Tp = self.BK(banks[half])
                pb = pbk.bitcast(BF16)
                for c in range(8):
                    kc = half * 8 + c
                    self.transpose(pb[:, c * 128:(c + 1) * 128], xs[:, kc * 128:(kc + 1) * 128], [Txs], [Tp], inc=(c == 7))

        def S3(t):
            dst3, Tdst = tiles[t][1], tiles[t][2]
            for half in range(2):
                pbk, Tp = self.BK(banks[half])
                pb = pbk.bitcast(BF16)
                self.copy(dst3[:, half * 8:(half + 1) * 8, :], pb.rearrange("p (c t) -> p c t", c=8), [Tp], [Tdst],
                          eng=('scalar' if half == 0 else 'vector'))

        S0(0)
        for r in range(n + 2):
            if r + 1 < n:
                S0(r + 1)
            if r < n:
                S1(r)
            if 0 <= r - 2 < n:
                S3(r - 2)
            if 0 <= r - 1 < n:
                S2(r - 1)
            yield

    def trig_tables(self, posf, Tpos, invcol, n, Cout, Sout, Tout):
        tb = self.trig_scr
        Tt = self.Ttrig
        ang, ki, kf, r = tb[0][:, 0:n], tb[1][:, 0:n].bitcast(I32), tb[2][:, 0:n], tb[3][:, 0:n]
        self.ts(ang, posf, invcol, None, ALU.mult, None, [Tpos, self.Tconst], [Tt])
        for which, dst in ((0, Sout), (1, Cout)):
            if which == 1:
                self.ts(ang, ang, PI / 2, None, ALU.add, None, [Tt], [Tt])
            self.ts(ki, ang, 1.0 / (2 * PI), None, ALU.mult, None, [Tt], [Tt])
            self.copy(kf, ki, [Tt], [Tt])
            self.stt(r, kf, -2 * PI, ang, ALU.mult, ALU.add, [Tt], [Tt])
            self.ts(kf, r, PI, -2 * PI, ALU.is_gt, ALU.mult, [Tt], [Tt])
            self.tt(r, r, kf, ALU.add, [Tt], [Tt])
            self.ts(kf, r, -PI, 2 * PI, ALU.is_lt, ALU.mult, [Tt], [Tt])
            self.tt(r, r, kf, ALU.add, [Tt], [Tt])
            self.ts(r, r, -PI, PI, ALU.max, ALU.min, [Tt], [Tt])
            self.act(dst, r, AF.Sin, [Tt], [Tout])

    def trig2(self, posf, Tpos, n, Stab, Ctab, Ttab):
        tb = self.trig_scr
        Tt = self.Ttrig
        r3 = lambda a: a[:, 0:2 * n].rearrange("p (y s) -> p y s", y=2)
        ang, ki, kf, r = tb[0][:, 0:2 * n], tb[1][:, 0:2 * n].bitcast(I32), tb[2][:, 0:2 * n], tb[3][:, 0:2 * n]
        inv2 = self.cols[:, C_INVK:C_INVK + 2]
        self.tt(r3(tb[0]), posf[:, 0:n].unsqueeze(1).to_broadcast([128, 2, n]), inv2.unsqueeze(2).to_broadcast([128, 2, n]),
                ALU.mult, [Tpos, self.Tconst], [Tt])
        for which, dst in ((0, Stab), (1, Ctab)):
            if which == 1:
                self.ts(ang, ang, PI / 2, None, ALU.add, None, [Tt], [Tt])
            self.ts(ki, ang, 1.0 / (2 * PI), None, ALU.mult, None, [Tt], [Tt])
            self.copy(kf, ki, [Tt], [Tt])
            self.stt(r, kf, -2 * PI, ang, ALU.mult, ALU.add, [Tt], [Tt])
            self.ts(r, r, -PI, PI, ALU.max, ALU.min, [Tt], [Tt])
            self.act(dst.rearrange("p y s -> p (y s)"), r, AF.Sin, [Tt], [Ttab])

    def rope_combine(self, out_bf, x_ap, Tx, xb_bf, Txb, ropeP, Cc, Sn, Ttab, Tout, n):
        pa, Tpa = self.pbank()
        self.mm(pa[:, 0:n], ropeP, xb_bf, True, True, [Txb, self.Tconst], [Tpa])
        t1, T1 = self.tmpf[self.tidx % len(self.tmpf)]
        self.tidx += 1
        t2, T2 = self.tmpf[self.tidx % len(self.tmpf)]
        self.tidx += 1
        self.tt(t1[:, 0:n], x_ap, Cc, ALU.mult, [Tx, Ttab], [T1])
        self.tt(t2[:, 0:n], pa[:, 0:n], Sn, ALU.mult, [Tpa, Ttab], [T2])
        self.tt(out_bf, t1[:, 0:n], t2[:, 0:n], ALU.add, [T1, T2], [Tout], eng='gpsimd')

    def rms_T(self, raws, Traws, ones, invn, n):
        pa, Tpa = self.pbank()
        for i, (raw, Tr) in enumerate(zip(raws, Traws)):
            sq, Tsq = self.tmpb[self.bidx % len(self.tmpb)]
            self.bidx += 1
            self.act(sq[:, 0:n], raw, AF.Square, [Tr], [Tsq])
            self.mm(pa[:, 0:n], ones, sq[:, 0:n], i == 0, i == len(raws) - 1, [Tsq, self.Tconst], [Tpa])
        sd, Tsd = self.tmpf[self.tidx % len(self.tmpf)]
        self.tidx += 1
        self.act(sd[:, 0:n], pa[:, 0:n], AF.Ln, [Tpa, self.Tconst], [Tsd], bias=self.cols[:, C_EPS:C_EPS + 1], scale=invn)
        self.act(sd[:, 0:n], sd[:, 0:n], AF.Exp, [Tsd], [Tsd], scale=-0.5)
        return sd[:, 0:n], Tsd

    def g_rope_combine(self, out_bf, x_ap, Tx, xb_bf, Txb, ropeP, Cc, Sn, Ttab, Tout, n):
        pa, Tpa = self.pbank()
        self.mm(pa[:, 0:n], ropeP, xb_bf, True, True, [Txb, self.Tconst], [Tpa])
        yield
        t1, T1 = self.tmpf[self.tidx % len(self.tmpf)]
        self.tidx += 1
        t2, T2 = self.tmpf[self.tidx % len(self.tmpf)]
        self.tidx += 1
        self.tt(t1[:, 0:n], x_ap, Cc, ALU.mult, [Tx, Ttab], [T1])
        self.tt(t2[:, 0:n], pa[:, 0:n], Sn, ALU.mult, [Tpa, Ttab], [T2])
        self.tt(out_bf, t1[:, 0:n], t2[:, 0:n], ALU.add, [T1, T2], [Tout], eng='gpsimd')

    def g_rms_T(self, raws, Traws, ones, invn, n):
        pa, Tpa = self.pbank()
        for i, (raw, Tr) in enumerate(zip(raws, Traws)):
            sq, Tsq = self.tmpb[self.bidx % len(self.tmpb)]
            self.bidx += 1
            self.act(sq[:, 0:n], raw, AF.Square, [Tr], [Tsq])
            self.mm(pa[:, 0:n], ones, sq[:, 0:n], i == 0, i == len(raws) - 1, [Tsq, self.Tconst], [Tpa])
        yield
        sd, Tsd = self.tmpf[self.tidx % len(self.tmpf)]
        self.tidx += 1
        self.act(sd[:, 0:n], pa[:, 0:n], AF.Ln, [Tpa, self.Tconst], [Tsd], bias=self.cols[:, C_EPS:C_EPS + 1], scale=invn)
        self.act(sd[:, 0:n], sd[:, 0:n], AF.Exp, [Tsd], [Tsd], scale=-0.5)
        return sd[:, 0:n], Tsd

    def mkctx(self, nf, nb_, raw_bank, aux_banks):
        return dict(tmpf=[(self.f32(512), T("cf")) for _ in range(nf)], tmpb=[(self.bf(512), T("cb")) for _ in range(nb_)],
                    ppool=list(aux_banks), raw=raw_bank, pidx=0, tidx=0, bidx=0)

    def _load(self, ctx):
        self.tmpf, self.tmpb, self.ppool = ctx['tmpf'], ctx['tmpb'], ctx['ppool']
        self.pidx, self.tidx, self.bidx = ctx['pidx'], ctx['tidx'], ctx['bidx']

    def _save(self, ctx):
        ctx['pidx'], ctx['tidx'], ctx['bidx'] = self.pidx, self.tidx, self.bidx

    def interleave(self, items):
        active = list(items)
        while active:
            for it in list(active):
                gen, ctx = it
                self._load(ctx)
                try:
                    next(gen)
                except StopIteration:
                    active.remove(it)
                self._save(ctx)

    def BK(self, b):
        return self.psum[:, b, :], self.pT[b]

    def build(self):
        nc = self.nc
        dbg = self.dbg
        di = lambda name, shape, dt=F32: nc.dram_tensor(name, list(shape), dt, kind="ExternalInput").ap()
        xall = di("xall", [SEQ, D])
        xown = di("xown", [NOWN, D])
        posall = di("posall", [128, SEQ], I32)
        posown = di("posown", [128, NOWN], I32)
        w_in = di("w_in", [D, D_IN])
        w_a = di("w_a", [1024, D])
        w_b = di("w_b", [1024, D])
        w_m = di("w_m", [1024, D])
        w_out = di("w_out", [D, D])
        w_mkv = di("w_mkv", [D, 2048])
        memx = di("memx", [256, D])
        gbc_d = di("gbc", [128, D])
        gmem_d = di("gmembc", [128, D])
        cols_d = di("cols", [128, NCOLS])
        lng_d = di("lng", [128, 1024])
        lnb_d = di("lnb", [128, 1024])
        wsT_d = di("wsT", [128, 8 * 128])
        sbt_d = di("sbt", [128, 8 * 128])
        cmneg_d = di("cmneg", [128, 512])
        cmat_d = di("cmat", [128, 3 * 128])
        pow2_d = di("pow2", [128, NIT + 2])
        y = nc.dram_tensor("y", [NOWN, D], F32, kind="ExternalOutput").ap()

        with ExitStack() as st:
            S = self.S = Sched(nc, st)
            self.AW = 53200
            self.arena = st.enter_context(nc.sbuf_tensor("arena", [128, self.AW], F32))
            self.psum = st.enter_context(nc.psum_tensor("psum", [128, 8, 512], F32))
            self.pT = [T(f"ps{b}", excl=True) for b in range(8)]
            self.ppool = list(range(8))
            self.pidx = 0
            self.top = 0
            self.final_toks = []
            self.widx = self.sidx = self.tidx = self.bidx = 0

            try:
                self.Tconst = Tconst = T("const")
                self.ident = self.bf(128)
                ones = self.bf(128)
                ropePk = self.bf(128)
                ropePi = self.bf(128)
                blk64 = self.bf(128)
                self.cols = cols = self.f32(NCOLS)
                pow2 = self.f32(NIT + 2)
                identf = self.f32(128)
                cmat_f = self.f32(3 * 128)
                S.op('gpsimd', lambda e: e.memset(identf, 0.0), [], [Tconst])
                S.op('gpsimd', lambda e: e.affine_select(out=identf, in_=identf, pattern=[[-1, 128]], compare_op=ALU.not_equal,
                                                         fill=1.0, base=0, channel_multiplier=1), [Tconst], [Tconst])
                self.copy(self.ident, identf, [Tconst], [Tconst])
                S.op('vector', lambda e: e.memset(ones, 1.0), [], [Tconst])
                self.dma('sync', cols, cols_d, 'c0', writes=[Tconst])
                self.dma('sync', pow2, pow2_d, 'c0', writes=[Tconst])
                self.dma('sync', cmat_f, cmat_d, 'c0', writes=[Tconst])
                self.copy(ropePk, cmat_f[:, 0:128], [Tconst], [Tconst])
                self.copy(ropePi, cmat_f[:, 128:256], [Tconst], [Tconst])
                self.copy(blk64, cmat_f[:, 256:384], [Tconst], [Tconst])
                base_top = self.top
                self.chk('C')

                kT = self.bf(2 * SEQ, (2, SEQ)); TkT = T("kT")
                Vt = self.bf(32 * 256, (32, 256)); TV = T("V")
                ikT = self.bf(SEQ); Tik = T("ikT")
                kvi_top = self.top

                gbc = self.f32(D); Tg = T("gbc")
                self.dma('scalar', gbc, gbc_d, 'c1', writes=[Tg])
                self.xbufs = [(self.f32(D), T(f"x{i}")) for i in range(3)]
                self.xsbufs = [(self.bf(D), T("xs0")), (self.bf(D), T("xs1"))]
                self.small = [(self.f32(8), T(f"sm{i}")) for i in range(4)]
                hbufs = [(self.bf(16 * 512, (16, 512)), T(f"hTg{i}")) for i in range(2)]
                wkv = self.bf(16 * 640, (16, 640)); Twkv = T("wkv")
                posf = self.f32(512); Tposf = T("posf")
                posi = self.f32(512).bitcast(I32); Tposi = T("posi")
                self.trig_scr = [self.f32(1024) for _ in range(4)]; self.Ttrig = T("trig")
                tabs = [(self.f32(1024, (2, 512)), self.f32(1024, (2, 512)), T(f"tab{i}")) for i in range(2)]
                ctxs = [self.mkctx(3, 2, 0, [1]), self.mkctx(5, 3, 2, [3])]
                ctxV = self.mkctx(0, 0, None, [6, 7])
                ctxH = self.mkctx(0, 0, None, [4, 5])

                wv3 = w_in.rearrange("(c p) n -> p c n", p=128)
                self.dma('gpsimd', wkv[:, :, 0:512], wv3[:, :, O_BK:O_BK + 512], 'wk', writes=[Twkv])
                self.dma('gpsimd', wkv[:, :, 512:576], wv3[:, :, O_IK:O_IK + 64], 'wk', writes=[Twkv])
                self.dma('gpsimd', wkv[:, :, 576:640], wv3[:, :, O_IK:O_IK + 64], 'wk', writes=[Twkv])

                NG = SEQ // 512

                def a_trig(g):
                    St, Ct, Ttb = tabs[g % 2]
                    self.dma('scalar', posi, posall[:, g * 512:(g + 1) * 512], 'pos', writes=[Tposi])
                    self.copy(posf, posi, [Tposi], [Tposf])
                    self.trig2(posf, Tposf, 512, St, Ct, Ttb)

                def a_hT(g, tt_):
                    hTg, ThTg = hbufs[g % 2]
                    ti = g * 4 + tt_
                    self.make_hT(xall[ti * 128:(ti + 1) * 128, :], gbc, Tg, hTg[:, :, tt_ * 128:(tt_ + 1) * 128], ThTg, ti)

                def g_khead(g, kvh, ctx):
                    hTg, ThTg = hbufs[g % 2]
                    St, Ct, Ttb = tabs[g % 2]
                    gs = slice(g * 512, (g + 1) * 512)
                    raw, Traw = self.BK(ctx['raw'])
                    self.projT(wkv, Twkv, kvh * 128, 128, hTg, ThTg, raw, Traw)
                    yield
                    rstd, Trs = yield from self.g_rms_T([raw], [Traw], ones, 1.0 / 128, 512)
                    kn, Tkn = self.tmpb[self.bidx % len(self.tmpb)]
                    self.bidx += 1
                    self.stt(kn, raw, cols[:, C_GK:C_GK + 1], rstd, ALU.mult, ALU.mult, [Traw, Trs, Tconst], [Tkn])
                    yield from self.g_rope_combine(kT[:, kvh, gs], kn, Tkn, kn, Tkn, ropePk, Ct[:, 0, :], St[:, 0, :], Ttb, TkT, 512)

                def g_ik(g, ctx):
                    hTg, ThTg = hbufs[g % 2]
                    St, Ct, Ttb = tabs[g % 2]
                    gs = slice(g * 512, (g + 1) * 512)
                    raw, Traw = self.BK(ctx['raw'])
                    self.projT(wkv, Twkv, 512, 128, hTg, ThTg, raw, Traw)
                    yield
                    ikb, Tikb = self.tmpb[0]
                    ikf, Tikf = self.tmpf[0]
                    self.copy(ikb, raw, [Traw], [Tikb], eng='scalar')
                    self.copy(ikf, raw, [Traw], [Tikf], eng='scalar')
                    pm, Tpm = self.pbank()
                    self.mm(pm, blk64, ikb, True, True, [Tikb, Tconst], [Tpm])
                    yield
                    cen, Tcen = self.tmpf[1]
                    self.tt(cen, ikf, pm, ALU.subtract, [Tikf, Tpm], [Tcen])
                    sq, Tsq = self.tmpb[1]
                    self.act(sq, cen, AF.Square, [Tcen], [Tsq])
                    pv, Tpv = self.pbank()
                    self.mm(pv, blk64, sq, True, True, [Tsq, Tconst], [Tpv])
                    yield
                    sd, Tsd = self.tmpf[2]
                    self.act(sd, pv, AF.Ln, [Tpv, Tconst], [Tsd], bias=cols[:, C_EPS:C_EPS + 1], scale=1.0)
                    self.act(sd, sd, AF.Exp, [Tsd], [Tsd], scale=-0.5)
                    self.tt(cen, cen, sd, ALU.mult, [Tcen, Tsd], [Tcen])
                    ikn, Tikn = self.tmpb[2]
                    self.ts(ikn, cen, cols[:, C_IKG:C_IKG + 1], cols[:, C_IKB:C_IKB + 1], ALU.mult, ALU.add, [Tcen, Tconst], [Tikn])
                    self.tidx = 3
                    yield from self.g_rope_combine(ikT[:, gs], ikn, Tikn, ikn, Tikn, ropePi, Ct[:, 1, :], St[:, 1, :], Ttb, Tik, 512)

                def g_v(g):
                    hTg, ThTg = hbufs[g % 2]
                    prev = None
                    for tt_ in range(4):
                        ti = g * 4 + tt_
                        pvb, Tpvb = self.BK(6 + (tt_ % 2))
                        for k in range(16):
                            self.mm(pvb[:, 0:256], hTg[:, k, tt_ * 128:(tt_ + 1) * 128], wkv[:, k, 256:512], k == 0, k == 15, [ThTg, Twkv], [Tpvb])
                        if prev is not None:
                            self.copy(Vt[:, prev[0], :], prev[1][:, 0:256], [prev[2]], [TV], eng='scalar')
                        prev = (ti, pvb, Tpvb)
                        yield
                    self.copy(Vt[:, prev[0], :], prev[1][:, 0:256], [prev[2]], [TV], eng='scalar')

                def g_kboth(g, ctx):
                    yield from g_khead(g, 0, ctx)
                    yield from g_khead(g, 1, ctx)

                all_tiles = []
                for g in range(NG):
                    hTg, ThTg = hbufs[g % 2]
                    for tt_ in range(4):
                        ti = g * 4 + tt_
                        all_tiles.append((xall[ti * 128:(ti + 1) * 128, :], hTg[:, :, tt_ * 128:(tt_ + 1) * 128], ThTg))
                prod = self.hT_stream(all_tiles, gbc, Tg, [4, 5])

                def g_prod(nrounds):
                    for _ in range(nrounds):
                        try:
                            next(prod)
                        except StopIteration:
                            return
                        yield

                a_trig(0)
                for _ in g_prod(6):
                    pass
                for g in range(NG):
                    if g + 1 < NG:
                        a_trig(g + 1)
                    items = [(g_v(g), ctxV)]
                    if g + 1 < NG:
                        items.append((g_prod(4), ctxH))
                    items += [(g_kboth(g, ctxs[0]), ctxs[0]), (g_ik(g, ctxs[1]), ctxs[1])]
                    self.interleave(items)
                self.ppool = list(range(8))

                self.tap("kT", kT, TkT, [128, 2, SEQ])
                self.tap("V", Vt, TV, [128, 32, 256])
                self.tap("ikT", ikT, Tik, [128, SEQ])
                S.barrier()
                if self.stop == 'A':
                    self._finish(y)
                    return nc

                self.top = kvi_top
                hT = self.bf(16 * NOWN, (16, NOWN)); ThT = T("hT")
                BT = self.bf(8 * NOWN, (8, NOWN)); TBT = T("BT")
                wbase = self.top
                self.wbufs = [(self.bf(16 * 512, (16, 512)), T("w0")), (self.bf(16 * 512, (16, 512)), T("w1"))]
                b1_top = self.top
                gbc = self.f32(D); Tg = T("gbc2")
                self.dma('scalar', gbc, gbc_d, 'c1', writes=[Tg])
                self.xbufs = [(self.f32(D), T(f"x{i}")) for i in range(3)]
                self.xsbufs = [(self.bf(D), T("xs0")), (self.bf(D), T("xs1"))]
                self.small = [(self.f32(8), T(f"sm{i}")) for i in range(4)]
                own_tiles = [(xown[ti * 128:(ti + 1) * 128, :], hT[:, :, ti * 128:(ti + 1) * 128], ThT) for ti in range(NB)]
                for _ in self.hT_stream(own_tiles, gbc, Tg, [4, 5]):
                    pass
                self.tap("hT", hT, ThT, [128, 16, NOWN])
                S.barrier()
                if self.stop == 'B0':
                    self._finish(y)
                    return nc

                self.top = b1_top
                iqT = self.bf(8 * NOWN, (8, NOWN)); TiqT = T("iqT")
                qT = self.bf(8 * NOWN, (8, NOWN)); TqT = T("qT")
                Sbuf = self.f32(SEQ); TS = T("S")
                iwf = self.f32(NB * 16, (NB, 16)); Tiw = T("iw")
                self.small = [(self.f32(8), T(f"sm{i}")) for i in range(4)]
                bis = self.f32(16); Tbis = T("bis")
                Wtab = self.f32(NIT + 2); TW = T("Wtab")
                cmnegb = self.bf(512); Tcm = T("cmneg")
                self.dma('gpsimd', cmnegb, cmneg_d, 'c3', writes=[Tcm])
                gqs = self.f32(8); Tgqs = T("gqs")
                self.ts(gqs[:, 0:1], cols[:, C_GQ:C_GQ + 1], float(128 ** -0.5), None, ALU.mult, None, [Tconst], [Tgqs])
                b1e_top = self.top
                diags = [(self.bf(16 * 128, (16, 128)), T(f"diag{i}")) for i in range(2)]
                negmT = self.bf(32 * 128, (32, 128)); TnT = T("negmT")
                Sbuf2 = self.f32(SEQ); TS2 = T("S2")
                self.top = b1e_top
                posf = self.f32(512); Tposf = T("posf")
                posi = self.f32(512).bitcast(I32); Tposi = T("posi")
                self.trig_scr = [self.f32(1024) for _ in range(4)]; self.Ttrig = T("trig")
                Ttab = T("tabo")
                So = [Sbuf[:, 0:1024].rearrange("p (y s) -> p y s", y=2), Sbuf[:, 1024:2048].rearrange("p (y s) -> p y s", y=2)]
                Co = [Sbuf[:, 2048:3072].rearrange("p (y s) -> p y s", y=2), Sbuf[:, 3072:4096].rearrange("p (y s) -> p y s", y=2)]
                for half in range(2):
                    self.dma('scalar', posi, posown[:, half * 512:(half + 1) * 512], 'pos', writes=[Tposi])
                    self.copy(posf, posi, [Tposi], [Tposf])
                    self.trig2(posf, Tposf, 512, So[half], Co[half], Ttab)

                S.barrier()
                self.top = b1e_top
                qctx = [self.mkctx(3, 2, 0, [1]), self.mkctx(3, 2, 2, [3]), self.mkctx(3, 2, 4, [5])]

                def g_iq(w, Tw, pp, p, half, ctx):
                    hs = slice(half * 512, (half + 1) * 512)
                    raw, Traw = self.BK(ctx['raw'])
                    self.projT(w, Tw, pp * 128, 128, hT[:, :, hs], ThT, raw, Traw)
                    yield
                    iqb, Tiqb = self.tmpb[self.bidx % len(self.tmpb)]
                    self.bidx += 1
                    self.copy(iqb, raw, [Traw], [Tiqb], eng='scalar')
                    yield from self.g_rope_combine(iqT[:, p, hs], raw, Traw, iqb, Tiqb, ropePi, Co[half][:, 1, :], So[half][:, 1, :], Ttab, TiqT, 512)

                def g_q(w, Tw, hh, h, half, ctx):
                    hs = slice(half * 512, (half + 1) * 512)
                    raw, Traw = self.BK(ctx['raw'])
                    self.projT(w, Tw, hh * 128, 128, hT[:, :, hs], ThT, raw, Traw)
                    yield
                    rstd, Trs = yield from self.g_rms_T([raw], [Traw], ones, 1.0 / 128, 512)
                    qn, Tqn = self.tmpb[self.bidx % len(self.tmpb)]
                    self.bidx += 1
                    self.stt(qn, raw, gqs[:, 0:1], rstd, ALU.mult, ALU.mult, [Traw, Trs, Tgqs], [Tqn])
                    yield from self.g_rope_combine(qT[:, h, hs], qn, Tqn, qn, Tqn, ropePk, Co[half][:, 0, :], So[half][:, 0, :], Ttab, TqT, 512)

                def run_batches(mk):
                    for b0 in range(0, len(mk), 3):
                        batch = mk[b0:b0 + 3]
                        self.interleave([(f(qctx[j]), qctx[j]) for j, f in enumerate(batch)])

                for ch in range(2):
                    w, Tw = self.wload(w_in, O_IQ + ch * 512, 512)
                    run_batches([(lambda ctx, pp=pp, half=half, w=w, Tw=Tw, ch=ch: g_iq(w, Tw, pp, ch * 4 + pp, half, ctx))
                                 for pp in range(4) for half in range(2)])
                for ch in range(2):
                    w, Tw = self.wload(w_in, O_BQ + ch * 512, 512)
                    run_batches([(lambda ctx, hh=hh, half=half, w=w, Tw=Tw, ch=ch: g_q(w, Tw, hh, ch * 4 + hh, half, ctx))
                                 for hh in range(4) for half in range(2)])
                self.ppool = [6, 7]
                self.pidx = 0
                w, Tw = self.wload(w_in, O_IW, 16)
                for i in range(NB):
                    pb, Tp = self.pbank()
                    for k in range(16):
                        self.mm(pb[:, 0:16], hT[:, k, i * 128:(i + 1) * 128], w[:, k, 0:16], k == 0, k == 15, [ThT, Tw], [Tp])
                    self.ts(iwf[:, i, :], pb[:, 0:16], float(0.25 * 0.125), None, ALU.mult, None, [Tp], [Tiw])
                self.tap("iqT", iqT, TiqT, [128, 8, NOWN])
                self.tap("qT", qT, TqT, [128, 8, NOWN])
                self.tap("iw", iwf, Tiw, [128, NB, 16])
                S.barrier()
                if self.stop == 'B1a':
                    self._finish(y)
                    return nc

                save_top = self.top
                self.top = wbase
                relu = [(self.bf(512), T(f"relu{i}")) for i in range(4)]
                Pb = [(self.bf(512), T(f"P{i}")) for i in range(4)]
                negc = self.bf(1024); Tnc = T("negc")
                self.tmpf = [(self.f32(512), T(f"tf{i}")) for i in range(4)]
                junkb = self.bf(SEQ); Tjb = T("junkb")
                assert self.top <= b1_top
                self.top = save_top
                Sbufs = [(Sbuf, TS), (Sbuf2, TS2)]
                BK = lambda b: (self.psum[:, b, :], self.pT[b])

                def emit_diag(i):
                    dg, Tdg = diags[i % 2]
                    self.tt(dg, self.ident.unsqueeze(1).to_broadcast([128, 16, 128]),
                            iwf[:, i, :].unsqueeze(2).to_broadcast([128, 16, 128]), ALU.mult, [Tconst, Tiw], [Tdg])

                def emit_idx(i):
                    Sb, TSb = Sbufs[i % 2]
                    dg, Tdg = diags[i % 2]
                    qs = slice(i * 128, (i + 1) * 128)
                    steps = [(c, h) for c in range(i + 1) for h in range(16)]
                    xb_ = (0, 1, 5)

                    def A(n):
                        c, h = steps[n]
                        p, sub = h // 2, h % 2
                        ps_ = slice(sub * 64, (sub + 1) * 64)
                        xh, Txh = BK(xb_[n % 3])
                        self.mm(xh, iqT[ps_, p, qs], ikT[ps_, c * 512:(c + 1) * 512], True, True, [TiqT, Tik], [Txh])

                    A(0)
                    if len(steps) > 1:
                        A(1)
                    for n, (c, h) in enumerate(steps):
                        cs = slice(c * 512, (c + 1) * 512)
                        acc, Tacc = BK(2 + (c % 2))
                        last = (c == i)
                        xh, Txh = BK(xb_[n % 3])
                        rl, Trl = relu[n % 4]
                        self.act(rl, xh, AF.Relu, [Txh], [Trl])
                        self.mm(acc, dg[:, h, :], rl, h == 0, (h == 15 and not last), [Tdg, Trl], [Tacc])
                        if h == 15:
                            if last:
                                self.mm(acc, self.ident, cmnegb, False, True, [Tconst, Tcm], [Tacc])
                            self.copy(Sb[:, cs], acc, [Tacc], [TSb], eng='scalar')
                        if n + 2 < len(steps):
                            A(n + 2)

                def emit_bis(i):
                    Sb, TSb = Sbufs[i % 2]
                    nk = 512 * (i + 1)
                    Sv = Sb[:, 0:nk]
                    hi0, lo0, mid, cnt, tmp, thr = (bis[:, j:j + 1] for j in range(6))
                    S.op('vector', lambda e: e.tensor_reduce(out=hi0, in_=Sv, axis=mybir.AxisListType.X, op=ALU.max), [TSb], [Tbis])
                    t_f, Tt_f = self.tmpf[0]
                    ls = slice(i * 512, (i + 1) * 512)
                    self.ts(t_f, Sb[:, ls], -1.0e29, 2.0e30, ALU.is_lt, ALU.mult, [TSb], [Tt_f])
                    self.tt(t_f, t_f, Sb[:, ls], ALU.add, [Tt_f, TSb], [Tt_f])
                    S.op('vector', lambda e: e.tensor_reduce(out=lo0, in_=t_f, axis=mybir.AxisListType.X, op=ALU.min), [Tt_f], [Tbis])
                    if i > 0:
                        Su = Sb[:, 0:i * 512]
                        S.op('vector', lambda e: e.tensor_reduce(out=tmp, in_=Su, axis=mybir.AxisListType.X, op=ALU.min), [TSb], [Tbis])
                        self.tt(lo0, lo0, tmp, ALU.min, [Tbis], [Tbis])
                    self.tt(tmp, lo0, lo0, ALU.mult, [Tbis], [Tbis])
                    self.stt(tmp, hi0, hi0, tmp, ALU.mult, ALU.add, [Tbis], [Tbis])
                    self.ts(tmp, tmp, 1.0, -1.0e-4, ALU.add, ALU.mult, [Tbis], [Tbis])
                    self.tt(lo0, lo0, tmp, ALU.add, [Tbis], [Tbis])
                    self.tt(tmp, hi0, lo0, ALU.subtract, [Tbis], [Tbis])
                    self.ts(tmp, tmp, 1.0e-6, None, ALU.add, None, [Tbis], [Tbis])
                    self.ts(Wtab, pow2, tmp, None, ALU.mult, None, [Tconst, Tbis], [TW])
                    self.tt(mid, lo0, Wtab[:, 0:1], ALU.add, [Tbis, TW], [Tbis])
                    for k in range(NIT):
                        self.ts(junkb[:, 0:nk], Sv, mid, 0.0, ALU.is_ge, ALU.add, [TSb, Tbis], [Tjb, Tbis], accum=cnt)
                        self.ts(tmp, cnt, TOPK - 0.5, Wtab[:, k:k + 1], ALU.is_ge, ALU.mult, [Tbis, TW], [Tbis])
                        self.stt(mid, tmp, Wtab[:, k + 1:k + 2], mid, ALU.subtract, ALU.add, [Tbis, TW], [Tbis])
                    self.tt(thr, mid, Wtab[:, NIT:NIT + 1], ALU.subtract, [Tbis, TW], [Tbis])
                    if i == 3:
                        self.tap("S3", Sb, TSb, [128, SEQ])
                        self.tap("thr3", bis, Tbis, [128, 16])
                    nt = 4 * (i + 1)
                    for c8 in range((nt + 7) // 8):
                        n8 = min(8, nt - c8 * 8)
                        self.ts(negc[:, 0:n8 * 128], Sb[:, c8 * 1024:c8 * 1024 + n8 * 128], thr, -30000.0, ALU.is_lt, ALU.mult, [TSb, Tbis], [Tnc])
                        pbk, Tp = BK(4 + (c8 % 2))
                        pb = pbk.bitcast(BF16)
                        for t8 in range(n8):
                            self.transpose(pb[:, t8 * 128:(t8 + 1) * 128], negc[:, t8 * 128:(t8 + 1) * 128], [Tnc], [Tp], inc=(t8 == n8 - 1))
                        self.copy(negmT[:, c8 * 8:c8 * 8 + n8, :], pb[:, 0:n8 * 128].rearrange("p (c t) -> p c t", c=n8), [Tp], [TnT], eng='vector')

                def emit_att(i):
                    nt = 4 * (i + 1)
                    qs = slice(i * 128, (i + 1) * 128)
                    steps = [(kvh, kt) for kvh in range(2) for kt in range(nt)]

                    def L(n):
                        kvh, kt = steps[n]
                        lg, Tlg = BK(4 + (n % 2))
                        lg4 = lg.rearrange("p (h t) -> p h t", h=4)
                        self.mm(lg4, kT[:, kvh, kt * 128:(kt + 1) * 128], qT[:, 4 * kvh:4 * kvh + 4, qs], True, False, [TkT, TqT], [Tlg])
                        self.mm(lg4, self.ident, negmT[:, kt, :].unsqueeze(1).to_broadcast([128, 4, 128]), False, True, [Tconst, TnT], [Tlg])

                    L(0)
                    for n, (kvh, kt) in enumerate(steps):
                        oacc, Toa = BK(6 if kvh == 0 else 2)
                        sacc, Tsa = BK(7 if kvh == 0 else 3)
                        lg, Tlg = BK(4 + (n % 2))
                        pbuf, TP = Pb[n % 4]
                        self.act(pbuf, lg, AF.Exp, [Tlg], [TP])
                        if n + 1 < len(steps):
                            L(n + 1)
                        self.mm(oacc, Vt[:, kt, kvh * 128:(kvh + 1) * 128], pbuf, kt == 0, kt == nt - 1, [TV, TP], [Toa])
                        self.mm(sacc, ones, pbuf, kt == 0, kt == nt - 1, [Tconst, TP], [Tsa])
                        if kt == nt - 1:
                            rs, Trs = self.tmpf[1 + kvh]
                            self.recip(rs, sacc, [Tsa], [Trs])
                            self.tt(BT[:, 4 * kvh:4 * kvh + 4, qs], oacc.rearrange("p (h t) -> p h t", h=4),
                                    rs.rearrange("p (h t) -> p h t", h=4), ALU.mult, [Toa, Trs], [TBT])

                emit_diag(0)
                emit_idx(0)
                for i in range(NB):
                    if i + 1 < NB:
                        emit_diag(i + 1)
                        emit_idx(i + 1)
                    emit_bis(i)
                    emit_att(i)
                self.ppool = list(range(8))
                self.tap("BT0", BT, TBT, [128, 8, NOWN])
                S.barrier()
                if self.stop == 'B1e':
                    self._finish(y)
                    return nc

                self.tmpb = [(self.bf_at(b1e_top + 256 * j, 512), T(f"tb{j}")) for j in range(4)]
                self.gate_mul(w_in, O_BZ, hT, ThT, BT, TBT)
                self.tap("BT", BT, TBT, [128, 8, NOWN])
                S.barrier()
                if self.stop == 'B1':
                    self._finish(y)
                    return nc

                self.top = b1_top
                MT = self.bf(8 * NOWN, (8, NOWN)); TMT = T("MT")
                AT = self.bf(8 * NOWN, (8, NOWN)); TAT = T("AT")
                b2_top = self.top
                mqT = self.bf(8 * NOWN, (8, NOWN)); TmqT = T("mqT")
                memT = self.bf(16 * 256, (16, 256)); TmemT = T("memT")
                kmT = self.bf(8 * 256, (8, 256)); TkmT = T("kmT")
                vm = self.bf(2 * 1024, (2, 1024)); Tvm = T("vm")
                sv_top = self.top
                self.top = base_top
                gbc = self.f32(D); Tg = T("gmem")
                self.dma('scalar', gbc, gmem_d, 'c1', writes=[Tg])
                self.xbufs = [(self.f32(D), T(f"x{i}")) for i in range(3)]
                self.xsbufs = [(self.bf(D), T("xs0")), (self.bf(D), T("xs1"))]
                assert self.top <= kvi_top
                self.top = sv_top
                self.small = [(self.f32(8), T(f"sm{i}")) for i in range(4)]
                self.tmpf = [(self.f32(512), T(f"tf{i}")) for i in range(6)]
                self.tmpb = [(self.bf(512), T(f"tb{i}")) for i in range(4)]
                Pb = [(self.bf(512), T(f"P{i}")) for i in range(2)]
                gms = self.f32(8); Tgms = T("gms")
                self.ts(gms[:, 0:2], cols[:, C_GMQ0:C_GMQ0 + 2], float(256 ** -0.5), None, ALU.mult, None, [Tconst], [Tgms])
                mem_tiles = [(memx[ti * 128:(ti + 1) * 128, :], memT[:, :, ti * 128:(ti + 1) * 128], TmemT) for ti in range(2)]
                for _ in self.hT_stream(mem_tiles, gbc, Tg, [4, 5]):
                    pass
                for ch in range(2):
                    w, Tw = self.wload(w_mkv, ch * 512, 512)
                    for hh in range(2):
                        h = ch * 2 + hh
                        raws = []
                        for dc in range(2):
                            raw, Traw = self.pbank()
                            self.projT(w, Tw, (hh * 2 + dc) * 128, 128, memT, TmemT, raw[:, 0:256], Traw)
                            raws.append((raw[:, 0:256], Traw))
                        rstd, Trs = self.rms_T([r for r, _ in raws], [t for _, t in raws], ones, 1.0 / 256, 256)
                        for dc in range(2):
                            self.stt(kmT[:, 2 * h + dc, :], raws[dc][0], cols[:, C_GMK0 + dc:C_GMK0 + dc + 1], rstd, ALU.mult, ALU.mult,
                                     [raws[dc][1], Trs, Tconst], [TkmT])
                for ch in range(2):
                    w, Tw = self.wload(w_mkv, 1024 + ch * 512, 512)
                    for mt in range(2):
                        pb, Tp = self.pbank()
                        for k in range(16):
                            self.mm(pb, memT[:, k, mt * 128:(mt + 1) * 128], w[:, k, :], k == 0, k == 15, [TmemT, Tw], [Tp])
                        self.copy(vm[:, mt, ch * 512:(ch + 1) * 512], pb, [Tp], [Tvm], eng='scalar')
                for ch in range(2):
                    w, Tw = self.wload(w_in, O_MQ + ch * 512, 512)
                    for hh in range(2):
                        h = ch * 2 + hh
                        for half in range(2):
                            hs = slice(half * 512, (half + 1) * 512)
                            raws = []
                            for dc in range(2):
                                raw, Traw = self.pbank()
                                self.projT(w, Tw, (hh * 2 + dc) * 128, 128, hT[:, :, hs], ThT, raw, Traw)
                                raws.append((raw, Traw))
                            rstd, Trs = self.rms_T([r for r, _ in raws], [t for _, t in raws], ones, 1.0 / 256, 512)
                            for dc in range(2):
                                self.stt(mqT[:, 2 * h + dc, hs], raws[dc][0], gms[:, dc:dc + 1], rstd, ALU.mult, ALU.mult,
                                         [raws[dc][1], Trs, Tgms], [TmqT])
                for h in range(4):
                    for half in range(2):
                        hs = slice(half * 512, (half + 1) * 512)
                        for mt in range(2):
                            lg, Tlg = self.pbank()
                            for dc in range(2):
                                self.mm(lg, kmT[:, 2 * h + dc, mt * 128:(mt + 1) * 128], mqT[:, 2 * h + dc, hs], dc == 0, dc == 1, [TkmT, TmqT], [Tlg])
                            self.act(Pb[mt][0], lg, AF.Exp, [Tlg], [Pb[mt][1]])
                        sm, Tsm = self.pbank()
                        for mt in range(2):
                            self.mm(sm, ones, Pb[mt][0], mt == 0, mt == 1, [Tconst, Pb[mt][1]], [Tsm])
                        rs, Trs = self.tmpf[self.tidx % len(self.tmpf)]
                        self.tidx += 1
                        self.recip(rs, sm, [Tsm], [Trs])
                        for dc in range(2):
                            po, Tpo = self.pbank()
                            for mt in range(2):
                                self.mm(po, vm[:, mt, h * 256 + dc * 128:h * 256 + (dc + 1) * 128], Pb[mt][0], mt == 0, mt == 1, [Tvm, Pb[mt][1]], [Tpo])
                            self.tt(MT[:, 2 * h + dc, hs], po, rs, ALU.mult, [Tpo, Trs], [TMT])
                self.tap("MT0", MT, TMT, [128, 8, NOWN])
                self.gate_mul(w_in, O_MZ, hT, ThT, MT, TMT)
                self.tap("MT", MT, TMT, [128, 8, NOWN])
                S.barrier()
                if self.stop == 'B2':
                    self._finish(y)
                    return nc

                self.top = b2_top
                vln = self.bf(NB * 1024, (NB, 1024)); Tvln = T("vln")
                gv = self.f32(1024); Tgv = T("gv")
                wsT = self.bf(8 * 128, (8, 128)); Tws = T("wsT")
                sv_top = self.top
                self.top = base_top
                wsf = self.f32(8 * 128, (8, 128))
                sbt = self.f32(8 * 128, (8, 128)); Tsbt = T("sbt")
                lng = self.f32(1024); lnb = self.f32(1024); Tln = T("ln")
                assert self.top <= kvi_top
                self.top = sv_top
                self.small = [(self.f32(16), T(f"sm{i}")) for i in range(4)]
                self.tmpf = [(self.f32(1024), T(f"tf{i}")) for i in range(3)]
                self.tmpb = [(self.bf(512), T(f"tb{i}")) for i in range(4)]
                self.dma('sync', wsf, wsT_d.rearrange("p (g t) -> p g t", g=8), 'c2', writes=[Tws])
                self.dma('sync', sbt, sbt_d.rearrange("p (g t) -> p g t", g=8), 'c2', writes=[Tsbt])
                self.dma('sync', lng, lng_d, 'c2', writes=[Tln])
                self.dma('sync', lnb, lnb_d, 'c2', writes=[Tln])
                S.op('gpsimd', lambda e: e.affine_select(out=wsf, in_=wsf, pattern=[[0, 8], [1, 128]], compare_op=ALU.is_ge,
                                                         fill=0.0, base=0, channel_multiplier=-1), [Tws], [Tws])
                self.copy(wsT, wsf, [Tws], [Tws])
                for ch in range(2):
                    w, Tw = self.wload(w_in, O_AU + ch * 512, 512)
                    for cc in range(4):
                        c = ch * 4 + cc
                        for half in range(2):
                            hs = slice(half * 512, (half + 1) * 512)
                            raw, Traw = self.pbank()
                            self.projT(w, Tw, cc * 128, 128, hT[:, :, hs], ThT, raw, Traw)
                            self.act(AT[:, c, hs], raw, AF.Gelu, [Traw], [TAT])
                wv0, Twv0 = self.wload(w_in, O_AV, 512)
                wv1, Twv1 = self.wload(w_in, O_AV + 512, 512)
                for i in range(NB):
                    for ch, (w, Tw) in enumerate(((wv0, Twv0), (wv1, Twv1))):
                        pb, Tp = self.pbank()
                        for k in range(16):
                            self.mm(pb, hT[:, k, i * 128:(i + 1) * 128], w[:, k, :], k == 0, k == 15, [ThT, Tw], [Tp])
                        self.act(gv[:, ch * 512:(ch + 1) * 512], pb, AF.Gelu, [Tp], [Tgv])
                    sm, Tsm = self.small[self.sidx % len(self.small)]
                    self.sidx += 1
                    S.op('vector', lambda e, sm=sm: e.bn_stats(out=sm[:, 0:6], in_=gv[:, 0:512]), [Tgv], [Tsm])
                    S.op('vector', lambda e, sm=sm: e.bn_stats(out=sm[:, 6:12], in_=gv[:, 512:1024]), [Tgv], [Tsm])
                    S.op('vector', lambda e, sm=sm: e.bn_aggr(out=sm[:, 12:14], in_=sm[:, 0:12].rearrange("p (a b) -> p a b", a=2)), [Tsm], [Tsm])
                    self.act(sm[:, 14:15], sm[:, 13:14], AF.Sqrt, [Tsm, Tconst], [Tsm], bias=cols[:, C_EPS:C_EPS + 1], scale=1.0)
                    self.recip(sm[:, 15:16], sm[:, 14:15], [Tsm], [Tsm])
                    t1, T1 = self.tmpf[i % 3]
                    self.ts(t1, gv, sm[:, 12:13], sm[:, 15:16], ALU.subtract, ALU.mult, [Tgv, Tsm], [T1])
                    self.tt(t1, t1, lng, ALU.mult, [T1, Tln], [T1], eng='gpsimd')
                    self.tt(vln[:, i, :], t1, lnb, ALU.add, [T1, Tln], [Tvln], eng='gpsimd')
                for i in range(NB):
                    qs = slice(i * 128, (i + 1) * 128)
                    for gh in range(2):
                        pb, Tp = self.pbank()
                        for gg in range(4):
                            g = gh * 4 + gg
                            self.mm(pb[:, gg * 128:(gg + 1) * 128], vln[:, i, g * 128:(g + 1) * 128], wsT[:, g, :], True, True, [Tvln, Tws], [Tp], inc=(gg == 3))
                        t1, T1 = self.tmpf[(i * 2 + gh) % 3]
                        self.tt(t1[:, 0:512], pb, sbt[:, gh * 4:(gh + 1) * 4, :].rearrange("p g t -> p (g t)"), ALU.add, [Tp, Tsbt], [T1])
                        self.tt(AT[:, gh * 4:(gh + 1) * 4, qs], t1[:, 0:512].rearrange("p (g t) -> p g t", g=4), AT[:, gh * 4:(gh + 1) * 4, qs],
                                ALU.mult, [T1, TAT], [TAT])
                self.tap("AT0", AT, TAT, [128, 8, NOWN])
                self.gate_mul(w_in, O_AZ, hT, ThT, AT, TAT)
                self.tap("AT", AT, TAT, [128, 8, NOWN])
                S.barrier()
                if self.stop == 'B3':
                    self._finish(y)
                    return nc

                self.top = b2_top
                mergedT = self.bf_at(base_top, 16 * NOWN, (16, NOWN)); Tmg = T("merged")
                accm = self.f32(4 * NOWN, (4, NOWN)); Tacc = T("accm")
                sgb = [(self.f32(512), T(f"sg{i}")) for i in range(3)]
                wbr = [(self.bf(8 * 512, (8, 512)), T(f"wbr{i}")) for i in range(2)]
                branches = ((O_GA, w_a, AT, TAT), (O_GB, w_b, BT, TBT), (O_GM, w_m, MT, TMT))
                nbr = 0
                for nq in range(4):
                    for bi, (og, wb_d, XT, TXT) in enumerate(branches):
                        wg, Twg = self.wload(w_in, og + nq * 512, 512)
                        wb_, Twb = wbr[nbr % 2]
                        nbr += 1
                        self.dma('gpsimd', wb_, wb_d.rearrange("(c p) n -> p c n", p=128)[:, :, nq * 512:(nq + 1) * 512], f'wb{nbr % 2}', writes=[Twb])
                        for nn in range(4):
                            hss = [slice(0, 512), slice(512, 1024)]
                            pgs = [self.pbank() for _ in range(2)]
                            for k in range(16):
                                for half in range(2):
                                    self.mm(pgs[half][0], wg[:, k, nn * 128:(nn + 1) * 128], hT[:, k, hss[half]], k == 0, k == 15,
                                            [Twg, ThT], [pgs[half][1]])
                            pys = [self.pbank() for _ in range(2)]
                            for c in range(8):
                                for half in range(2):
                                    self.mm(pys[half][0], wb_[:, c, nn * 128:(nn + 1) * 128], XT[:, c, hss[half]], c == 0, c == 7,
                                            [Twb, TXT], [pys[half][1]])
                            for half in range(2):
                                hs = hss[half]
                                pg, Tpg = pgs[half]
                                py, Tpy = pys[half]
                                sg, Tsg = sgb[(nn * 2 + half) % 3]
                                self.act(sg, pg, AF.Sigmoid, [Tpg], [Tsg])
                                if bi == 0:
                                    self.tt(accm[:, nn, hs], py, sg, ALU.mult, [Tpy, Tsg], [Tacc])
                                else:
                                    self.tt(sg, py, sg, ALU.mult, [Tpy, Tsg], [Tsg])
                                    if bi == 1:
                                        self.tt(accm[:, nn, hs], accm[:, nn, hs], sg, ALU.add, [Tacc, Tsg], [Tacc], eng='gpsimd')
                                    else:
                                        self.tt(mergedT[:, nq * 4 + nn, hs], accm[:, nn, hs], sg, ALU.add, [Tacc, Tsg], [Tmg], eng='gpsimd')
                self.tap("merged", mergedT, Tmg, [128, 16, NOWN])
                S.barrier()
                if self.stop == 'B4':
                    self._finish(y)
                    return nc

                self.top = b2_top
                xo = [(self.f32(512), T(f"xo{i}")) for i in range(3)]
                ot = [(self.f32(512), T(f"ot{i}")) for i in range(3)]
                n_o = 0
                for dch in range(4):
                    ds_ = slice(dch * 512, (dch + 1) * 512)
                    w, Tw = self.wload(w_out, dch * 512, 512)
                    for i in range(NB):
                        rows = slice(i * 128, (i + 1) * 128)
                        xo_, Txo = xo[n_o % 3]
                        ot_, Tot = ot[n_o % 3]
                        self.dma('sync', xo_, xown[rows, ds_], f'xo{n_o % 3}', writes=[Txo])
                        pb, Tp = self.pbank()
                        for k in range(16):
                            self.mm(pb, mergedT[:, k, rows], w[:, k, :], k == 0, k == 15, [Tmg, Tw], [Tp])
                        self.tt(ot_, pb, xo_, ALU.add, [Tp, Txo], [Tot])
                        self.final_toks.append(self.dma('scalar', y[rows, ds_], ot_, f'yo{n_o % 3}', reads=[Tot]))
                        n_o += 1
                self._finish()
            except StopBuild:
                self.S.barrier()
                self._finish(y)
        return nc

    def _finish(self, y=None):
        S = self.S
        if y is not None:
            z, Tz = self.f32_at(0, 8), T("z")
            self.final_toks.append(self.dma('sync', y[0:128, 0:8], z, 'yo0', reads=[Tz]))
        last = {}
        for k, v in self.final_toks:
            last[k] = max(last.get(k, 0), v)
        S.streams['sync'].append((list(last.items()), None, None, 0))
        S.emit()

    def gate_mul(self, wd, col0, hT, ThT, XT, TXT):
        for ch in range(2):
            w, Tw = self.wload(wd, col0 + ch * 512, 512)
            for cc in range(4):
                c = ch * 4 + cc
                for half in range(2):
                    hs = slice(half * 512, (half + 1) * 512)
                    raw, Traw = self.pbank()
                    self.projT(w, Tw, cc * 128, 128, hT[:, :, hs], ThT, raw, Traw)
                    z, Tz = self.tmpb[self.bidx % len(self.tmpb)]
                    self.bidx += 1
                    self.act(z, raw, AF.Silu, [Traw], [Tz])
                    self.tt(XT[:, c, hs], XT[:, c, hs], z, ALU.mult, [TXT, Tz], [TXT], eng=('vector' if (cc + half) % 2 == 0 else 'gpsimd'))


def _host_consts():
    theta = 500000.0
    invk = np.zeros(128, np.float32)
    invk[:32] = (theta ** (-(np.arange(32) % 16).astype(np.float32) / 16.0)).astype(np.float32)
    invi = np.zeros(128, np.float32)
    for base in (0, 64):
        invi[base:base + 16] = (theta ** (-(np.arange(16) % 8).astype(np.float32) / 8.0)).astype(np.float32)
    Pk = np.zeros((128, 128), np.float32)
    for m in range(16):
        Pk[m, m + 16] = -1.0
        Pk[m + 16, m] = 1.0
    Pi = np.zeros((128, 128), np.float32)
    for base in (0, 64):
        for m in range(8):
            Pi[base + m, base + m + 8] = -1.0
            Pi[base + m + 8, base + m] = 1.0
    blk = np.zeros((128, 128), np.float32)
    blk[:64, :64] = 1.0 / 64
    blk[64:, 64:] = 1.0 / 64
    cmat = np.concatenate([Pk.T, Pi.T, blk], axis=1).astype(np.float32)
    pow2 = np.broadcast_to((2.0 ** -(np.arange(NIT + 2) + 1.0)).astype(np.float32), (128, NIT + 2)).copy()
    return invk, invi, cmat, pow2


_PROG = {}


def _get_prog(dbg=()):
    key = tuple(sorted(dbg))
    if key not in _PROG:
        b = Builder(dbg)
        nc = b.build()
        _PROG[key] = (nc, b)
    return _PROG[key]


def make_in_maps(x, mem, positions, norm_gain, w_in, gmlp_ln_gain, gmlp_ln_bias, spatial_w, spatial_b, w_branch_a,
                 q_norm_gain, k_norm_gain, idx_k_ln_gain, idx_k_ln_bias, w_branch_b, mem_norm_gain, w_mem_kv,
                 mem_q_norm_gain, mem_k_norm_gain, w_branch_m, w_out):
    f = lambda a: np.ascontiguousarray(np.asarray(a), dtype=np.float32)
    x = f(x); mem = f(mem)
    positions = np.ascontiguousarray(np.asarray(positions), dtype=np.int32)
    invk, invi, cmat, pow2 = _host_consts()
    cols = np.zeros((128, NCOLS), np.float32)
    cols[:, C_GQ] = f(q_norm_gain)[0]
    cols[:, C_GK] = f(k_norm_gain)[0]
    cols[:, C_GMQ0] = f(mem_q_norm_gain)[0][:128]
    cols[:, C_GMQ1] = f(mem_q_norm_gain)[0][128:]
    cols[:, C_GMK0] = f(mem_k_norm_gain)[0][:128]
    cols[:, C_GMK1] = f(mem_k_norm_gain)[0][128:]
    cols[:, C_IKG] = np.tile(f(idx_k_ln_gain)[0], 2)
    cols[:, C_IKB] = np.tile(f(idx_k_ln_bias)[0], 2)
    cols[:, C_INVK] = invk
    cols[:, C_INVI] = invi
    cols[:, C_EPS] = EPS
    cols[:, C_NPI] = -PI
    rep = lambda v, n: np.ascontiguousarray(np.broadcast_to(f(v).reshape(1, -1), (128, n)))
    shared = {
        "w_in": f(w_in)[0], "w_a": f(w_branch_a)[0], "w_b": f(w_branch_b)[0], "w_m": f(w_branch_m)[0],
        "w_out": f(w_out)[0], "w_mkv": f(w_mem_kv)[0],
        "gbc": rep(norm_gain[0], D), "gmembc": rep(mem_norm_gain[0], D), "cols": cols,
        "lng": rep(gmlp_ln_gain[0], 1024), "lnb": rep(gmlp_ln_bias[0], 1024),
        "wsT": np.ascontiguousarray(f(spatial_w)[0].transpose(2, 0, 1)).reshape(128, 1024),
        "sbt": rep(f(spatial_b)[0].reshape(-1), 1024),
        "cmat": cmat, "pow2": pow2,
    }
    in_maps = []
    tt = np.arange(128)
    for c in range(8):
        b, j = c // 4, c % 4
        own = np.concatenate([np.arange((j + 4 * i) * 128, (j + 4 * i + 1) * 128) for i in range(NB)])
        cm = np.zeros((128, 4, 128), np.float32)
        for ktl in range(4):
            if ktl > j:
                cm[:, ktl, :] = NEG
            elif ktl == j:
                cm[:, ktl, :] = np.where(tt[None, :] <= tt[:, None], 0.0, NEG)
        m = dict(shared)
        m["xall"] = x[b]
        m["xown"] = np.ascontiguousarray(x[b][own])
        m["posall"] = np.ascontiguousarray(np.broadcast_to(positions[b][None, :], (128, SEQ)))
        m["posown"] = np.ascontiguousarray(np.broadcast_to(positions[b][own][None, :], (128, NOWN)))
        m["memx"] = mem[b]
        m["cmneg"] = cm.reshape(128, 512)
        in_maps.append(m)
    return in_maps


def kernel(**inputs):
    nc, _ = _get_prog()
    in_maps = make_in_maps(**inputs)
    res = run_bass_kernel_spmd(nc, in_maps, core_ids=list(range(8)))
    out = np.zeros((2, SEQ, D), np.float32)
    for c in range(8):
        b, j = c // 4, c % 4
        yc = np.asarray(res.results[c]["y"])
        for i in range(NB):
            g = j + 4 * i
            out[b, g * 128:(g + 1) * 128, :] = yc[i * 128:(i + 1) * 128, :]
    return out
```

```python
import numpy as np
from contextlib import ExitStack
import concourse.bass as bass
import concourse.mybir as mybir
from concourse.bass_utils import run_bass_kernel_spmd

F32 = mybir.dt.float32
BF16 = mybir.dt.bfloat16
I32 = mybir.dt.int32
AF = mybir.ActivationFunctionType
ALU = mybir.AluOpType

D = 2048
SEQ = 4096
NB = 8
NOWN = NB * 128
EPS = 1e-6
NIT = 20
TOPK = 256
NEG = -1.0e30
PI = float(np.pi)

O_AU, O_AV, O_AZ = 0, 1024, 2048
O_BQ, O_BK, O_BV, O_BZ = 3072, 4096, 4352, 4608
O_IQ, O_IK, O_IW = 5632, 6656, 6720
O_MQ, O_MZ = 6736, 7760
O_GA, O_GB, O_GM = 8784, 10832, 12880
D_IN = 14928

C_GQ, C_GK, C_GMQ0, C_GMQ1, C_GMK0, C_GMK1, C_IKG, C_IKB, C_INVK, C_INVI, C_EPS, C_NPI = range(12)
NCOLS = 16


class T:
    __slots__ = ("w", "r", "name", "excl")

    def __init__(self, name="", excl=False):
        self.w = None
        self.r = []
        self.name = name
        self.excl = excl


class Sched:
    ENG = ['sync', 'scalar', 'vector', 'gpsimd', 'tensor']

    def __init__(self, nc, stack):
        self.nc = nc
        self.stack = stack
        self.streams = {e: [] for e in self.ENG}
        self.count = {e: 0 for e in self.ENG}
        self.sems = {}
        self.waited = {e: {} for e in self.ENG}
        self.dcount = {}
        for e in self.ENG:
            self.sems[e] = stack.enter_context(nc.semaphore("s_" + e))

    def _deps(self, eng, reads, writes):
        need = {}

        def add(tok, kind):
            if tok is None:
                return
            k, v = tok
            if k == eng and (eng == 'tensor' or kind == 'war'):
                return
            if need.get(k, 0) < v:
                need[k] = v
        for r in reads:
            add(r.w, 'raw')
        for w in writes:
            add(w.w, 'waw')
            for t in w.r:
                add(t, 'war')
        waits = []
        for k, v in need.items():
            if k in self.dcount:
                v = self.dcount[k]
            if self.waited[eng].get(k, 0) >= v:
                continue
            self.waited[eng][k] = v
            waits.append((k, v))
        return waits

    def _record(self, tok, reads, writes):
        for r in reads:
            r.r.append(tok)
            if len(r.r) > 64:
                mx = {}
                for k, v in r.r:
                    if mx.get(k, 0) < v:
                        mx[k] = v
                r.r = list(mx.items())
        for w in writes:
            w.w = tok
            w.r = []

    def op(self, eng, fn, reads=(), writes=(), inc=True):
        if any(r.excl for r in reads):
            writes = list(writes) + [r for r in reads if r.excl and r not in writes]
            reads = [r for r in reads if not r.excl]
        waits = self._deps(eng, reads, writes)
        tok = (eng, self.count[eng] + 1)
        if inc:
            self.count[eng] += 1
        self.streams[eng].append((waits, fn, eng if inc else None, 1))
        self._record(tok, reads, writes)
        return tok

    def dma(self, eng, fn, key, reads=(), writes=()):
        if key not in self.sems:
            self.sems[key] = self.stack.enter_context(self.nc.semaphore("d_" + key))
            self.dcount[key] = 0
        waits = self._deps(eng, reads, writes)
        self.dcount[key] += 16
        tok = (key, self.dcount[key])
        self.streams[eng].append((waits, fn, key, 16))
        self._record(tok, reads, writes)
        return tok

    def barrier(self):
        toks = [(e, self.count[e]) for e in self.ENG if self.count[e] > 0]
        toks += [(k, v) for k, v in self.dcount.items() if v > 0]
        for e in self.ENG:
            waits = []
            for k, v in toks:
                if k == e:
                    continue
                if self.waited[e].get(k, 0) >= v:
                    continue
                self.waited[e][k] = v
                waits.append((k, v))
            if waits:
                self.streams[e].append((waits, None, None, 0))

    def emit(self):
        nc = self.nc
        S = self
        with nc.Block() as block:
            def mk(ename):
                def body(e):
                    for waits, fn, inc, n in S.streams[ename]:
                        for k, v in waits:
                            e.wait_ge(S.sems[k], v)
                        if fn is not None:
                            ins = fn(e)
                            if inc is not None:
                                ins.then_inc(S.sems[inc], n)
                return body
            block.sync(mk('sync'))
            block.scalar(mk('scalar'))
            block.vector(mk('vector'))
            block.gpsimd(mk('gpsimd'))
            block.tensor(mk('tensor'))


class StopBuild(Exception):
    pass


class Builder:
    def chk(self, name):
        if self.stop == name:
            raise StopBuild()

    def __init__(self, dbg=(), stop=None):
        self.dbg = set(dbg)
        self.stop = stop
        self.nc = bass.Bass("TRN2", target_bir_lowering=False)
        self.dbg_outs = {}

    def alloc(self, nwords):
        off = self.top
        self.top += (nwords + 7) // 8 * 8
        assert self.top <= self.AW, f"arena overflow {self.top} > {self.AW}"
        return off

    def f32(self, nwords, shape=None):
        off = self.alloc(nwords)
        return self.f32_at(off, nwords, shape)

    def f32_at(self, off, nwords, shape=None):
        ap = self.arena[:, off:off + nwords]
        if shape is not None:
            ap = self._reshape(ap, shape)
        return ap

    def bf(self, nelem, shape=None):
        assert nelem % 2 == 0
        off = self.alloc(nelem // 2)
        return self.bf_at(off, nelem, shape)

    def bf_at(self, off, nelem, shape=None):
        ap = self.arena[:, off:off + nelem // 2].bitcast(BF16)
        if shape is not None:
            ap = self._reshape(ap, shape)
        return ap

    @staticmethod
    def _reshape(ap, shape):
        if len(shape) == 2:
            return ap.rearrange("p (a b) -> p a b", a=shape[0], b=shape[1])
        if len(shape) == 3:
            return ap.rearrange("p (a b c) -> p a b c", a=shape[0], b=shape[1], c=shape[2])
        raise ValueError

    def pbank(self):
        b = self.ppool[self.pidx % len(self.ppool)]
        self.pidx += 1
        return self.psum[:, b, :], self.pT[b]

    def pbank_bf(self):
        b = self.ppool[self.pidx % len(self.ppool)]
        self.pidx += 1
        return self.psum[:, b, :].bitcast(BF16), self.pT[b]

    def act(self, out, in_, func, reads, writes, bias=None, scale=None, accum=None):
        kw = {}
        if bias is not None:
            kw['bias'] = bias
        if scale is not None:
            kw['scale'] = scale
        if accum is not None:
            kw['accum_out'] = accum
        return self.S.op('scalar', lambda e: e.activation(out=out, in_=in_, func=func, **kw), reads, writes)

    def tt(self, out, a, b, op, reads, writes, eng='vector'):
        return self.S.op(eng, lambda e: e.tensor_tensor(out=out, in0=a, in1=b, op=op), reads, writes)

    def ts(self, out, a, s1, s2, op0, op1, reads, writes, eng='vector', accum=None):
        if op1 is None:
            return self.S.op(eng, lambda e: e.tensor_scalar(out=out, in0=a, scalar1=s1, scalar2=None, op0=op0), reads, writes)
        if accum is not None:
            return self.S.op(eng, lambda e: e.tensor_scalar(out=out, in0=a, scalar1=s1, scalar2=s2, op0=op0, op1=op1, accum_out=accum), reads, writes)
        return self.S.op(eng, lambda e: e.tensor_scalar(out=out, in0=a, scalar1=s1, scalar2=s2, op0=op0, op1=op1), reads, writes)

    def stt(self, out, a, s, b, op0, op1, reads, writes, eng='vector'):
        return self.S.op(eng, lambda e: e.scalar_tensor_tensor(out=out, in0=a, scalar=s, in1=b, op0=op0, op1=op1), reads, writes)

    def copy(self, out, in_, reads, writes, eng='vector'):
        if eng == 'scalar':
            return self.S.op('scalar', lambda e: e.activation(out=out, in_=in_, func=AF.Copy), reads, writes)
        return self.S.op(eng, lambda e: e.tensor_copy(out=out, in_=in_), reads, writes)

    def recip(self, out, in_, reads, writes):
        return self.S.op('vector', lambda e: e.reciprocal(out=out, in_=in_), reads, writes)

    def mm(self, out, lhsT, rhs, start, stop, reads, writes, inc=None):
        if inc is None:
            inc = stop
        return self.S.op('tensor', lambda e: e.matmul(out, lhsT=lhsT, rhs=rhs, start=start, stop=stop), reads, writes, inc=inc)

    def transpose(self, out, in_, reads, writes, inc=True):
        ident = self.ident
        return self.S.op('tensor', lambda e: e.transpose(out=out, in_=in_, identity=ident), list(reads) + [self.Tconst], writes, inc=inc)

    def dma(self, q, out, in_, key, reads=(), writes=()):
        return self.S.dma(q, lambda e: e.dma_start(out=out, in_=in_), key, reads, writes)

    def tap(self, name, ap, Tt, shape):
        if name not in self.dbg:
            return
        o = self.nc.dram_tensor("dbg_" + name, list(shape), F32 if ap.dtype == F32 else ap.dtype, kind="ExternalOutput").ap()
        self.dbg_outs[name] = o
        tok = self.dma('sync', o, ap, 'dbg', reads=[Tt])
        self.final_toks.append(tok)

    def wload(self, dram2d, col0, ncols, krows=2048):
        i = self.widx % len(self.wbufs)
        self.widx += 1
        buf, Tb = self.wbufs[i]
        kc = krows // 128
        view = buf[:, 0:kc, 0:ncols]
        src = dram2d.rearrange("(c p) n -> p c n", p=128)[:, :, col0:col0 + ncols]
        self.dma('gpsimd', view, src, f'w{i}', writes=[Tb])
        return view, Tb

    def projT(self, w, Tw, wc0, nout, rhs3, Trhs, out_ps, Tps, kc=16):
        for k in range(kc):
            self.mm(out_ps, w[:, k, wc0:wc0 + nout], rhs3[:, k, :], k == 0, k == kc - 1, [Tw, Trhs], [Tps])

    def make_hT(self, xrows, gbc, Tg, dst3, Tdst, xi):
        xt, Txt = self.xbufs[xi % 2]
        self.dma('sync', xt, xrows, f'x{xi % 2}', writes=[Txt])
        ssq, Tss = self.small[self.sidx % len(self.small)]
        self.sidx += 1
        self.act(self.junk, xt, AF.Square, [Txt], [self.Tjunk, Tss], accum=ssq[:, 0:1])
        self.act(ssq[:, 1:2], ssq[:, 0:1], AF.Sqrt, [Tss, self.Tconst], [Tss], bias=self.cols[:, C_EPS:C_EPS + 1], scale=1.0 / D)
        self.recip(ssq[:, 2:3], ssq[:, 1:2], [Tss], [Tss])
        xs, Txs = self.xsbufs[xi % 2]
        self.stt(xs, xt, ssq[:, 2:3], gbc, ALU.mult, ALU.mult, [Txt, Tss, Tg], [Txs])
        for half in range(2):
            pb, Tp = self.pbank_bf()
            for c in range(8):
                kc = half * 8 + c
                self.transpose(pb[:, c * 128:(c + 1) * 128], xs[:, kc * 128:(kc + 1) * 128], [Txs], [Tp], inc=(c == 7))
            self.copy(dst3[:, half * 8:(half + 1) * 8, :], pb.rearrange("p (c t) -> p c t", c=8), [Tp], [Tdst],
                      eng=('scalar' if half == 0 else 'vector'))

    def hT_stream(self, tiles, gbc, Tg, banks):
        n = len(tiles)
        cols = self.cols

        def S0(t):
            xt, Txt = self.xbufs[t % 3]
            self.dma('sync', xt, tiles[t][0], f'x{t % 3}', writes=[Txt])

        def S1(t):
            xt, Txt = self.xbufs[t % 3]
            ssq, Tss = self.small[t % 4]
            xs, Txs = self.xsbufs[t % 2]
            self.act(xs, xt, AF.Square, [Txt], [Txs, Tss], accum=ssq[:, 0:1])
            self.act(ssq[:, 1:2], ssq[:, 0:1], AF.Ln, [Tss, self.Tconst], [Tss], bias=cols[:, C_EPS:C_EPS + 1], scale=1.0 / D)
            self.act(ssq[:, 2:3], ssq[:, 1:2], AF.Exp, [Tss], [Tss], scale=-0.5)

        def S2(t):
            xt, Txt = self.xbufs[t % 3]
            ssq, Tss = self.small[t % 4]
            xs, Txs = self.xsbufs[t % 2]
            self.stt(xs, xt, ssq[:, 2:3], gbc, ALU.mult, ALU.mult, [Txt, Tss, Tg], [Txs])
            for half in range(2):
                pbk, Tp = self.BK(banks[half])
                pb = pbk.bitcast(BF16)
                for c in range(8):
                    kc = half * 8 + c
                    self.transpose(pb[:, c * 128:(c + 1) * 128], xs[:, kc * 128:(kc + 1) * 128], [Txs], [Tp], inc=(c == 7))

        def S3(t):
            dst3, Tdst = tiles[t][1], tiles[t][2]
            for half in range(2):
                pbk, Tp = self.BK(banks[half])
                pb = pbk.bitcast(BF16)
                self.copy(dst3[:, half * 8:(half + 1) * 8, :], pb.rearrange("p (c t) -> p c t", c=8), [Tp], [Tdst],
                          eng=('scalar' if half == 0 else 'vector'))

        S0(0)
        for r in range(n + 2):
            if r + 1 < n:
                S0(r + 1)
            if r < n:
                S1(r)
            if 0 <= r - 2 < n:
                S3(r - 2)
            if 0 <= r - 1 < n:
                S2(r - 1)
            yield

    def trig_tables(self, posf, Tpos, invcol, n, Cout, Sout, Tout):
        tb = self.trig_scr
        Tt = self.Ttrig
        ang, ki, kf, r = tb[0][:, 0:n], tb[1][:, 0:n].bitcast(I32), tb[2][:, 0:n], tb[3][:, 0:n]
        self.ts(ang, posf, invcol, None, ALU.mult, None, [Tpos, self.Tconst], [Tt])
        for which, dst in ((0, Sout), (1, Cout)):
            if which == 1:
                self.ts(ang, ang, PI / 2, None, ALU.add, None, [Tt], [Tt])
            self.ts(ki, ang, 1.0 / (2 * PI), None, ALU.mult, None, [Tt], [Tt])
            self.copy(kf, ki, [Tt], [Tt])
            self.stt(r, kf, -2 * PI, ang, ALU.mult, ALU.add, [Tt], [Tt])
            self.ts(kf, r, PI, -2 * PI, ALU.is_gt, ALU.mult, [Tt], [Tt])
            self.tt(r, r, kf, ALU.add, [Tt], [Tt])
            self.ts(kf, r, -PI, 2 * PI, ALU.is_lt, ALU.mult, [Tt], [Tt])
            self.tt(r, r, kf, ALU.add, [Tt], [Tt])
            self.ts(r, r, -PI, PI, ALU.max, ALU.min, [Tt], [Tt])
            self.act(dst, r, AF.Sin, [Tt], [Tout])

    def trig2(self, posf, Tpos, n, Stab, Ctab, Ttab):
        tb = self.trig_scr
        Tt = self.Ttrig
        r3 = lambda a: a[:, 0:2 * n].rearrange("p (y s) -> p y s", y=2)
        ang, ki, kf, r = tb[0][:, 0:2 * n], tb[1][:, 0:2 * n].bitcast(I32), tb[2][:, 0:2 * n], tb[3][:, 0:2 * n]
        inv2 = self.cols[:, C_INVK:C_INVK + 2]
        self.tt(r3(tb[0]), posf[:, 0:n].unsqueeze(1).to_broadcast([128, 2, n]), inv2.unsqueeze(2).to_broadcast([128, 2, n]),
                ALU.mult, [Tpos, self.Tconst], [Tt])
        for which, dst in ((0, Stab), (1, Ctab)):
            if which == 1:
                self.ts(ang, ang, PI / 2, None, ALU.add, None, [Tt], [Tt])
            self.ts(ki, ang, 1.0 / (2 * PI), None, ALU.mult, None, [Tt], [Tt])
            self.copy(kf, ki, [Tt], [Tt])
            self.stt(r, kf, -2 * PI, ang, ALU.mult, ALU.add, [Tt], [Tt])
            self.ts(r, r, -PI, PI, ALU.max, ALU.min, [Tt], [Tt])
            self.act(dst.rearrange("p y s -> p (y s)"), r, AF.Sin, [Tt], [Ttab])

    def rope_combine(self, out_bf, x_ap, Tx, xb_bf, Txb, ropeP, Cc, Sn, Ttab, Tout, n):
        pa, Tpa = self.pbank()
        self.mm(pa[:, 0:n], ropeP, xb_bf, True, True, [Txb, self.Tconst], [Tpa])
        t1, T1 = self.tmpf[self.tidx % len(self.tmpf)]
        self.tidx += 1
        t2, T2 = self.tmpf[self.tidx % len(self.tmpf)]
        self.tidx += 1
        self.tt(t1[:, 0:n], x_ap, Cc, ALU.mult, [Tx, Ttab], [T1])
        self.tt(t2[:, 0:n], pa[:, 0:n], Sn, ALU.mult, [Tpa, Ttab], [T2])
        self.tt(out_bf, t1[:, 0:n], t2[:, 0:n], ALU.add, [T1, T2], [Tout], eng='gpsimd')

    def rms_T(self, raws, Traws, ones, invn, n):
        pa, Tpa = self.pbank()
        for i, (raw, Tr) in enumerate(zip(raws, Traws)):
            sq, Tsq = self.tmpb[self.bidx % len(self.tmpb)]
            self.bidx += 1
            self.act(sq[:, 0:n], raw, AF.Square, [Tr], [Tsq])
            self.mm(pa[:, 0:n], ones, sq[:, 0:n], i == 0, i == len(raws) - 1, [Tsq, self.Tconst], [Tpa])
        sd, Tsd = self.tmpf[self.tidx % len(self.tmpf)]
        self.tidx += 1
        self.act(sd[:, 0:n], pa[:, 0:n], AF.Ln, [Tpa, self.Tconst], [Tsd], bias=self.cols[:, C_EPS:C_EPS + 1], scale=invn)
        self.act(sd[:, 0:n], sd[:, 0:n], AF.Exp, [Tsd], [Tsd], scale=-0.5)
        return sd[:, 0:n], Tsd

    def g_rope_combine(self, out_bf, x_ap, Tx, xb_bf, Txb, ropeP, Cc, Sn, Ttab, Tout, n):
        pa, Tpa = self.pbank()
        self.mm(pa[:, 0:n], ropeP, xb_bf, True, True, [Txb, self.Tconst], [Tpa])
        yield
        t1, T1 = self.tmpf[self.tidx % len(self.tmpf)]
        self.tidx += 1
        t2, T2 = self.tmpf[self.tidx % len(self.tmpf)]
        self.tidx += 1
        self.tt(t1[:, 0:n], x_ap, Cc, ALU.mult, [Tx, Ttab], [T1])
        self.tt(t2[:, 0:n], pa[:, 0:n], Sn, ALU.mult, [Tpa, Ttab], [T2])
        self.tt(out_bf, t1[:, 0:n], t2[:, 0:n], ALU.add, [T1, T2], [Tout], eng='gpsimd')

    def g_rms_T(self, raws, Traws, ones, invn, n):
        pa, Tpa = self.pbank()
        for i, (raw, Tr) in enumerate(zip(raws, Traws)):
            sq, Tsq = self.tmpb[self.bidx % len(self.tmpb)]
            self.bidx += 1
            self.act(sq[:, 0:n], raw, AF.Square, [Tr], [Tsq])
            self.mm(pa[:, 0:n], ones, sq[:, 0:n], i == 0, i == len(raws) - 1, [Tsq, self.Tconst], [Tpa])
        yield
        sd, Tsd = self.tmpf[self.tidx % len(self.tmpf)]
        self.tidx += 1
        self.act(sd[:, 0:n], pa[:, 0:n], AF.Ln, [Tpa, self.Tconst], [Tsd], bias=self.cols[:, C_EPS:C_EPS + 1], scale=invn)
        self.act(sd[:, 0:n], sd[:, 0:n], AF.Exp, [Tsd], [Tsd], scale=-0.5)
        return sd[:, 0:n], Tsd

    def mkctx(self, nf, nb_, raw_bank, aux_banks):
        return dict(tmpf=[(self.f32(512), T("cf")) for _ in range(nf)], tmpb=[(self.bf(512), T("cb")) for _ in range(nb_)],
                    ppool=list(aux_banks), raw=raw_bank, pidx=0, tidx=0, bidx=0)

    def _load(self, ctx):
        self.tmpf, self.tmpb, self.ppool = ctx['tmpf'], ctx['tmpb'], ctx['ppool']
        self.pidx, self.tidx, self.bidx = ctx['pidx'], ctx['tidx'], ctx['bidx']

    def _save(self, ctx):
        ctx['pidx'], ctx['tidx'], ctx['bidx'] = self.pidx, self.tidx, self.bidx

    def interleave(self, items):
        active = list(items)
        while active:
            for it in list(active):
                gen, ctx = it
                self._load(ctx)
                try:
                    next(gen)
                except StopIteration:
                    active.remove(it)
                self._save(ctx)

    def BK(self, b):
        return self.psum[:, b, :], self.pT[b]

    def build(self):
        nc = self.nc
        dbg = self.dbg
        di = lambda name, shape, dt=F32: nc.dram_tensor(name, list(shape), dt, kind="ExternalInput").ap()
        xall = di("xall", [SEQ, D])
        xown = di("xown", [NOWN, D])
        posall = di("posall", [128, SEQ], I32)
        posown = di("posown", [128, NOWN], I32)
        w_in = di("w_in", [D, D_IN])
        w_a = di("w_a", [1024, D])
        w_b = di("w_b", [1024, D])
        w_m = di("w_m", [1024, D])
        w_out = di("w_out", [D, D])
        w_mkv = di("w_mkv", [D, 2048])
        memx = di("memx", [256, D])
        gbc_d = di("gbc", [128, D])
        gmem_d = di("gmembc", [128, D])
        cols_d = di("cols", [128, NCOLS])
        lng_d = di("lng", [128, 1024])
        lnb_d = di("lnb", [128, 1024])
        wsT_d = di("wsT", [128, 8 * 128])
        sbt_d = di("sbt", [128, 8 * 128])
        cmneg_d = di("cmneg", [128, 512])
        cmat_d = di("cmat", [128, 3 * 128])
        pow2_d = di("pow2", [128, NIT + 2])
        y = nc.dram_tensor("y", [NOWN, D], F32, kind="ExternalOutput").ap()

        with ExitStack() as st:
            S = self.S = Sched(nc, st)
            self.AW = 53200
            self.arena = st.enter_context(nc.sbuf_tensor("arena", [128, self.AW], F32))
            self.psum = st.enter_context(nc.psum_tensor("psum", [128, 8, 512], F32))
            self.pT = [T(f"ps{b}", excl=True) for b in range(8)]
            self.ppool = list(range(8))
            self.pidx = 0
            self.top = 0
            self.final_toks = []
            self.widx = self.sidx = self.tidx = self.bidx = 0

            try:
                self.Tconst = Tconst = T("const")
                self.ident = self.bf(128)
                ones = self.bf(128)
                ropePk = self.bf(128)
                ropePi = self.bf(128)
                blk64 = self.bf(128)
                self.cols = cols = self.f32(NCOLS)
                pow2 = self.f32(NIT + 2)
                identf = self.f32(128)
                cmat_f = self.f32(3 * 128)
                S.op('gpsimd', lambda e: e.memset(identf, 0.0), [], [Tconst])
                S.op('gpsimd', lambda e: e.affine_select(out=identf, in_=identf, pattern=[[-1, 128]], compare_op=ALU.not_equal,
                                                         fill=1.0, base=0, channel_multiplier=1), [Tconst], [Tconst])
                self.copy(self.ident, identf, [Tconst], [Tconst])
                S.op('vector', lambda e: e.memset(ones, 1.0), [], [Tconst])
                self.dma('sync', cols, cols_d, 'c0', writes=[Tconst])
                self.dma('sync', pow2, pow2_d, 'c0', writes=[Tconst])
                self.dma('sync', cmat_f, cmat_d, 'c0', writes=[Tconst])
                self.copy(ropePk, cmat_f[:, 0:128], [Tconst], [Tconst])
                self.copy(ropePi, cmat_f[:, 128:256], [Tconst], [Tconst])
                self.copy(blk64, cmat_f[:, 256:384], [Tconst], [Tconst])
                base_top = self.top
                self.chk('C')

                kT = self.bf(2 * SEQ, (2, SEQ)); TkT = T("kT")
                Vt = self.bf(32 * 256, (32, 256)); TV = T("V")
                ikT = self.bf(SEQ); Tik = T("ikT")
                kvi_top = self.top

                gbc = self.f32(D); Tg = T("gbc")
                self.dma('scalar', gbc, gbc_d, 'c1', writes=[Tg])
                self.xbufs = [(self.f32(D), T(f"x{i}")) for i in range(3)]
                self.xsbufs = [(self.bf(D), T("xs0")), (self.bf(D), T("xs1"))]
                self.small = [(self.f32(8), T(f"sm{i}")) for i in range(4)]
                hbufs = [(self.bf(16 * 512, (16, 512)), T(f"hTg{i}")) for i in range(2)]
                wkv = self.bf(16 * 640, (16, 640)); Twkv = T("wkv")
                posf = self.f32(512); Tposf = T("posf")
                posi = self.f32(512).bitcast(I32); Tposi = T("posi")
                self.trig_scr = [self.f32(1024) for _ in range(4)]; self.Ttrig = T("trig")
                tabs = [(self.f32(1024, (2, 512)), self.f32(1024, (2, 512)), T(f"tab{i}")) for i in range(2)]
                ctxs = [self.mkctx(3, 2, 0, [1]), self.mkctx(5, 3, 2, [3])]
                ctxV = self.mkctx(0, 0, None, [6, 7])
                ctxH = self.mkctx(0, 0, None, [4, 5])

                wv3 = w_in.rearrange("(c p) n -> p c n", p=128)
                self.dma('gpsimd', wkv[:, :, 0:512], wv3[:, :, O_BK:O_BK + 512], 'wk', writes=[Twkv])
                self.dma('gpsimd', wkv[:, :, 512:576], wv3[:, :, O_IK:O_IK + 64], 'wk', writes=[Twkv])
                self.dma('gpsimd', wkv[:, :, 576:640], wv3[:, :, O_IK:O_IK + 64], 'wk', writes=[Twkv])

                NG = SEQ // 512

                def a_trig(g):
                    St, Ct, Ttb = tabs[g % 2]
                    self.dma('scalar', posi, posall[:, g * 512:(g + 1) * 512], 'pos', writes=[Tposi])
                    self.copy(posf, posi, [Tposi], [Tposf])
                    self.trig2(posf, Tposf, 512, St, Ct, Ttb)

                def a_hT(g, tt_):
                    hTg, ThTg = hbufs[g % 2]
                    ti = g * 4 + tt_
                    self.make_hT(xall[ti * 128:(ti + 1) * 128, :], gbc, Tg, hTg[:, :, tt_ * 128:(tt_ + 1) * 128], ThTg, ti)

                def g_khead(g, kvh, ctx):
                    hTg, ThTg = hbufs[g % 2]
                    St, Ct, Ttb = tabs[g % 2]
                    gs = slice(g * 512, (g + 1) * 512)
                    raw, Traw = self.BK(ctx['raw'])
                    self.projT(wkv, Twkv, kvh * 128, 128, hTg, ThTg, raw, Traw)
                    yield
                    rstd, Trs = yield from self.g_rms_T([raw], [Traw], ones, 1.0 / 128, 512)
                    kn, Tkn = self.tmpb[self.bidx % len(self.tmpb)]
                    self.bidx += 1
                    self.stt(kn, raw, cols[:, C_GK:C_GK + 1], rstd, ALU.mult, ALU.mult, [Traw, Trs, Tconst], [Tkn])
                    yield from self.g_rope_combine(kT[:, kvh, gs], kn, Tkn, kn, Tkn, ropePk, Ct[:, 0, :], St[:, 0, :], Ttb, TkT, 512)

                def g_ik(g, ctx):
                    hTg, ThTg = hbufs[g % 2]
                    St, Ct, Ttb = tabs[g % 2]
                    gs = slice(g * 512, (g + 1) * 512)
                    raw, Traw = self.BK(ctx['raw'])
                    self.projT(wkv, Twkv, 512, 128, hTg, ThTg, raw, Traw)
                    yield
                    ikb, Tikb = self.tmpb[0]
                    ikf, Tikf = self.tmpf[0]
                    self.copy(ikb, raw, [Traw], [Tikb], eng='scalar')
                    self.copy(ikf, raw, [Traw], [Tikf], eng='scalar')
                    pm, Tpm = self.pbank()
                    self.mm(pm, blk64, ikb, True, True, [Tikb, Tconst], [Tpm])
                    yield
                    cen, Tcen = self.tmpf[1]
                    self.tt(cen, ikf, pm, ALU.subtract, [Tikf, Tpm], [Tcen])
                    sq, Tsq = self.tmpb[1]
                    self.act(sq, cen, AF.Square, [Tcen], [Tsq])
                    pv, Tpv = self.pbank()
                    self.mm(pv, blk64, sq, True, True, [Tsq, Tconst], [Tpv])
                    yield
                    sd, Tsd = self.tmpf[2]
                    self.act(sd, pv, AF.Ln, [Tpv, Tconst], [Tsd], bias=cols[:, C_EPS:C_EPS + 1], scale=1.0)
                    self.act(sd, sd, AF.Exp, [Tsd], [Tsd], scale=-0.5)
                    self.tt(cen, cen, sd, ALU.mult, [Tcen, Tsd], [Tcen])
                    ikn, Tikn = self.tmpb[2]
                    self.ts(ikn, cen, cols[:, C_IKG:C_IKG + 1], cols[:, C_IKB:C_IKB + 1], ALU.mult, ALU.add, [Tcen, Tconst], [Tikn])
                    self.tidx = 3
                    yield from self.g_rope_combine(ikT[:, gs], ikn, Tikn, ikn, Tikn, ropePi, Ct[:, 1, :], St[:, 1, :], Ttb, Tik, 512)

                def g_v(g):
                    hTg, ThTg = hbufs[g % 2]
                    prev = None
                    for tt_ in range(4):
                        ti = g * 4 + tt_
                        pvb, Tpvb = self.BK(6 + (tt_ % 2))
                        for k in range(16):
                            self.mm(pvb[:, 0:256], hTg[:, k, tt_ * 128:(tt_ + 1) * 128], wkv[:, k, 256:512], k == 0, k == 15, [ThTg, Twkv], [Tpvb])
                        if prev is not None:
                            self.copy(Vt[:, prev[0], :], prev[1][:, 0:256], [prev[2]], [TV], eng='scalar')
                        prev = (ti, pvb, Tpvb)
                        yield
                    self.copy(Vt[:, prev[0], :], prev[1][:, 0:256], [prev[2]], [TV], eng='scalar')

                def g_kboth(g, ctx):
                    yield from g_khead(g, 0, ctx)
                    yield from g_khead(g, 1, ctx)

                all_tiles = []
                for g in range(NG):
                    hTg, ThTg = hbufs[g % 2]
                    for tt_ in range(4):
                        ti = g * 4 + tt_
                        all_tiles.append((xall[ti * 128:(ti + 1) * 128, :], hTg[:, :, tt_ * 128:(tt_ + 1) * 128], ThTg))
                prod = self.hT_stream(all_tiles, gbc, Tg, [4, 5])

                def g_prod(nrounds):
                    for _ in range(nrounds):
                        try:
                            next(prod)
                        except StopIteration:
                            return
                        yield

                a_trig(0)
                for _ in g_prod(6):
                    pass
                for g in range(NG):
                    if g + 1 < NG:
                        a_trig(g + 1)
                    items = [(g_v(g), ctxV)]
                    if g + 1 < NG:
                        items.append((g_prod(4), ctxH))
                    items += [(g_kboth(g, ctxs[0]), ctxs[0]), (g_ik(g, ctxs[1]), ctxs[1])]
                    self.interleave(items)
                self.ppool = list(range(8))

                self.tap("kT", kT, TkT, [128, 2, SEQ])
                self.tap("V", Vt, TV, [128, 32, 256])
                self.tap("ikT", ikT, Tik, [128, SEQ])
                S.barrier()
                if self.stop == 'A':
                    self._finish(y)
                    return nc

                self.top = kvi_top
                hT = self.bf(16 * NOWN, (16, NOWN)); ThT = T("hT")
                BT = self.bf(8 * NOWN, (8, NOWN)); TBT = T("BT")
                wbase = self.top
                self.wbufs = [(self.bf(16 * 512, (16, 512)), T("w0")), (self.bf(16 * 512, (16, 512)), T("w1"))]
                b1_top = self.top
                gbc = self.f32(D); Tg = T("gbc2")
                self.dma('scalar', gbc, gbc_d, 'c1', writes=[Tg])
                self.xbufs = [(self.f32(D), T(f"x{i}")) for i in range(3)]
                self.xsbufs = [(self.bf(D), T("xs0")), (self.bf(D), T("xs1"))]
                self.small = [(self.f32(8), T(f"sm{i}")) for i in range(4)]
                own_tiles = [(xown[ti * 128:(ti + 1) * 128, :], hT[:, :, ti * 128:(ti + 1) * 128], ThT) for ti in range(NB)]
                for _ in self.hT_stream(own_tiles, gbc, Tg, [4, 5]):
                    pass
                self.tap("hT", hT, ThT, [128, 16, NOWN])
                S.barrier()
                if self.stop == 'B0':
                    self._finish(y)
                    return nc

                self.top = b1_top
                iqT = self.bf(8 * NOWN, (8, NOWN)); TiqT = T("iqT")
                qT = self.bf(8 * NOWN, (8, NOWN)); TqT = T("qT")
                Sbuf = self.f32(SEQ); TS = T("S")
                iwf = self.f32(NB * 16, (NB, 16)); Tiw = T("iw")
                self.small = [(self.f32(8), T(f"sm{i}")) for i in range(4)]
                bis = self.f32(16); Tbis = T("bis")
                Wtab = self.f32(NIT + 2); TW = T("Wtab")
                cmnegb = self.bf(512); Tcm = T("cmneg")
                self.dma('gpsimd', cmnegb, cmneg_d, 'c3', writes=[Tcm])
                gqs = self.f32(8); Tgqs = T("gqs")
                self.ts(gqs[:, 0:1], cols[:, C_GQ:C_GQ + 1], float(128 ** -0.5), None, ALU.mult, None, [Tconst], [Tgqs])
                b1e_top = self.top
                diags = [(self.bf(16 * 128, (16, 128)), T(f"diag{i}")) for i in range(2)]
                negmT = self.bf(32 * 128, (32, 128)); TnT = T("negmT")
                Sbuf2 = self.f32(SEQ); TS2 = T("S2")
                self.top = b1e_top
                posf = self.f32(512); Tposf = T("posf")
                posi = self.f32(512).bitcast(I32); Tposi = T("posi")
                self.trig_scr = [self.f32(1024) for _ in range(4)]; self.Ttrig = T("trig")
                Ttab = T("tabo")
                So = [Sbuf[:, 0:1024].rearrange("p (y s) -> p y s", y=2), Sbuf[:, 1024:2048].rearrange("p (y s) -> p y s", y=2)]
                Co = [Sbuf[:, 2048:3072].rearrange("p (y s) -> p y s", y=2), Sbuf[:, 3072:4096].rearrange("p (y s) -> p y s", y=2)]
                for half in range(2):
                    self.dma('scalar', posi, posown[:, half * 512:(half + 1) * 512], 'pos', writes=[Tposi])
                    self.copy(posf, posi, [Tposi], [Tposf])
                    self.trig2(posf, Tposf, 512, So[half], Co[half], Ttab)

                S.barrier()
                self.top = b1e_top
                qctx = [self.mkctx(3, 2, 0, [1]), self.mkctx(3, 2, 2, [3]), self.mkctx(3, 2, 4, [5])]

                def g_iq(w, Tw, pp, p, half, ctx):
                    hs = slice(half * 512, (half + 1) * 512)
                    raw, Traw = self.BK(ctx['raw'])
                    self.projT(w, Tw, pp * 128, 128, hT[:, :, hs], ThT, raw, Traw)
                    yield
                    iqb, Tiqb = self.tmpb[self.bidx % len(self.tmpb)]
                    self.bidx += 1
                    self.copy(iqb, raw, [Traw], [Tiqb], eng='scalar')
                    yield from self.g_rope_combine(iqT[:, p, hs], raw, Traw, iqb, Tiqb, ropePi, Co[half][:, 1, :], So[half][:, 1, :], Ttab, TiqT, 512)

                def g_q(w, Tw, hh, h, half, ctx):
                    hs = slice(half * 512, (half + 1) * 512)
                    raw, Traw = self.BK(ctx['raw'])
                    self.projT(w, Tw, hh * 128, 128, hT[:, :, hs], ThT, raw, Traw)
                    yield
                    rstd, Trs = yield from self.g_rms_T([raw], [Traw], ones, 1.0 / 128, 512)
                    qn, Tqn = self.tmpb[self.bidx % len(self.tmpb)]
                    self.bidx += 1
                    self.stt(qn, raw, gqs[:, 0:1], rstd, ALU.mult, ALU.mult, [Traw, Trs, Tgqs], [Tqn])
                    yield from self.g_rope_combine(qT[:, h, hs], qn, Tqn, qn, Tqn, ropePk, Co[half][:, 0, :], So[half][:, 0, :], Ttab, TqT, 512)

                def run_batches(mk):
                    for b0 in range(0, len(mk), 3):
                        batch = mk[b0:b0 + 3]
                        self.interleave([(f(qctx[j]), qctx[j]) for j, f in enumerate(batch)])

                for ch in range(2):
                    w, Tw = self.wload(w_in, O_IQ + ch * 512, 512)
                    run_batches([(lambda ctx, pp=pp, half=half, w=w, Tw=Tw, ch=ch: g_iq(w, Tw, pp, ch * 4 + pp, half, ctx))
                                 for pp in range(4) for half in range(2)])
                for ch in range(2):
                    w, Tw = self.wload(w_in, O_BQ + ch * 512, 512)
                    run_batches([(lambda ctx, hh=hh, half=half, w=w, Tw=Tw, ch=ch: g_q(w, Tw, hh, ch * 4 + hh, half, ctx))
                                 for hh in range(4) for half in range(2)])
                self.ppool = [6, 7]
                self.pidx = 0
                w, Tw = self.wload(w_in, O_IW, 16)
                for i in range(NB):
                    pb, Tp = self.pbank()
                    for k in range(16):
                        self.mm(pb[:, 0:16], hT[:, k, i * 128:(i + 1) * 128], w[:, k, 0:16], k == 0, k == 15, [ThT, Tw], [Tp])
                    self.ts(iwf[:, i, :], pb[:, 0:16], float(0.25 * 0.125), None, ALU.mult, None, [Tp], [Tiw])
                self.tap("iqT", iqT, TiqT, [128, 8, NOWN])
                self.tap("qT", qT, TqT, [128, 8, NOWN])
                self.tap("iw", iwf, Tiw, [128, NB, 16])
                S.barrier()
                if self.stop == 'B1a':
                    self._finish(y)
                    return nc

                save_top = self.top
                self.top = wbase
                relu = [(self.bf(512), T(f"relu{i}")) for i in range(4)]
                Pb = [(self.bf(512), T(f"P{i}")) for i in range(4)]
                negc = self.bf(1024); Tnc = T("negc")
                self.tmpf = [(self.f32(512), T(f"tf{i}")) for i in range(4)]
                junkb = self.bf(SEQ); Tjb = T("junkb")
                assert self.top <= b1_top
                self.top = save_top
                Sbufs = [(Sbuf, TS), (Sbuf2, TS2)]
                BK = lambda b: (self.psum[:, b, :], self.pT[b])

                def emit_diag(i):
                    dg, Tdg = diags[i % 2]
                    self.tt(dg, self.ident.unsqueeze(1).to_broadcast([128, 16, 128]),
                            iwf[:, i, :].unsqueeze(2).to_broadcast([128, 16, 128]), ALU.mult, [Tconst, Tiw], [Tdg])

                def emit_idx(i):
                    Sb, TSb = Sbufs[i % 2]
                    dg, Tdg = diags[i % 2]
                    qs = slice(i * 128, (i + 1) * 128)
                    steps = [(c, h) for c in range(i + 1) for h in range(16)]
                    xb_ = (0, 1, 5)

                    def A(n):
                        c, h = steps[n]
                        p, sub = h // 2, h % 2
                        ps_ = slice(sub * 64, (sub + 1) * 64)
                        xh, Txh = BK(xb_[n % 3])
                        self.mm(xh, iqT[ps_, p, qs], ikT[ps_, c * 512:(c + 1) * 512], True, True, [TiqT, Tik], [Txh])

                    A(0)
                    if len(steps) > 1:
                        A(1)
                    for n, (c, h) in enumerate(steps):
                        cs = slice(c * 512, (c + 1) * 512)
                        acc, Tacc = BK(2 + (c % 2))
                        last = (c == i)
                        xh, Txh = BK(xb_[n % 3])
                        rl, Trl = relu[n % 4]
                        self.act(rl, xh, AF.Relu, [Txh], [Trl])
                        self.mm(acc, dg[:, h, :], rl, h == 0, (h == 15 and not last), [Tdg, Trl], [Tacc])
                        if h == 15:
                            if last:
                                self.mm(acc, self.ident, cmnegb, False, True, [Tconst, Tcm], [Tacc])
                            self.copy(Sb[:, cs], acc, [Tacc], [TSb], eng='scalar')
                        if n + 2 < len(steps):
                            A(n + 2)

                def emit_bis(i):
                    Sb, TSb = Sbufs[i % 2]
                    nk = 512 * (i + 1)
                    Sv = Sb[:, 0:nk]
                    hi0, lo0, mid, cnt, tmp, thr = (bis[:, j:j + 1] for j in range(6))
                    S.op('vector', lambda e: e.tensor_reduce(out=hi0, in_=Sv, axis=mybir.AxisListType.X, op=ALU.max), [TSb], [Tbis])
                    t_f, Tt_f = self.tmpf[0]
                    ls = slice(i * 512, (i + 1) * 512)
                    self.ts(t_f, Sb[:, ls], -1.0e29, 2.0e30, ALU.is_lt, ALU.mult, [TSb], [Tt_f])
                    self.tt(t_f, t_f, Sb[:, ls], ALU.add, [Tt_f, TSb], [Tt_f])
                    S.op('vector', lambda e: e.tensor_reduce(out=lo0, in_=t_f, axis=mybir.AxisListType.X, op=ALU.min), [Tt_f], [Tbis])
                    if i > 0:
                        Su = Sb[:, 0:i * 512]
                        S.op('vector', lambda e: e.tensor_reduce(out=tmp, in_=Su, axis=mybir.AxisListType.X, op=ALU.min), [TSb], [Tbis])
                        self.tt(lo0, lo0, tmp, ALU.min, [Tbis], [Tbis])
                    self.tt(tmp, lo0, lo0, ALU.mult, [Tbis], [Tbis])
                    self.stt(tmp, hi0, hi0, tmp, ALU.mult, ALU.add, [Tbis], [Tbis])
                    self.ts(tmp, tmp, 1.0, -1.0e-4, ALU.add, ALU.mult, [Tbis], [Tbis])
                    self.tt(lo0, lo0, tmp, ALU.add, [Tbis], [Tbis])
                    self.tt(tmp, hi0, lo0, ALU.subtract, [Tbis], [Tbis])
                    self.ts(tmp, tmp, 1.0e-6, None, ALU.add, None, [Tbis], [Tbis])
                    self.ts(Wtab, pow2, tmp, None, ALU.mult, None, [Tconst, Tbis], [TW])
                    self.tt(mid, lo0, Wtab[:, 0:1], ALU.add, [Tbis, TW], [Tbis])
                    for k in range(NIT):
                        self.ts(junkb[:, 0:nk], Sv, mid, 0.0, ALU.is_ge, ALU.add, [TSb, Tbis], [Tjb, Tbis], accum=cnt)
                        self.ts(tmp, cnt, TOPK - 0.5, Wtab[:, k:k + 1], ALU.is_ge, ALU.mult, [Tbis, TW], [Tbis])
                        self.stt(mid, tmp, Wtab[:, k + 1:k + 2], mid, ALU.subtract, ALU.add, [Tbis, TW], [Tbis])
                    self.tt(thr, mid, Wtab[:, NIT:NIT + 1], ALU.subtract, [Tbis, TW], [Tbis])
                    if i == 3:
                        self.tap("S3", Sb, TSb, [128, SEQ])
                        self.tap("thr3", bis, Tbis, [128, 16])
                    nt = 4 * (i + 1)
                    for c8 in range((nt + 7) // 8):
                        n8 = min(8, nt - c8 * 8)
                        self.ts(negc[:, 0:n8 * 128], Sb[:, c8 * 1024:c8 * 1024 + n8 * 128], thr, -30000.0, ALU.is_lt, ALU.mult, [TSb, Tbis], [Tnc])
                        pbk, Tp = BK(4 + (c8 % 2))
                        pb = pbk.bitcast(BF16)
                        for t8 in range(n8):
                            self.transpose(pb[:, t8 * 128:(t8 + 1) * 128], negc[:, t8 * 128:(t8 + 1) * 128], [Tnc], [Tp], inc=(t8 == n8 - 1))
                        self.copy(negmT[:, c8 * 8:c8 * 8 + n8, :], pb[:, 0:n8 * 128].rearrange("p (c t) -> p c t", c=n8), [Tp], [TnT], eng='vector')

                def emit_att(i):
                    nt = 4 * (i + 1)
                    qs = slice(i * 128, (i + 1) * 128)
                    steps = [(kvh, kt) for kvh in range(2) for kt in range(nt)]

                    def L(n):
                        kvh, kt = steps[n]
                        lg, Tlg = BK(4 + (n % 2))
                        lg4 = lg.rearrange("p (h t) -> p h t", h=4)
                        self.mm(lg4, kT[:, kvh, kt * 128:(kt + 1) * 128], qT[:, 4 * kvh:4 * kvh + 4, qs], True, False, [TkT, TqT], [Tlg])
                        self.mm(lg4, self.ident, negmT[:, kt, :].unsqueeze(1).to_broadcast([128, 4, 128]), False, True, [Tconst, TnT], [Tlg])

                    L(0)
                    for n, (kvh, kt) in enumerate(steps):
                        oacc, Toa = BK(6 if kvh == 0 else 2)
                        sacc, Tsa = BK(7 if kvh == 0 else 3)
                        lg, Tlg = BK(4 + (n % 2))
                        pbuf, TP = Pb[n % 4]
                        self.act(pbuf, lg, AF.Exp, [Tlg], [TP])
                        if n + 1 < len(steps):
                            L(n + 1)
                        self.mm(oacc, Vt[:, kt, kvh * 128:(kvh + 1) * 128], pbuf, kt == 0, kt == nt - 1, [TV, TP], [Toa])
                        self.mm(sacc, ones, pbuf, kt == 0, kt == nt - 1, [Tconst, TP], [Tsa])
                        if kt == nt - 1:
                            rs, Trs = self.tmpf[1 + kvh]
                            self.recip(rs, sacc, [Tsa], [Trs])
                            self.tt(BT[:, 4 * kvh:4 * kvh + 4, qs], oacc.rearrange("p (h t) -> p h t", h=4),
                                    rs.rearrange("p (h t) -> p h t", h=4), ALU.mult, [Toa, Trs], [TBT])

                emit_diag(0)
                emit_idx(0)
                for i in range(NB):
                    if i + 1 < NB:
                        emit_diag(i + 1)
                        emit_idx(i + 1)
                    emit_bis(i)
                    emit_att(i)
                self.ppool = list(range(8))
                self.tap("BT0", BT, TBT, [128, 8, NOWN])
                S.barrier()
                if self.stop == 'B1e':
                    self._finish(y)
                    return nc

                self.tmpb = [(self.bf_at(b1e_top + 256 * j, 512), T(f"tb{j}")) for j in range(4)]
                self.gate_mul(w_in, O_BZ, hT, ThT, BT, TBT)
                self.tap("BT", BT, TBT, [128, 8, NOWN])
                S.barrier()
                if self.stop == 'B1':
                    self._finish(y)
                    return nc

                self.top = b1_top
                MT = self.bf(8 * NOWN, (8, NOWN)); TMT = T("MT")
                AT = self.bf(8 * NOWN, (8, NOWN)); TAT = T("AT")
                b2_top = self.top
                mqT = self.bf(8 * NOWN, (8, NOWN)); TmqT = T("mqT")
                memT = self.bf(16 * 256, (16, 256)); TmemT = T("memT")
                kmT = self.bf(8 * 256, (8, 256)); TkmT = T("kmT")
                vm = self.bf(2 * 1024, (2, 1024)); Tvm = T("vm")
                sv_top = self.top
                self.top = base_top
                gbc = self.f32(D); Tg = T("gmem")
                self.dma('scalar', gbc, gmem_d, 'c1', writes=[Tg])
                self.xbufs = [(self.f32(D), T(f"x{i}")) for i in range(3)]
                self.xsbufs = [(self.bf(D), T("xs0")), (self.bf(D), T("xs1"))]
                assert self.top <= kvi_top
                self.top = sv_top
                self.small = [(self.f32(8), T(f"sm{i}")) for i in range(4)]
                self.tmpf = [(self.f32(512), T(f"tf{i}")) for i in range(6)]
                self.tmpb = [(self.bf(512), T(f"tb{i}")) for i in range(4)]
                Pb = [(self.bf(512), T(f"P{i}")) for i in range(2)]
                gms = self.f32(8); Tgms = T("gms")
                self.ts(gms[:, 0:2], cols[:, C_GMQ0:C_GMQ0 + 2], float(256 ** -0.5), None, ALU.mult, None, [Tconst], [Tgms])
                mem_tiles = [(memx[ti * 128:(ti + 1) * 128, :], memT[:, :, ti * 128:(ti + 1) * 128], TmemT) for ti in range(2)]
                for _ in self.hT_stream(mem_tiles, gbc, Tg, [4, 5]):
                    pass
                for ch in range(2):
                    w, Tw = self.wload(w_mkv, ch * 512, 512)
                    for hh in range(2):
                        h = ch * 2 + hh
                        raws = []
                        for dc in range(2):
                            raw, Traw = self.pbank()
                            self.projT(w, Tw, (hh * 2 + dc) * 128, 128, memT, TmemT, raw[:, 0:256], Traw)
                            raws.append((raw[:, 0:256], Traw))
                        rstd, Trs = self.rms_T([r for r, _ in raws], [t for _, t in raws], ones, 1.0 / 256, 256)
                        for dc in range(2):
                            self.stt(kmT[:, 2 * h + dc, :], raws[dc][0], cols[:, C_GMK0 + dc:C_GMK0 + dc + 1], rstd, ALU.mult, ALU.mult,
                                     [raws[dc][1], Trs, Tconst], [TkmT])
                for ch in range(2):
                    w, Tw = self.wload(w_mkv, 1024 + ch * 512, 512)
                    for mt in range(2):
                        pb, Tp = self.pbank()
                        for k in range(16):
                            self.mm(pb, memT[:, k, mt * 128:(mt + 1) * 128], w[:, k, :], k == 0, k == 15, [TmemT, Tw], [Tp])
                        self.copy(vm[:, mt, ch * 512:(ch + 1) * 512], pb, [Tp], [Tvm], eng='scalar')
                for ch in range(2):
                    w, Tw = self.wload(w_in, O_MQ + ch * 512, 512)
                    for hh in range(2):
                        h = ch * 2 + hh
                        for half in range(2):
                            hs = slice(half * 512, (half + 1) * 512)
                            raws = []
                            for dc in range(2):
                                raw, Traw = self.pbank()
                                self.projT(w, Tw, (hh * 2 + dc) * 128, 128, hT[:, :, hs], ThT, raw, Traw)
                                raws.append((raw, Traw))
                            rstd, Trs = self.rms_T([r for r, _ in raws], [t for _, t in raws], ones, 1.0 / 256, 512)
                            for dc in range(2):
                                self.stt(mqT[:, 2 * h + dc, hs], raws[dc][0], gms[:, dc:dc + 1], rstd, ALU.mult, ALU.mult,
                                         [raws[dc][1], Trs, Tgms], [TmqT])
                for h in range(4):
                    for half in range(2):
                        hs = slice(half * 512, (half + 1) * 512)
                        for mt in range(2):
                            lg, Tlg = self.pbank()
                            for dc in range(2):
                                self.mm(lg, kmT[:, 2 * h + dc, mt * 128:(mt + 1) * 128], mqT[:, 2 * h + dc, hs], dc == 0, dc == 1, [TkmT, TmqT], [Tlg])
                            self.act(Pb[mt][0], lg, AF.Exp, [Tlg], [Pb[mt][1]])
                        sm, Tsm = self.pbank()
                        for mt in range(2):
                            self.mm(sm, ones, Pb[mt][0], mt == 0, mt == 1, [Tconst, Pb[mt][1]], [Tsm])
                        rs, Trs = self.tmpf[self.tidx % len(self.tmpf)]
                        self.tidx += 1
                        self.recip(rs, sm, [Tsm], [Trs])
                        for dc in range(2):
                            po, Tpo = self.pbank()
                            for mt in range(2):
                                self.mm(po, vm[:, mt, h * 256 + dc * 128:h * 256 + (dc + 1) * 128], Pb[mt][0], mt == 0, mt == 1, [Tvm, Pb[mt][1]], [Tpo])
                            self.tt(MT[:, 2 * h + dc, hs], po, rs, ALU.mult, [Tpo, Trs], [TMT])
                self.tap("MT0", MT, TMT, [128, 8, NOWN])
                self.gate_mul(w_in, O_MZ, hT, ThT, MT, TMT)
                self.tap("MT", MT, TMT, [128, 8, NOWN])
                S.barrier()
                if self.stop == 'B2':
                    self._finish(y)
                    return nc

                self.top = b2_top
                vln = self.bf(NB * 1024, (NB, 1024)); Tvln = T("vln")
                gv = self.f32(1024); Tgv = T("gv")
                wsT = self.bf(8 * 128, (8, 128)); Tws = T("wsT")
                sv_top = self.top
                self.top = base_top
                wsf = self.f32(8 * 128, (8, 128))
                sbt = self.f32(8 * 128, (8, 128)); Tsbt = T("sbt")
                lng = self.f32(1024); lnb = self.f32(1024); Tln = T("ln")
                assert self.top <= kvi_top
                self.top = sv_top
                self.small = [(self.f32(16), T(f"sm{i}")) for i in range(4)]
                self.tmpf = [(self.f32(1024), T(f"tf{i}")) for i in range(3)]
                self.tmpb = [(self.bf(512), T(f"tb{i}")) for i in range(4)]
                self.dma('sync', wsf, wsT_d.rearrange("p (g t) -> p g t", g=8), 'c2', writes=[Tws])
                self.dma('sync', sbt, sbt_d.rearrange("p (g t) -> p g t", g=8), 'c2', writes=[Tsbt])
                self.dma('sync', lng, lng_d, 'c2', writes=[Tln])
                self.dma('sync', lnb, lnb_d, 'c2', writes=[Tln])
                S.op('gpsimd', lambda e: e.affine_select(out=wsf, in_=wsf, pattern=[[0, 8], [1, 128]], compare_op=ALU.is_ge,
                                                         fill=0.0, base=0, channel_multiplier=-1), [Tws], [Tws])
                self.copy(wsT, wsf, [Tws], [Tws])
                for ch in range(2):
                    w, Tw = self.wload(w_in, O_AU + ch * 512, 512)
                    for cc in range(4):
                        c = ch * 4 + cc
                        for half in range(2):
                            hs = slice(half * 512, (half + 1) * 512)
                            raw, Traw = self.pbank()
                            self.projT(w, Tw, cc * 128, 128, hT[:, :, hs], ThT, raw, Traw)
                            self.act(AT[:, c, hs], raw, AF.Gelu, [Traw], [TAT])
                wv0, Twv0 = self.wload(w_in, O_AV, 512)
                wv1, Twv1 = self.wload(w_in, O_AV + 512, 512)
                for i in range(NB):
                    for ch, (w, Tw) in enumerate(((wv0, Twv0), (wv1, Twv1))):
                        pb, Tp = self.pbank()
                        for k in range(16):
                            self.mm(pb, hT[:, k, i * 128:(i + 1) * 128], w[:, k, :], k == 0, k == 15, [ThT, Tw], [Tp])
                        self.act(gv[:, ch * 512:(ch + 1) * 512], pb, AF.Gelu, [Tp], [Tgv])
                    sm, Tsm = self.small[self.sidx % len(self.small)]
                    self.sidx += 1
                    S.op('vector', lambda e, sm=sm: e.bn_stats(out=sm[:, 0:6], in_=gv[:, 0:512]), [Tgv], [Tsm])
                    S.op('vector', lambda e, sm=sm: e.bn_stats(out=sm[:, 6:12], in_=gv[:, 512:1024]), [Tgv], [Tsm])
                    S.op('vector', lambda e, sm=sm: e.bn_aggr(out=sm[:, 12:14], in_=sm[:, 0:12].rearrange("p (a b) -> p a b", a=2)), [Tsm], [Tsm])
                    self.act(sm[:, 14:15], sm[:, 13:14], AF.Sqrt, [Tsm, Tconst], [Tsm], bias=cols[:, C_EPS:C_EPS + 1], scale=1.0)
                    self.recip(sm[:, 15:16], sm[:, 14:15], [Tsm], [Tsm])
                    t1, T1 = self.tmpf[i % 3]
                    self.ts(t1, gv, sm[:, 12:13], sm[:, 15:16], ALU.subtract, ALU.mult, [Tgv, Tsm], [T1])
                    self.tt(t1, t1, lng, ALU.mult, [T1, Tln], [T1])
                    self.tt(vln[:, i, :], t1, lnb, ALU.add, [T1, Tln], [Tvln])
                for i in range(NB):
                    qs = slice(i * 128, (i + 1) * 128)
                    for gh in range(2):
                        pb, Tp = self.pbank()
                        for gg in range(4):
                            g = gh * 4 + gg
                            self.mm(pb[:, gg * 128:(gg + 1) * 128], vln[:, i, g * 128:(g + 1) * 128], wsT[:, g, :], True, True, [Tvln, Tws], [Tp], inc=(gg == 3))
                        t1, T1 = self.tmpf[(i * 2 + gh) % 3]
                        self.tt(t1[:, 0:512], pb, sbt[:, gh * 4:(gh + 1) * 4, :].rearrange("p g t -> p (g t)"), ALU.add, [Tp, Tsbt], [T1])
                        self.tt(AT[:, gh * 4:(gh + 1) * 4, qs], t1[:, 0:512].rearrange("p (g t) -> p g t", g=4), AT[:, gh * 4:(gh + 1) * 4, qs],
                                ALU.mult, [T1, TAT], [TAT])
                self.tap("AT0", AT, TAT, [128, 8, NOWN])
                self.gate_mul(w_in, O_AZ, hT, ThT, AT, TAT)
                self.tap("AT", AT, TAT, [128, 8, NOWN])
                S.barrier()
                if self.stop == 'B3':
                    self._finish(y)
                    return nc

                self.top = b2_top
                mergedT = self.bf_at(base_top, 16 * NOWN, (16, NOWN)); Tmg = T("merged")
                accm = self.f32(4 * NOWN, (4, NOWN)); Tacc = T("accm")
                sgb = [(self.f32(512), T(f"sg{i}")) for i in range(3)]
                wbr = [(self.bf(8 * 512, (8, 512)), T(f"wbr{i}")) for i in range(2)]
                branches = ((O_GA, w_a, AT, TAT), (O_GB, w_b, BT, TBT), (O_GM, w_m, MT, TMT))
                nbr = 0
                for nq in range(4):
                    for bi, (og, wb_d, XT, TXT) in enumerate(branches):
                        wg, Twg = self.wload(w_in, og + nq * 512, 512)
                        wb_, Twb = wbr[nbr % 2]
                        nbr += 1
                        self.dma('gpsimd', wb_, wb_d.rearrange("(c p) n -> p c n", p=128)[:, :, nq * 512:(nq + 1) * 512], f'wb{nbr % 2}', writes=[Twb])
                        for nn in range(4):
                            hss = [slice(0, 512), slice(512, 1024)]
                            pgs = [self.pbank() for _ in range(2)]
                            for k in range(16):
                                for half in range(2):
                                    self.mm(pgs[half][0], wg[:, k, nn * 128:(nn + 1) * 128], hT[:, k, hss[half]], k == 0, k == 15,
                                            [Twg, ThT], [pgs[half][1]])
                            pys = [self.pbank() for _ in range(2)]
                            for c in range(8):
                                for half in range(2):
                                    self.mm(pys[half][0], wb_[:, c, nn * 128:(nn + 1) * 128], XT[:, c, hss[half]], c == 0, c == 7,
                                            [Twb, TXT], [pys[half][1]])
                            for half in range(2):
                                hs = hss[half]
                                pg, Tpg = pgs[half]
                                py, Tpy = pys[half]
                                sg, Tsg = sgb[(nn * 2 + half) % 3]
                                self.act(sg, pg, AF.Sigmoid, [Tpg], [Tsg])
                                if bi == 0:
                                    self.tt(accm[:, nn, hs], py, sg, ALU.mult, [Tpy, Tsg], [Tacc])
                                else:
                                    self.tt(sg, py, sg, ALU.mult, [Tpy, Tsg], [Tsg])
                                    if bi == 1:
                                        self.tt(accm[:, nn, hs], accm[:, nn, hs], sg, ALU.add, [Tacc, Tsg], [Tacc], eng='gpsimd')
                                    else:
                                        self.tt(mergedT[:, nq * 4 + nn, hs], accm[:, nn, hs], sg, ALU.add, [Tacc, Tsg], [Tmg], eng='gpsimd')
                self.tap("merged", mergedT, Tmg, [128, 16, NOWN])
                S.barrier()
                if self.stop == 'B4':
                    self._finish(y)
                    return nc

                self.top = b2_top
                xo = [(self.f32(512), T(f"xo{i}")) for i in range(3)]
                ot = [(self.f32(512), T(f"ot{i}")) for i in range(3)]
                n_o = 0
                for dch in range(4):
                    ds_ = slice(dch * 512, (dch + 1) * 512)
                    w, Tw = self.wload(w_out, dch * 512, 512)
                    for i in range(NB):
                        rows = slice(i * 128, (i + 1) * 128)
                        xo_, Txo = xo[n_o % 3]
                        ot_, Tot = ot[n_o % 3]
                        self.dma('sync', xo_, xown[rows, ds_], f'xo{n_o % 3}', writes=[Txo])
                        pb, Tp = self.pbank()
                        for k in range(16):
                            self.mm(pb, mergedT[:, k, rows], w[:, k, :], k == 0, k == 15, [Tmg, Tw], [Tp])
                        self.tt(ot_, pb, xo_, ALU.add, [Tp, Txo], [Tot])
                        self.final_toks.append(self.dma('scalar', y[rows, ds_], ot_, f'yo{n_o % 3}', reads=[Tot]))
                        n_o += 1
                self._finish()
            except StopBuild:
                self.S.barrier()
                self._finish(y)
        return nc

    def _finish(self, y=None):
        S = self.S
        if y is not None:
            z, Tz = self.f32_at(0, 8), T("z")
            self.final_toks.append(self.dma('sync', y[0:128, 0:8], z, 'yo0', reads=[Tz]))
        last = {}
        for k, v in self.final_toks:
            last[k] = max(last.get(k, 0), v)
        S.streams['sync'].append((list(last.items()), None, None, 0))
        S.emit()

    def gate_mul(self, wd, col0, hT, ThT, XT, TXT):
        for ch in range(2):
            w, Tw = self.wload(wd, col0 + ch * 512, 512)
            for cc in range(4):
                c = ch * 4 + cc
                for half in range(2):
                    hs = slice(half * 512, (half + 1) * 512)
                    raw, Traw = self.pbank()
                    self.projT(w, Tw, cc * 128, 128, hT[:, :, hs], ThT, raw, Traw)
                    z, Tz = self.tmpb[self.bidx % len(self.tmpb)]
                    self.bidx += 1
                    self.act(z, raw, AF.Silu, [Traw], [Tz])
                    self.tt(XT[:, c, hs], XT[:, c, hs], z, ALU.mult, [TXT, Tz], [TXT], eng=('vector' if (cc + half) % 2 == 0 else 'gpsimd'))


def _host_consts():
    theta = 500000.0
    invk = np.zeros(128, np.float32)
    invk[:32] = (theta ** (-(np.arange(32) % 16).astype(np.float32) / 16.0)).astype(np.float32)
    invi = np.zeros(128, np.float32)
    for base in (0, 64):
        invi[base:base + 16] = (theta ** (-(np.arange(16) % 8).astype(np.float32) / 8.0)).astype(np.float32)
    Pk = np.zeros((128, 128), np.float32)
    for m in range(16):
        Pk[m, m + 16] = -1.0
        Pk[m + 16, m] = 1.0
    Pi = np.zeros((128, 128), np.float32)
    for base in (0, 64):
        for m in range(8):
            Pi[base + m, base + m + 8] = -1.0
            Pi[base + m + 8, base + m] = 1.0
    blk = np.zeros((128, 128), np.float32)
    blk[:64, :64] = 1.0 / 64
    blk[64:, 64:] = 1.0 / 64
    cmat = np.concatenate([Pk.T, Pi.T, blk], axis=1).astype(np.float32)
    pow2 = np.broadcast_to((2.0 ** -(np.arange(NIT + 2) + 1.0)).astype(np.float32), (128, NIT + 2)).copy()
    return invk, invi, cmat, pow2


_PROG = {}


def _get_prog(dbg=()):
    key = tuple(sorted(dbg))
    if key not in _PROG:
        b = Builder(dbg)
        nc = b.build()
        _PROG[key] = (nc, b)
    return _PROG[key]


def make_in_maps(x, mem, positions, norm_gain, w_in, gmlp_ln_gain, gmlp_ln_bias, spatial_w, spatial_b, w_branch_a,
                 q_norm_gain, k_norm_gain, idx_k_ln_gain, idx_k_ln_bias, w_branch_b, mem_norm_gain, w_mem_kv,
                 mem_q_norm_gain, mem_k_norm_gain, w_branch_m, w_out):
    f = lambda a: np.ascontiguousarray(np.asarray(a), dtype=np.float32)
    x = f(x); mem = f(mem)
    positions = np.ascontiguousarray(np.asarray(positions), dtype=np.int32)
    invk, invi, cmat, pow2 = _host_consts()
    cols = np.zeros((128, NCOLS), np.float32)
    cols[:, C_GQ] = f(q_norm_gain)[0]
    cols[:, C_GK] = f(k_norm_gain)[0]
    cols[:, C_GMQ0] = f(mem_q_norm_gain)[0][:128]
    cols[:, C_GMQ1] = f(mem_q_norm_gain)[0][128:]
    cols[:, C_GMK0] = f(mem_k_norm_gain)[0][:128]
    cols[:, C_GMK1] = f(mem_k_norm_gain)[0][128:]
    cols[:, C_IKG] = np.tile(f(idx_k_ln_gain)[0], 2)
    cols[:, C_IKB] = np.tile(f(idx_k_ln_bias)[0], 2)
    cols[:, C_INVK] = invk
    cols[:, C_INVI] = invi
    cols[:, C_EPS] = EPS
    cols[:, C_NPI] = -PI
    rep = lambda v, n: np.ascontiguousarray(np.broadcast_to(f(v).reshape(1, -1), (128, n)))
    shared = {
        "w_in": f(w_in)[0], "w_a": f(w_branch_a)[0], "w_b": f(w_branch_b)[0], "w_m": f(w_branch_m)[0],
        "w_out": f(w_out)[0], "w_mkv": f(w_mem_kv)[0],
        "gbc": rep(norm_gain[0], D), "gmembc": rep(mem_norm_gain[0], D), "cols": cols,
        "lng": rep(gmlp_ln_gain[0], 1024), "lnb": rep(gmlp_ln_bias[0], 1024),
        "wsT": np.ascontiguousarray(f(spatial_w)[0].transpose(2, 0, 1)).reshape(128, 1024),
        "sbt": rep(f(spatial_b)[0].reshape(-1), 1024),
        "cmat": cmat, "pow2": pow2,
    }
    in_maps = []
    tt = np.arange(128)
    for c in range(8):
        b, j = c // 4, c % 4
        own = np.concatenate([np.arange((j + 4 * i) * 128, (j + 4 * i + 1) * 128) for i in range(NB)])
        cm = np.zeros((128, 4, 128), np.float32)
        for ktl in range(4):
            if ktl > j:
                cm[:, ktl, :] = NEG
            elif ktl == j:
                cm[:, ktl, :] = np.where(tt[None, :] <= tt[:, None], 0.0, NEG)
        m = dict(shared)
        m["xall"] = x[b]
        m["xown"] = np.ascontiguousarray(x[b][own])
        m["posall"] = np.ascontiguousarray(np.broadcast_to(positions[b][None, :], (128, SEQ)))
        m["posown"] = np.ascontiguousarray(np.broadcast_to(positions[b][own][None, :], (128, NOWN)))
        m["memx"] = mem[b]
        m["cmneg"] = cm.reshape(128, 512)
        in_maps.append(m)
    return in_maps


def kernel(**inputs):
    nc, _ = _get_prog()
    in_maps = make_in_maps(**inputs)
    res = run_bass_kernel_spmd(nc, in_maps, core_ids=list(range(8)))
    out = np.zeros((2, SEQ, D), np.float32)
    for c in range(8):
        b, j = c // 4, c % 4
        yc = np.asarray(res.results[c]["y"])
        for i in range(NB):
            g = j + 4 * i
            out[b, g * 128:(g + 1) * 128, :] = yc[i * 128:(i + 1) * 128, :]
    return out
```

```python
import numpy as np
from contextlib import ExitStack
import concourse.bass as bass
import concourse.mybir as mybir
from concourse.bass_utils import run_bass_kernel_spmd

F32 = mybir.dt.float32
BF16 = mybir.dt.bfloat16
I32 = mybir.dt.int32
AF = mybir.ActivationFunctionType
ALU = mybir.AluOpType

D = 2048
SEQ = 4096
NB = 8
NOWN = NB * 128
EPS = 1e-6
NIT = 20
TOPK = 256
NEG = -1.0e30
PI = float(np.pi)

O_AU, O_AV, O_AZ = 0, 1024, 2048
O_BQ, O_BK, O_BV, O_BZ = 3072, 4096, 4352, 4608
O_IQ, O_IK, O_IW = 5632, 6656, 6720
O_MQ, O_MZ = 6736, 7760
O_GA, O_GB, O_GM = 8784, 10832, 12880
D_IN = 14928

C_GQ, C_GK, C_GMQ0, C_GMQ1, C_GMK0, C_GMK1, C_IKG, C_IKB, C_INVK, C_INVI, C_EPS, C_NPI = range(12)
NCOLS = 16


class T:
    __slots__ = ("w", "r", "name", "excl")

    def __init__(self, name="", excl=False):
        self.w = None
        self.r = []
        self.name = name
        self.excl = excl


class Sched:
    ENG = ['sync', 'scalar', 'vector', 'gpsimd', 'tensor']

    def __init__(self, nc, stack):
        self.nc = nc
        self.stack = stack
        self.streams = {e: [] for e in self.ENG}
        self.count = {e: 0 for e in self.ENG}
        self.sems = {}
        self.waited = {e: {} for e in self.ENG}
        self.dcount = {}
        for e in self.ENG:
            self.sems[e] = stack.enter_context(nc.semaphore("s_" + e))

    def _deps(self, eng, reads, writes):
        need = {}

        def add(tok, kind):
            if tok is None:
                return
            k, v = tok
            if k == eng and (eng == 'tensor' or kind == 'war'):
                return
            if need.get(k, 0) < v:
                need[k] = v
        for r in reads:
            add(r.w, 'raw')
        for w in writes:
            add(w.w, 'waw')
            for t in w.r:
                add(t, 'war')
        waits = []
        for k, v in need.items():
            if k in self.dcount:
                v = self.dcount[k]
            if self.waited[eng].get(k, 0) >= v:
                continue
            self.waited[eng][k] = v
            waits.append((k, v))
        return waits

    def _record(self, tok, reads, writes):
        for r in reads:
            r.r.append(tok)
            if len(r.r) > 64:
                mx = {}
                for k, v in r.r:
                    if mx.get(k, 0) < v:
                        mx[k] = v
                r.r = list(mx.items())
        for w in writes:
            w.w = tok
            w.r = []

    def op(self, eng, fn, reads=(), writes=(), inc=True):
        if any(r.excl for r in reads):
            writes = list(writes) + [r for r in reads if r.excl and r not in writes]
            reads = [r for r in reads if not r.excl]
        waits = self._deps(eng, reads, writes)
        tok = (eng, self.count[eng] + 1)
        if inc:
            self.count[eng] += 1
        self.streams[eng].append((waits, fn, eng if inc else None, 1))
        self._record(tok, reads, writes)
        return tok

    def dma(self, eng, fn, key, reads=(), writes=()):
        if key not in self.sems:
            self.sems[key] = self.stack.enter_context(self.nc.semaphore("d_" + key))
            self.dcount[key] = 0
        waits = self._deps(eng, reads, writes)
        self.dcount[key] += 16
        tok = (key, self.dcount[key])
        self.streams[eng].append((waits, fn, key, 16))
        self._record(tok, reads, writes)
        return tok

    def barrier(self):
        toks = [(e, self.count[e]) for e in self.ENG if self.count[e] > 0]
        toks += [(k, v) for k, v in self.dcount.items() if v > 0]
        for e in self.ENG:
            waits = []
            for k, v in toks:
                if k == e:
                    continue
                if self.waited[e].get(k, 0) >= v:
                    continue
                self.waited[e][k] = v
                waits.append((k, v))
            if waits:
                self.streams[e].append((waits, None, None, 0))

    def emit(self):
        nc = self.nc
        S = self
        with nc.Block() as block:
            def mk(ename):
                def body(e):
                    for waits, fn, inc, n in S.streams[ename]:
                        for k, v in waits:
                            e.wait_ge(S.sems[k], v)
                        if fn is not None:
                            ins = fn(e)
                            if inc is not None:
                                ins.then_inc(S.sems[inc], n)
                return body
            block.sync(mk('sync'))
            block.scalar(mk('scalar'))
            block.vector(mk('vector'))
            block.gpsimd(mk('gpsimd'))
            block.tensor(mk('tensor'))


class StopBuild(Exception):
    pass


class Builder:
    def chk(self, name):
        if self.stop == name:
            raise StopBuild()

    def __init__(self, dbg=(), stop=None):
        self.dbg = set(dbg)
        self.stop = stop
        self.nc = bass.Bass("TRN2", target_bir_lowering=False)
        self.dbg_outs = {}

    def alloc(self, nwords):
        off = self.top
        self.top += (nwords + 7) // 8 * 8
        assert self.top <= self.AW, f"arena overflow {self.top} > {self.AW}"
        return off

    def f32(self, nwords, shape=None):
        off = self.alloc(nwords)
        return self.f32_at(off, nwords, shape)

    def f32_at(self, off, nwords, shape=None):
        ap = self.arena[:, off:off + nwords]
        if shape is not None:
            ap = self._reshape(ap, shape)
        return ap

    def bf(self, nelem, shape=None):
        assert nelem % 2 == 0
        off = self.alloc(nelem // 2)
        return self.bf_at(off, nelem, shape)

    def bf_at(self, off, nelem, shape=None):
        ap = self.arena[:, off:off + nelem // 2].bitcast(BF16)
        if shape is not None:
            ap = self._reshape(ap, shape)
        return ap

    @staticmethod
    def _reshape(ap, shape):
        if len(shape) == 2:
            return ap.rearrange("p (a b) -> p a b", a=shape[0], b=shape[1])
        if len(shape) == 3:
            return ap.rearrange("p (a b c) -> p a b c", a=shape[0], b=shape[1], c=shape[2])
        raise ValueError

    def pbank(self):
        b = self.ppool[self.pidx % len(self.ppool)]
        self.pidx += 1
        return self.psum[:, b, :], self.pT[b]

    def pbank_bf(self):
        b = self.ppool[self.pidx % len(self.ppool)]
        self.pidx += 1
        return self.psum[:, b, :].bitcast(BF16), self.pT[b]

    def act(self, out, in_, func, reads, writes, bias=None, scale=None, accum=None):
        kw = {}
        if bias is not None:
            kw['bias'] = bias
        if scale is not None:
            kw['scale'] = scale
        if accum is not None:
            kw['accum_out'] = accum
        return self.S.op('scalar', lambda e: e.activation(out=out, in_=in_, func=func, **kw), reads, writes)

    def tt(self, out, a, b, op, reads, writes, eng='vector'):
        return self.S.op(eng, lambda e: e.tensor_tensor(out=out, in0=a, in1=b, op=op), reads, writes)

    def ts(self, out, a, s1, s2, op0, op1, reads, writes, eng='vector', accum=None):
        if op1 is None:
            return self.S.op(eng, lambda e: e.tensor_scalar(out=out, in0=a, scalar1=s1, scalar2=None, op0=op0), reads, writes)
        if accum is not None:
            return self.S.op(eng, lambda e: e.tensor_scalar(out=out, in0=a, scalar1=s1, scalar2=s2, op0=op0, op1=op1, accum_out=accum), reads, writes)
        return self.S.op(eng, lambda e: e.tensor_scalar(out=out, in0=a, scalar1=s1, scalar2=s2, op0=op0, op1=op1), reads, writes)

    def stt(self, out, a, s, b, op0, op1, reads, writes, eng='vector'):
        return self.S.op(eng, lambda e: e.scalar_tensor_tensor(out=out, in0=a, scalar=s, in1=b, op0=op0, op1=op1), reads, writes)

    def copy(self, out, in_, reads, writes, eng='vector'):
        if eng == 'scalar':
            return self.S.op('scalar', lambda e: e.activation(out=out, in_=in_, func=AF.Copy), reads, writes)
        return self.S.op(eng, lambda e: e.tensor_copy(out=out, in_=in_), reads, writes)

    def recip(self, out, in_, reads, writes):
        return self.S.op('vector', lambda e: e.reciprocal(out=out, in_=in_), reads, writes)

    def mm(self, out, lhsT, rhs, start, stop, reads, writes, inc=None):
        if inc is None:
            inc = stop
        return self.S.op('tensor', lambda e: e.matmul(out, lhsT=lhsT, rhs=rhs, start=start, stop=stop), reads, writes, inc=inc)

    def transpose(self, out, in_, reads, writes, inc=True):
        ident = self.ident
        return self.S.op('tensor', lambda e: e.transpose(out=out, in_=in_, identity=ident), list(reads) + [self.Tconst], writes, inc=inc)

    def dma(self, q, out, in_, key, reads=(), writes=()):
        return self.S.dma(q, lambda e: e.dma_start(out=out, in_=in_), key, reads, writes)

    def tap(self, name, ap, Tt, shape):
        if name not in self.dbg:
            return
        o = self.nc.dram_tensor("dbg_" + name, list(shape), F32 if ap.dtype == F32 else ap.dtype, kind="ExternalOutput").ap()
        self.dbg_outs[name] = o
        tok = self.dma('sync', o, ap, 'dbg', reads=[Tt])
        self.final_toks.append(tok)

    def wload(self, dram2d, col0, ncols, krows=2048):
        i = self.widx % len(self.wbufs)
        self.widx += 1
        buf, Tb = self.wbufs[i]
        kc = krows // 128
        view = buf[:, 0:kc, 0:ncols]
        src = dram2d.rearrange("(c p) n -> p c n", p=128)[:, :, col0:col0 + ncols]
        self.dma('gpsimd', view, src, f'w{i}', writes=[Tb])
        return view, Tb

    def projT(self, w, Tw, wc0, nout, rhs3, Trhs, out_ps, Tps, kc=16):
        for k in range(kc):
            self.mm(out_ps, w[:, k, wc0:wc0 + nout], rhs3[:, k, :], k == 0, k == kc - 1, [Tw, Trhs], [Tps])

    def make_hT(self, xrows, gbc, Tg, dst3, Tdst, xi):
        xt, Txt = self.xbufs[xi % 2]
        self.dma('sync', xt, xrows, f'x{xi % 2}', writes=[Txt])
        ssq, Tss = self.small[self.sidx % len(self.small)]
        self.sidx += 1
        self.act(self.junk, xt, AF.Square, [Txt], [self.Tjunk, Tss], accum=ssq[:, 0:1])
        self.act(ssq[:, 1:2], ssq[:, 0:1], AF.Sqrt, [Tss, self.Tconst], [Tss], bias=self.cols[:, C_EPS:C_EPS + 1], scale=1.0 / D)
        self.recip(ssq[:, 2:3], ssq[:, 1:2], [Tss], [Tss])
        xs, Txs = self.xsbufs[xi % 2]
        self.stt(xs, xt, ssq[:, 2:3], gbc, ALU.mult, ALU.mult, [Txt, Tss, Tg], [Txs])
        for half in range(2):
            pb, Tp = self.pbank_bf()
            for c in range(8):
                kc = half * 8 + c
                self.transpose(pb[:, c * 128:(c + 1) * 128], xs[:, kc * 128:(kc + 1) * 128], [Txs], [Tp], inc=(c == 7))
            self.copy(dst3[:, half * 8:(half + 1) * 8, :], pb.rearrange("p (c t) -> p c t", c=8), [Tp], [Tdst],
                      eng=('scalar' if half == 0 else 'vector'))

    def hT_stream(self, tiles, gbc, Tg, banks):
        n = len(tiles)
        cols = self.cols

        def S0(t):
            xt, Txt = self.xbufs[t % 3]
            self.dma('sync', xt, tiles[t][0], f'x{t % 3}', writes=[Txt])

        def S1(t):
            xt, Txt = self.xbufs[t % 3]
            ssq, Tss = self.small[t % 4]
            xs, Txs = self.xsbufs[t % 2]
            self.act(xs, xt, AF.Square, [Txt], [Txs, Tss], accum=ssq[:, 0:1])
            self.act(ssq[:, 1:2], ssq[:, 0:1], AF.Ln, [Tss, self.Tconst], [Tss], bias=cols[:, C_EPS:C_EPS + 1], scale=1.0 / D)
            self.act(ssq[:, 2:3], ssq[:, 1:2], AF.Exp, [Tss], [Tss], scale=-0.5)

        def S2(t):
            xt, Txt = self.xbufs[t % 3]
            ssq, Tss = self.small[t % 4]
            xs, Txs = self.xsbufs[t % 2]
            self.stt(xs, xt, ssq[:, 2:3], gbc, ALU.mult, ALU.mult, [Txt, Tss, Tg], [Txs])
            for half in range(2):
                pbk, Tp = self.BK(banks[half])
                pb = pbk.bitcast(BF16)
                for c in range(8):
                    kc = half * 8 + c
                    self.transpose(pb[:, c * 128:(c + 1) * 128], xs[:, kc * 128:(kc + 1) * 128], [Txs], [Tp], inc=(c == 7))

        def S3(t):
            dst3, Tdst = tiles[t][1], tiles[t][2]
            for half in range(2):
                pbk, Tp = self.BK(banks[half])
                pb = pbk.bitcast(BF16)
                self.copy(dst3[:, half * 8:(half + 1) * 8, :], pb.rearrange("p (c t) -> p c t", c=8), [Tp], [Tdst],
                          eng=('scalar' if half == 0 else 'vector'))

        S0(0)
        for r in range(n + 2):
            if r + 1 < n:
                S0(r + 1)
            if r < n:
                S1(r)
            if 0 <= r - 2 < n:
                S3(r - 2)
            if 0 <= r - 1 < n:
                S2(r - 1)
            yield

    def trig_tables(self, posf, Tpos, invcol, n, Cout, Sout, Tout):
        tb = self.trig_scr
        Tt = self.Ttrig
        ang, ki, kf, r = tb[0][:, 0:n], tb[1][:, 0:n].bitcast(I32), tb[2][:, 0:n], tb[3][:, 0:n]
        self.ts(ang, posf, invcol, None, ALU.mult, None, [Tpos, self.Tconst], [Tt])
        for which, dst in ((0, Sout), (1, Cout)):
            if which == 1:
                self.ts(ang, ang, PI / 2, None, ALU.add, None, [Tt], [Tt])
            self.ts(ki, ang, 1.0 / (2 * PI), None, ALU.mult, None, [Tt], [Tt])
            self.copy(kf, ki, [Tt], [Tt])
            self.stt(r, kf, -2 * PI, ang, ALU.mult, ALU.add, [Tt], [Tt])
            self.ts(kf, r, PI, -2 * PI, ALU.is_gt, ALU.mult, [Tt], [Tt])
            self.tt(r, r, kf, ALU.add, [Tt], [Tt])
            self.ts(kf, r, -PI, 2 * PI, ALU.is_lt, ALU.mult, [Tt], [Tt])
            self.tt(r, r, kf, ALU.add, [Tt], [Tt])
            self.ts(r, r, -PI, PI, ALU.max, ALU.min, [Tt], [Tt])
            self.act(dst, r, AF.Sin, [Tt], [Tout])

    def trig2(self, posf, Tpos, n, Stab, Ctab, Ttab):
        tb = self.trig_scr
        Tt = self.Ttrig
        r3 = lambda a: a[:, 0:2 * n].rearrange("p (y s) -> p y s", y=2)
        ang, ki, kf, r = tb[0][:, 0:2 * n], tb[1][:, 0:2 * n].bitcast(I32), tb[2][:, 0:2 * n], tb[3][:, 0:2 * n]
        inv2 = self.cols[:, C_INVK:C_INVK + 2]
        self.tt(r3(tb[0]), posf[:, 0:n].unsqueeze(1).to_broadcast([128, 2, n]), inv2.unsqueeze(2).to_broadcast([128, 2, n]),
                ALU.mult, [Tpos, self.Tconst], [Tt])
        for which, dst in ((0, Stab), (1, Ctab)):
            if which == 1:
                self.ts(ang, ang, PI / 2, None, ALU.add, None, [Tt], [Tt])
            self.ts(ki, ang, 1.0 / (2 * PI), None, ALU.mult, None, [Tt], [Tt])
            self.copy(kf, ki, [Tt], [Tt])
            self.stt(r, kf, -2 * PI, ang, ALU.mult, ALU.add, [Tt], [Tt])
            self.ts(r, r, -PI, PI, ALU.max, ALU.min, [Tt], [Tt])
            self.act(dst.rearrange("p y s -> p (y s)"), r, AF.Sin, [Tt], [Ttab])

    def rope_combine(self, out_bf, x_ap, Tx, xb_bf, Txb, ropeP, Cc, Sn, Ttab, Tout, n):
        pa, Tpa = self.pbank()
        self.mm(pa[:, 0:n], ropeP, xb_bf, True, True, [Txb, self.Tconst], [Tpa])
        t1, T1 = self.tmpf[self.tidx % len(self.tmpf)]
        self.tidx += 1
        t2, T2 = self.tmpf[self.tidx % len(self.tmpf)]
        self.tidx += 1
        self.tt(t1[:, 0:n], x_ap, Cc, ALU.mult, [Tx, Ttab], [T1])
        self.tt(t2[:, 0:n], pa[:, 0:n], Sn, ALU.mult, [Tpa, Ttab], [T2])
        self.tt(out_bf, t1[:, 0:n], t2[:, 0:n], ALU.add, [T1, T2], [Tout], eng='gpsimd')

    def rms_T(self, raws, Traws, ones, invn, n):
        pa, Tpa = self.pbank()
        for i, (raw, Tr) in enumerate(zip(raws, Traws)):
            sq, Tsq = self.tmpb[self.bidx % len(self.tmpb)]
            self.bidx += 1
            self.act(sq[:, 0:n], raw, AF.Square, [Tr], [Tsq])
            self.mm(pa[:, 0:n], ones, sq[:, 0:n], i == 0, i == len(raws) - 1, [Tsq, self.Tconst], [Tpa])
        sd, Tsd = self.tmpf[self.tidx % len(self.tmpf)]
        self.tidx += 1
        self.act(sd[:, 0:n], pa[:, 0:n], AF.Ln, [Tpa, self.Tconst], [Tsd], bias=self.cols[:, C_EPS:C_EPS + 1], scale=invn)
        self.act(sd[:, 0:n], sd[:, 0:n], AF.Exp, [Tsd], [Tsd], scale=-0.5)
        return sd[:, 0:n], Tsd

    def g_rope_combine(self, out_bf, x_ap, Tx, xb_bf, Txb, ropeP, Cc, Sn, Ttab, Tout, n):
        pa, Tpa = self.pbank()
        self.mm(pa[:, 0:n], ropeP, xb_bf, True, True, [Txb, self.Tconst], [Tpa])
        yield
        t1, T1 = self.tmpf[self.tidx % len(self.tmpf)]
        self.tidx += 1
        t2, T2 = self.tmpf[self.tidx % len(self.tmpf)]
        self.tidx += 1
        self.tt(t1[:, 0:n], x_ap, Cc, ALU.mult, [Tx, Ttab], [T1])
        self.tt(t2[:, 0:n], pa[:, 0:n], Sn, ALU.mult, [Tpa, Ttab], [T2])
        self.tt(out_bf, t1[:, 0:n], t2[:, 0:n], ALU.add, [T1, T2], [Tout], eng='gpsimd')

    def g_rms_T(self, raws, Traws, ones, invn, n):
        pa, Tpa = self.pbank()
        for i, (raw, Tr) in enumerate(zip(raws, Traws)):
            sq, Tsq = self.tmpb[self.bidx % len(self.tmpb)]
            self.bidx += 1
            self.act(sq[:, 0:n], raw, AF.Square, [Tr], [Tsq])
            self.mm(pa[:, 0:n], ones, sq[:, 0:n], i == 0, i == len(raws) - 1, [Tsq, self.Tconst], [Tpa])
        yield
        sd, Tsd = self.tmpf[self.tidx % len(self.tmpf)]
        self.tidx += 1
        self.act(sd[:, 0:n], pa[:, 0:n], AF.Ln, [Tpa, self.Tconst], [Tsd], bias=self.cols[:, C_EPS:C_EPS + 1], scale=invn)
        self.act(sd[:, 0:n], sd[:, 0:n], AF.Exp, [Tsd], [Tsd], scale=-0.5)
        return sd[:, 0:n], Tsd

    def mkctx(self, nf, nb_, raw_bank, aux_banks):
        return dict(tmpf=[(self.f32(512), T("cf")) for _ in range(nf)], tmpb=[(self.bf(512), T("cb")) for _ in range(nb_)],
                    ppool=list(aux_banks), raw=raw_bank, pidx=0, tidx=0, bidx=0)

    def _load(self, ctx):
        self.tmpf, self.tmpb, self.ppool = ctx['tmpf'], ctx['tmpb'], ctx['ppool']
        self.pidx, self.tidx, self.bidx = ctx['pidx'], ctx['tidx'], ctx['bidx']

    def _save(self, ctx):
        ctx['pidx'], ctx['tidx'], ctx['bidx'] = self.pidx, self.tidx, self.bidx

    def interleave(self, items):
        active = list(items)
        while active:
            for it in list(active):
                gen, ctx = it
                self._load(ctx)
                try:
                    next(gen)
                except StopIteration:
                    active.remove(it)
                self._save(ctx)

    def BK(self, b):
        return self.psum[:, b, :], self.pT[b]

    def build(self):
        nc = self.nc
        dbg = self.dbg
        di = lambda name, shape, dt=F32: nc.dram_tensor(name, list(shape), dt, kind="ExternalInput").ap()
        xall = di("xall", [SEQ, D])
        xown = di("xown", [NOWN, D])
        posall = di("posall", [128, SEQ], I32)
        posown = di("posown", [128, NOWN], I32)
        w_in = di("w_in", [D, D_IN])
        w_a = di("w_a", [1024, D])
        w_b = di("w_b", [1024, D])
        w_m = di("w_m", [1024, D])
        w_out = di("w_out", [D, D])
        w_mkv = di("w_mkv", [D, 2048])
        memx = di("memx", [256, D])
        gbc_d = di("gbc", [128, D])
        gmem_d = di("gmembc", [128, D])
        cols_d = di("cols", [128, NCOLS])
        lng_d = di("lng", [128, 1024])
        lnb_d = di("lnb", [128, 1024])
        wsT_d = di("wsT", [128, 8 * 128])
        sbt_d = di("sbt", [128, 8 * 128])
        cmneg_d = di("cmneg", [128, 512])
        cmat_d = di("cmat", [128, 3 * 128])
        pow2_d = di("pow2", [128, NIT + 2])
        y = nc.dram_tensor("y", [NOWN, D], F32, kind="ExternalOutput").ap()

        with ExitStack() as st:
            S = self.S = Sched(nc, st)
            self.AW = 53200
            self.arena = st.enter_context(nc.sbuf_tensor("arena", [128, self.AW], F32))
            self.psum = st.enter_context(nc.psum_tensor("psum", [128, 8, 512], F32))
            self.pT = [T(f"ps{b}", excl=True) for b in range(8)]
            self.ppool = list(range(8))
            self.pidx = 0
            self.top = 0
            self.final_toks = []
            self.widx = self.sidx = self.tidx = self.bidx = 0

            try:
                self.Tconst = Tconst = T("const")
                self.ident = self.bf(128)
                ones = self.bf(128)
                ropePk = self.bf(128)
                ropePi = self.bf(128)
                blk64 = self.bf(128)
                self.cols = cols = self.f32(NCOLS)
                pow2 = self.f32(NIT + 2)
                identf = self.f32(128)
                cmat_f = self.f32(3 * 128)
                S.op('gpsimd', lambda e: e.memset(identf, 0.0), [], [Tconst])
                S.op('gpsimd', lambda e: e.affine_select(out=identf, in_=identf, pattern=[[-1, 128]], compare_op=ALU.not_equal,
                                                         fill=1.0, base=0, channel_multiplier=1), [Tconst], [Tconst])
                self.copy(self.ident, identf, [Tconst], [Tconst])
                S.op('vector', lambda e: e.memset(ones, 1.0), [], [Tconst])
                self.dma('sync', cols, cols_d, 'c0', writes=[Tconst])
                self.dma('sync', pow2, pow2_d, 'c0', writes=[Tconst])
                self.dma('sync', cmat_f, cmat_d, 'c0', writes=[Tconst])
                self.copy(ropePk, cmat_f[:, 0:128], [Tconst], [Tconst])
                self.copy(ropePi, cmat_f[:, 128:256], [Tconst], [Tconst])
                self.copy(blk64, cmat_f[:, 256:384], [Tconst], [Tconst])
                base_top = self.top
                self.chk('C')

                kT = self.bf(2 * SEQ, (2, SEQ)); TkT = T("kT")
                Vt = self.bf(32 * 256, (32, 256)); TV = T("V")
                ikT = self.bf(SEQ); Tik = T("ikT")
                kvi_top = self.top

                gbc = self.f32(D); Tg = T("gbc")
                self.dma('scalar', gbc, gbc_d, 'c1', writes=[Tg])
                self.xbufs = [(self.f32(D), T(f"x{i}")) for i in range(3)]
                self.xsbufs = [(self.bf(D), T("xs0")), (self.bf(D), T("xs1"))]
                self.small = [(self.f32(8), T(f"sm{i}")) for i in range(4)]
                hbufs = [(self.bf(16 * 512, (16, 512)), T(f"hTg{i}")) for i in range(2)]
                wkv = self.bf(16 * 640, (16, 640)); Twkv = T("wkv")
                posf = self.f32(512); Tposf = T("posf")
                posi = self.f32(512).bitcast(I32); Tposi = T("posi")
                self.trig_scr = [self.f32(1024) for _ in range(4)]; self.Ttrig = T("trig")
                tabs = [(self.f32(1024, (2, 512)), self.f32(1024, (2, 512)), T(f"tab{i}")) for i in range(2)]
                ctxs = [self.mkctx(3, 2, 0, [1]), self.mkctx(5, 3, 2, [3])]
                ctxV = self.mkctx(0, 0, None, [6, 7])
                ctxH = self.mkctx(0, 0, None, [4, 5])

                wv3 = w_in.rearrange("(c p) n -> p c n", p=128)
                self.dma('gpsimd', wkv[:, :, 0:512], wv3[:, :, O_BK:O_BK + 512], 'wk', writes=[Twkv])
                self.dma('gpsimd', wkv[:, :, 512:576], wv3[:, :, O_IK:O_IK + 64], 'wk', writes=[Twkv])
                self.dma('gpsimd', wkv[:, :, 576:640], wv3[:, :, O_IK:O_IK + 64], 'wk', writes=[Twkv])

                NG = SEQ // 512

                def a_trig(g):
                    St, Ct, Ttb = tabs[g % 2]
                    self.dma('scalar', posi, posall[:, g * 512:(g + 1) * 512], 'pos', writes=[Tposi])
                    self.copy(posf, posi, [Tposi], [Tposf])
                    self.trig2(posf, Tposf, 512, St, Ct, Ttb)

                def a_hT(g, tt_):
                    hTg, ThTg = hbufs[g % 2]
                    ti = g * 4 + tt_
                    self.make_hT(xall[ti * 128:(ti + 1) * 128, :], gbc, Tg, hTg[:, :, tt_ * 128:(tt_ + 1) * 128], ThTg, ti)

                def g_khead(g, kvh, ctx):
                    hTg, ThTg = hbufs[g % 2]
                    St, Ct, Ttb = tabs[g % 2]
                    gs = slice(g * 512, (g + 1) * 512)
                    raw, Traw = self.BK(ctx['raw'])
                    self.projT(wkv, Twkv, kvh * 128, 128, hTg, ThTg, raw, Traw)
                    yield
                    rstd, Trs = yield from self.g_rms_T([raw], [Traw], ones, 1.0 / 128, 512)
                    kn, Tkn = self.tmpb[self.bidx % len(self.tmpb)]
                    self.bidx += 1
                    self.stt(kn, raw, cols[:, C_GK:C_GK + 1], rstd, ALU.mult, ALU.mult, [Traw, Trs, Tconst], [Tkn])
                    yield from self.g_rope_combine(kT[:, kvh, gs], kn, Tkn, kn, Tkn, ropePk, Ct[:, 0, :], St[:, 0, :], Ttb, TkT, 512)

                def g_ik(g, ctx):
                    hTg, ThTg = hbufs[g % 2]
                    St, Ct, Ttb = tabs[g % 2]
                    gs = slice(g * 512, (g + 1) * 512)
                    raw, Traw = self.BK(ctx['raw'])
                    self.projT(wkv, Twkv, 512, 128, hTg, ThTg, raw, Traw)
                    yield
                    ikb, Tikb = self.tmpb[0]
                    ikf, Tikf = self.tmpf[0]
                    self.copy(ikb, raw, [Traw], [Tikb], eng='scalar')
                    self.copy(ikf, raw, [Traw], [Tikf], eng='scalar')
                    pm, Tpm = self.pbank()
                    self.mm(pm, blk64, ikb, True, True, [Tikb, Tconst], [Tpm])
                    yield
                    cen, Tcen = self.tmpf[1]
                    self.tt(cen, ikf, pm, ALU.subtract, [Tikf, Tpm], [Tcen])
                    sq, Tsq = self.tmpb[1]
                    self.act(sq, cen, AF.Square, [Tcen], [Tsq])
                    pv, Tpv = self.pbank()
                    self.mm(pv, blk64, sq, True, True, [Tsq, Tconst], [Tpv])
                    yield
                    sd, Tsd = self.tmpf[2]
                    self.act(sd, pv, AF.Ln, [Tpv, Tconst], [Tsd], bias=cols[:, C_EPS:C_EPS + 1], scale=1.0)
                    self.act(sd, sd, AF.Exp, [Tsd], [Tsd], scale=-0.5)
                    self.tt(cen, cen, sd, ALU.mult, [Tcen, Tsd], [Tcen])
                    ikn, Tikn = self.tmpb[2]
                    self.ts(ikn, cen, cols[:, C_IKG:C_IKG + 1], cols[:, C_IKB:C_IKB + 1], ALU.mult, ALU.add, [Tcen, Tconst], [Tikn])
                    self.tidx = 3
                    yield from self.g_rope_combine(ikT[:, gs], ikn, Tikn, ikn, Tikn, ropePi, Ct[:, 1, :], St[:, 1, :], Ttb, Tik, 512)

                def g_v(g):
                    hTg, ThTg = hbufs[g % 2]
                    prev = None
                    for tt_ in range(4):
                        ti = g * 4 + tt_
                        pvb, Tpvb = self.BK(6 + (tt_ % 2))
                        for k in range(16):
                            self.mm(pvb[:, 0:256], hTg[:, k, tt_ * 128:(tt_ + 1) * 128], wkv[:, k, 256:512], k == 0, k == 15, [ThTg, Twkv], [Tpvb])
                        if prev is not None:
                            self.copy(Vt[:, prev[0], :], prev[1][:, 0:256], [prev[2]], [TV], eng='scalar')
                        prev = (ti, pvb, Tpvb)
                        yield
                    self.copy(Vt[:, prev[0], :], prev[1][:, 0:256], [prev[2]], [TV], eng='scalar')

                def g_kboth(g, ctx):
                    yield from g_khead(g, 0, ctx)
                    yield from g_khead(g, 1, ctx)

                all_tiles = []
                for g in range(NG):
                    hTg, ThTg = hbufs[g % 2]
                    for tt_ in range(4):
                        ti = g * 4 + tt_
                        all_tiles.append((xall[ti * 128:(ti + 1) * 128, :], hTg[:, :, tt_ * 128:(tt_ + 1) * 128], ThTg))
                prod = self.hT_stream(all_tiles, gbc, Tg, [4, 5])

                def g_prod(nrounds):
                    for _ in range(nrounds):
                        try:
                            next(prod)
                        except StopIteration:
                            return
                        yield

                a_trig(0)
                for _ in g_prod(6):
                    pass
                for g in range(NG):
                    if g + 1 < NG:
                        a_trig(g + 1)
                    items = [(g_v(g), ctxV)]
                    if g + 1 < NG:
                        items.append((g_prod(4), ctxH))
                    items += [(g_kboth(g, ctxs[0]), ctxs[0]), (g_ik(g, ctxs[1]), ctxs[1])]
                    self.interleave(items)
                self.ppool = list(range(8))

                self.tap("kT", kT, TkT, [128, 2, SEQ])
                self.tap("V", Vt, TV, [128, 32, 256])
                self.tap("ikT", ikT, Tik, [128, SEQ])
                S.barrier()
                if self.stop == 'A':
                    self._finish(y)
                    return nc

                self.top = kvi_top
                hT = self.bf(16 * NOWN, (16, NOWN)); ThT = T("hT")
                BT = self.bf(8 * NOWN, (8, NOWN)); TBT = T("BT")
                wbase = self.top
                self.wbufs = [(self.bf(16 * 512, (16, 512)), T("w0")), (self.bf(16 * 512, (16, 512)), T("w1"))]
                b1_top = self.top
                gbc = self.f32(D); Tg = T("gbc2")
                self.dma('scalar', gbc, gbc_d, 'c1', writes=[Tg])
                self.xbufs = [(self.f32(D), T(f"x{i}")) for i in range(3)]
                self.xsbufs = [(self.bf(D), T("xs0")), (self.bf(D), T("xs1"))]
                self.small = [(self.f32(8), T(f"sm{i}")) for i in range(4)]
                own_tiles = [(xown[ti * 128:(ti + 1) * 128, :], hT[:, :, ti * 128:(ti + 1) * 128], ThT) for ti in range(NB)]
                for _ in self.hT_stream(own_tiles, gbc, Tg, [4, 5]):
                    pass
                self.tap("hT", hT, ThT, [128, 16, NOWN])
                S.barrier()
                if self.stop == 'B0':
                    self._finish(y)
                    return nc

                self.top = b1_top
                iqT = self.bf(8 * NOWN, (8, NOWN)); TiqT = T("iqT")
                qT = self.bf(8 * NOWN, (8, NOWN)); TqT = T("qT")
                Sbuf = self.f32(SEQ); TS = T("S")
                iwf = self.f32(NB * 16, (NB, 16)); Tiw = T("iw")
                self.small = [(self.f32(8), T(f"sm{i}")) for i in range(4)]
                bis = self.f32(16); Tbis = T("bis")
                Wtab = self.f32(NIT + 2); TW = T("Wtab")
                cmnegb = self.bf(512); Tcm = T("cmneg")
                self.dma('gpsimd', cmnegb, cmneg_d, 'c3', writes=[Tcm])
                gqs = self.f32(8); Tgqs = T("gqs")
                self.ts(gqs[:, 0:1], cols[:, C_GQ:C_GQ + 1], float(128 ** -0.5), None, ALU.mult, None, [Tconst], [Tgqs])
                b1e_top = self.top
                diags = [(self.bf(16 * 128, (16, 128)), T(f"diag{i}")) for i in range(2)]
                negmT = self.bf(32 * 128, (32, 128)); TnT = T("negmT")
                Sbuf2 = self.f32(SEQ); TS2 = T("S2")
                self.top = b1e_top
                posf = self.f32(512); Tposf = T("posf")
                posi = self.f32(512).bitcast(I32); Tposi = T("posi")
                self.trig_scr = [self.f32(1024) for _ in range(4)]; self.Ttrig = T("trig")
                Ttab = T("tabo")
                So = [Sbuf[:, 0:1024].rearrange("p (y s) -> p y s", y=2), Sbuf[:, 1024:2048].rearrange("p (y s) -> p y s", y=2)]
                Co = [Sbuf[:, 2048:3072].rearrange("p (y s) -> p y s", y=2), Sbuf[:, 3072:4096].rearrange("p (y s) -> p y s", y=2)]
                for half in range(2):
                    self.dma('scalar', posi, posown[:, half * 512:(half + 1) * 512], 'pos', writes=[Tposi])
                    self.copy(posf, posi, [Tposi], [Tposf])
                    self.trig2(posf, Tposf, 512, So[half], Co[half], Ttab)

                S.barrier()
                self.top = b1e_top
                qctx = [self.mkctx(3, 2, 0, [1]), self.mkctx(3, 2, 2, [3]), self.mkctx(3, 2, 4, [5])]

                def g_iq(w, Tw, pp, p, half, ctx):
                    hs = slice(half * 512, (half + 1) * 512)
                    raw, Traw = self.BK(ctx['raw'])
                    self.projT(w, Tw, pp * 128, 128, hT[:, :, hs], ThT, raw, Traw)
                    yield
                    iqb, Tiqb = self.tmpb[self.bidx % len(self.tmpb)]
                    self.bidx += 1
                    self.copy(iqb, raw, [Traw], [Tiqb], eng='scalar')
                    yield from self.g_rope_combine(iqT[:, p, hs], raw, Traw, iqb, Tiqb, ropePi, Co[half][:, 1, :], So[half][:, 1, :], Ttab, TiqT, 512)

                def g_q(w, Tw, hh, h, half, ctx):
                    hs = slice(half * 512, (half + 1) * 512)
                    raw, Traw = self.BK(ctx['raw'])
                    self.projT(w, Tw, hh * 128, 128, hT[:, :, hs], ThT, raw, Traw)
                    yield
                    rstd, Trs = yield from self.g_rms_T([raw], [Traw], ones, 1.0 / 128, 512)
                    qn, Tqn = self.tmpb[self.bidx % len(self.tmpb)]
                    self.bidx += 1
                    self.stt(qn, raw, gqs[:, 0:1], rstd, ALU.mult, ALU.mult, [Traw, Trs, Tgqs], [Tqn])
                    yield from self.g_rope_combine(qT[:, h, hs], qn, Tqn, qn, Tqn, ropePk, Co[half][:, 0, :], So[half][:, 0, :], Ttab, TqT, 512)

                def run_batches(mk):
                    for b0 in range(0, len(mk), 3):
                        batch = mk[b0:b0 + 3]
                        self.interleave([(f(qctx[j]), qctx[j]) for j, f in enumerate(batch)])

                for ch in range(2):
                    w, Tw = self.wload(w_in, O_IQ + ch * 512, 512)
                    run_batches([(lambda ctx, pp=pp, half=half, w=w, Tw=Tw, ch=ch: g_iq(w, Tw, pp, ch * 4 + pp, half, ctx))
                                 for pp in range(4) for half in range(2)])
                for ch in range(2):
                    w, Tw = self.wload(w_in, O_BQ + ch * 512, 512)
                    run_batches([(lambda ctx, hh=hh, half=half, w=w, Tw=Tw, ch=ch: g_q(w, Tw, hh, ch * 4 + hh, half, ctx))
                                 for hh in range(4) for half in range(2)])
                self.ppool = [6, 7]
                self.pidx = 0
                w, Tw = self.wload(w_in, O_IW, 16)
                for i in range(NB):
                    pb, Tp = self.pbank()
                    for k in range(16):
                        self.mm(pb[:, 0:16], hT[:, k, i * 128:(i + 1) * 128], w[:, k, 0:16], k == 0, k == 15, [ThT, Tw], [Tp])
                    self.ts(iwf[:, i, :], pb[:, 0:16], float(0.25 * 0.125), None, ALU.mult, None, [Tp], [Tiw])
                self.tap("iqT", iqT, TiqT, [128, 8, NOWN])
                self.tap("qT", qT, TqT, [128, 8, NOWN])
                self.tap("iw", iwf, Tiw, [128, NB, 16])
                S.barrier()
                if self.stop == 'B1a':
                    self._finish(y)
                    return nc

                save_top = self.top
                self.top = wbase
                relu = [(self.bf(512), T(f"relu{i}")) for i in range(4)]
                Pb = [(self.bf(512), T(f"P{i}")) for i in range(4)]
                negc = self.bf(1024); Tnc = T("negc")
                self.tmpf = [(self.f32(512), T(f"tf{i}")) for i in range(4)]
                junkb = self.bf(SEQ); Tjb = T("junkb")
                assert self.top <= b1_top
                self.top = save_top
                Sbufs = [(Sbuf, TS), (Sbuf2, TS2)]
                BK = lambda b: (self.psum[:, b, :], self.pT[b])

                def emit_diag(i):
                    dg, Tdg = diags[i % 2]
                    self.tt(dg, self.ident.unsqueeze(1).to_broadcast([128, 16, 128]),
                            iwf[:, i, :].unsqueeze(2).to_broadcast([128, 16, 128]), ALU.mult, [Tconst, Tiw], [Tdg])

                def emit_idx(i):
                    Sb, TSb = Sbufs[i % 2]
                    dg, Tdg = diags[i % 2]
                    qs = slice(i * 128, (i + 1) * 128)
                    steps = [(c, h) for c in range(i + 1) for h in range(16)]
                    xb_ = (0, 1, 5)

                    def A(n):
                        c, h = steps[n]
                        p, sub = h // 2, h % 2
                        ps_ = slice(sub * 64, (sub + 1) * 64)
                        xh, Txh = BK(xb_[n % 3])
                        self.mm(xh, iqT[ps_, p, qs], ikT[ps_, c * 512:(c + 1) * 512], True, True, [TiqT, Tik], [Txh])

                    A(0)
                    if len(steps) > 1:
                        A(1)
                    for n, (c, h) in enumerate(steps):
                        cs = slice(c * 512, (c + 1) * 512)
                        acc, Tacc = BK(2 + (c % 2))
                        last = (c == i)
                        xh, Txh = BK(xb_[n % 3])
                        rl, Trl = relu[n % 4]
                        self.act(rl, xh, AF.Relu, [Txh], [Trl])
                        self.mm(acc, dg[:, h, :], rl, h == 0, (h == 15 and not last), [Tdg, Trl], [Tacc])
                        if h == 15:
                            if last:
                                self.mm(acc, self.ident, cmnegb, False, True, [Tconst, Tcm], [Tacc])
                            self.copy(Sb[:, cs], acc, [Tacc], [TSb], eng='scalar')
                        if n + 2 < len(steps):
                            A(n + 2)

                def emit_bis(i):
                    Sb, TSb = Sbufs[i % 2]
                    nk = 512 * (i + 1)
                    Sv = Sb[:, 0:nk]
                    hi0, lo0, mid, cnt, tmp, thr = (bis[:, j:j + 1] for j in range(6))
                    S.op('vector', lambda e: e.tensor_reduce(out=hi0, in_=Sv, axis=mybir.AxisListType.X, op=ALU.max), [TSb], [Tbis])
                    t_f, Tt_f = self.tmpf[0]
                    ls = slice(i * 512, (i + 1) * 512)
                    self.ts(t_f, Sb[:, ls], -1.0e29, 2.0e30, ALU.is_lt, ALU.mult, [TSb], [Tt_f])
                    self.tt(t_f, t_f, Sb[:, ls], ALU.add, [Tt_f, TSb], [Tt_f])
                    S.op('vector', lambda e: e.tensor_reduce(out=lo0, in_=t_f, axis=mybir.AxisListType.X, op=ALU.min), [Tt_f], [Tbis])
                    if i > 0:
                        Su = Sb[:, 0:i * 512]
                        S.op('vector', lambda e: e.tensor_reduce(out=tmp, in_=Su, axis=mybir.AxisListType.X, op=ALU.min), [TSb], [Tbis])
                        self.tt(lo0, lo0, tmp, ALU.min, [Tbis], [Tbis])
                    self.tt(tmp, lo0, lo0, ALU.mult, [Tbis], [Tbis])
                    self.stt(tmp, hi0, hi0, tmp, ALU.mult, ALU.add, [Tbis], [Tbis])
                    self.ts(tmp, tmp, 1.0, -1.0e-4, ALU.add, ALU.mult, [Tbis], [Tbis])
                    self.tt(lo0, lo0, tmp, ALU.add, [Tbis], [Tbis])
                    self.tt(tmp, hi0, lo0, ALU.subtract, [Tbis], [Tbis])
                    self.ts(tmp, tmp, 1.0e-6, None, ALU.add, None, [Tbis], [Tbis])
                    self.ts(Wtab, pow2, tmp, None, ALU.mult, None, [Tconst, Tbis], [TW])
                    self.tt(mid, lo0, Wtab[:, 0:1], ALU.add, [Tbis, TW], [Tbis])
                    for k in range(NIT):
                        self.ts(junkb[:, 0:nk], Sv, mid, 0.0, ALU.is_ge, ALU.add, [TSb, Tbis], [Tjb, Tbis], accum=cnt)
                        self.ts(tmp, cnt, TOPK - 0.5, Wtab[:, k:k + 1], ALU.is_ge, ALU.mult, [Tbis, TW], [Tbis])
                        self.stt(mid, tmp, Wtab[:, k + 1:k + 2], mid, ALU.subtract, ALU.add, [Tbis, TW], [Tbis])
                    self.tt(thr, mid, Wtab[:, NIT:NIT + 1], ALU.subtract, [Tbis, TW], [Tbis])
                    if i == 3:
                        self.tap("S3", Sb, TSb, [128, SEQ])
                        self.tap("thr3", bis, Tbis, [128, 16])
                    nt = 4 * (i + 1)
                    for c8 in range((nt + 7) // 8):
                        n8 = min(8, nt - c8 * 8)
                        self.ts(negc[:, 0:n8 * 128], Sb[:, c8 * 1024:c8 * 1024 + n8 * 128], thr, -30000.0, ALU.is_lt, ALU.mult, [TSb, Tbis], [Tnc])
                        pbk, Tp = BK(4 + (c8 % 2))
                        pb = pbk.bitcast(BF16)
                        for t8 in range(n8):
                            self.transpose(pb[:, t8 * 128:(t8 + 1) * 128], negc[:, t8 * 128:(t8 + 1) * 128], [Tnc], [Tp], inc=(t8 == n8 - 1))
                        self.copy(negmT[:, c8 * 8:c8 * 8 + n8, :], pb[:, 0:n8 * 128].rearrange("p (c t) -> p c t", c=n8), [Tp], [TnT], eng='vector')

                def emit_att(i):
                    nt = 4 * (i + 1)
                    qs = slice(i * 128, (i + 1) * 128)
                    steps = [(kvh, kt) for kvh in range(2) for kt in range(nt)]

                    def L(n):
                        kvh, kt = steps[n]
                        lg, Tlg = BK(4 + (n % 2))
                        lg4 = lg.rearrange("p (h t) -> p h t", h=4)
                        self.mm(lg4, kT[:, kvh, kt * 128:(kt + 1) * 128], qT[:, 4 * kvh:4 * kvh + 4, qs], True, False, [TkT, TqT], [Tlg])
                        self.mm(lg4, self.ident, negmT[:, kt, :].unsqueeze(1).to_broadcast([128, 4, 128]), False, True, [Tconst, TnT], [Tlg])

                    L(0)
                    for n, (kvh, kt) in enumerate(steps):
                        oacc, Toa = BK(6 if kvh == 0 else 2)
                        sacc, Tsa = BK(7 if kvh == 0 else 3)
                        lg, Tlg = BK(4 + (n % 2))
                        pbuf, TP = Pb[n % 4]
                        self.act(pbuf, lg, AF.Exp, [Tlg], [TP])
                        if n + 1 < len(steps):
                            L(n + 1)
                        self.mm(oacc, Vt[:, kt, kvh * 128:(kvh + 1) * 128], pbuf, kt == 0, kt == nt - 1, [TV, TP], [Toa])
                        self.mm(sacc, ones, pbuf, kt == 0, kt == nt - 1, [Tconst, TP], [Tsa])
                        if kt == nt - 1:
                            rs, Trs = self.tmpf[1 + kvh]
                            self.recip(rs, sacc, [Tsa], [Trs])
                            self.tt(BT[:, 4 * kvh:4 * kvh + 4, qs], oacc.rearrange("p (h t) -> p h t", h=4),
                                    rs.rearrange("p (h t) -> p h t", h=4), ALU.mult, [Toa, Trs], [TBT])

                emit_diag(0)
                emit_idx(0)
                for i in range(NB):
                    if i + 1 < NB:
                        emit_diag(i + 1)
                        emit_idx(i + 1)
                    emit_bis(i)
                    emit_att(i)
                self.ppool = list(range(8))
                self.tap("BT0", BT, TBT, [128, 8, NOWN])
                S.barrier()
                if self.stop == 'B1e':
                    self._finish(y)
                    return nc

                self.tmpb = [(self.bf_at(b1e_top + 256 * j, 512), T(f"tb{j}")) for j in range(4)]
                self.gate_mul(w_in, O_BZ, hT, ThT, BT, TBT)
                self.tap("BT", BT, TBT, [128, 8, NOWN])
                S.barrier()
                if self.stop == 'B1':
                    self._finish(y)
                    return nc

                self.top = b1_top
                MT = self.bf(8 * NOWN, (8, NOWN)); TMT = T("MT")
                AT = self.bf(8 * NOWN, (8, NOWN)); TAT = T("AT")
                b2_top = self.top
                mqT = self.bf(8 * NOWN, (8, NOWN)); TmqT = T("mqT")
                memT = self.bf(16 * 256, (16, 256)); TmemT = T("memT")
                kmT = self.bf(8 * 256, (8, 256)); TkmT = T("kmT")
                vm = self.bf(2 * 1024, (2, 1024)); Tvm = T("vm")
                sv_top = self.top
                self.top = base_top
                gbc = self.f32(D); Tg = T("gmem")
                self.dma('scalar', gbc, gmem_d, 'c1', writes=[Tg])
                self.xbufs = [(self.f32(D), T(f"x{i}")) for i in range(3)]
                self.xsbufs = [(self.bf(D), T("xs0")), (self.bf(D), T("xs1"))]
                assert self.top <= kvi_top
                self.top = sv_top
                self.small = [(self.f32(8), T(f"sm{i}")) for i in range(4)]
                self.tmpf = [(self.f32(512), T(f"tf{i}")) for i in range(6)]
                self.tmpb = [(self.bf(512), T(f"tb{i}")) for i in range(4)]
                Pb = [(self.bf(512), T(f"P{i}")) for i in range(2)]
                gms = self.f32(8); Tgms = T("gms")
                self.ts(gms[:, 0:2], cols[:, C_GMQ0:C_GMQ0 + 2], float(256 ** -0.5), None, ALU.mult, None, [Tconst], [Tgms])
                mem_tiles = [(memx[ti * 128:(ti + 1) * 128, :], memT[:, :, ti * 128:(ti + 1) * 128], TmemT) for ti in range(2)]
                for _ in self.hT_stream(mem_tiles, gbc, Tg, [4, 5]):
                    pass
                for ch in range(2):
                    w, Tw = self.wload(w_mkv, ch * 512, 512)
                    for hh in range(2):
                        h = ch * 2 + hh
                        raws = []
                        for dc in range(2):
                            raw, Traw = self.pbank()
                            self.projT(w, Tw, (hh * 2 + dc) * 128, 128, memT, TmemT, raw[:, 0:256], Traw)
                            raws.append((raw[:, 0:256], Traw))
                        rstd, Trs = self.rms_T([r for r, _ in raws], [t for _, t in raws], ones, 1.0 / 256, 256)
                        for dc in range(2):
                            self.stt(kmT[:, 2 * h + dc, :], raws[dc][0], cols[:, C_GMK0 + dc:C_GMK0 + dc + 1], rstd, ALU.mult, ALU.mult,
                                     [raws[dc][1], Trs, Tconst], [TkmT])
                for ch in range(2):
                    w, Tw = self.wload(w_mkv, 1024 + ch * 512, 512)
                    for mt in range(2):
                        pb, Tp = self.pbank()
                        for k in range(16):
                            self.mm(pb, memT[:, k, mt * 128:(mt + 1) * 128], w[:, k, :], k == 0, k == 15, [TmemT, Tw], [Tp])
                        self.copy(vm[:, mt, ch * 512:(ch + 1) * 512], pb, [Tp], [Tvm], eng='scalar')
                for ch in range(2):
                    w, Tw = self.wload(w_in, O_MQ + ch * 512, 512)
                    for hh in range(2):
                        h = ch * 2 + hh
                        for half in range(2):
                            hs = slice(half * 512, (half + 1) * 512)
                            raws = []
                            for dc in range(2):
                                raw, Traw = self.pbank()
                                self.projT(w, Tw, (hh * 2 + dc) * 128, 128, hT[:, :, hs], ThT, raw, Traw)
                                raws.append((raw, Traw))
                            rstd, Trs = self.rms_T([r for r, _ in raws], [t for _, t in raws], ones, 1.0 / 256, 512)
                            for dc in range(2):
                                self.stt(mqT[:, 2 * h + dc, hs], raws[dc][0], gms[:, dc:dc + 1], rstd, ALU.mult, ALU.mult,
                                         [raws[dc][1], Trs, Tgms], [TmqT])
                for h in range(4):
                    for half in range(2):
                        hs = slice(half * 512, (half + 1) * 512)
                        for mt in range(2):
                            lg, Tlg = self.pbank()
                            for dc in range(2):
                                self.mm(lg, kmT[:, 2 * h + dc, mt * 128:(mt + 1) * 128], mqT[:, 2 * h + dc, hs], dc == 0, dc == 1, [TkmT, TmqT], [Tlg])
                            self.act(Pb[mt][0], lg, AF.Exp, [Tlg], [Pb[mt][1]])
                        sm, Tsm = self.pbank()
                        for mt in range(2):
                            self.mm(sm, ones, Pb[mt][0], mt == 0, mt == 1, [Tconst, Pb[mt][1]], [Tsm])
                        rs, Trs = self.tmpf[self.tidx % len(self.tmpf)]
                        self.tidx += 1
                        self.recip(rs, sm, [Tsm], [Trs])
                        for dc in range(2):
                            po, Tpo = self.pbank()
                            for mt in range(2):
                                self.mm(po, vm[:, mt, h * 256 + dc * 128:h * 256 + (dc + 1) * 128], Pb[mt][0], mt == 0, mt == 1, [Tvm, Pb[mt][1]], [Tpo])
                            self.tt(MT[:, 2 * h + dc, hs], po, rs, ALU.mult, [Tpo, Trs], [TMT])
                self.tap("MT0", MT, TMT, [128, 8, NOWN])
                self.gate_mul(w_in, O_MZ, hT, ThT, MT, TMT)
                self.tap("MT", MT, TMT, [128, 8, NOWN])
                S.barrier()
                if self.stop == 'B2':
                    self._finish(y)
                    return nc

                self.top = b2_top
                vln = self.bf(NB * 1024, (NB, 1024)); Tvln = T("vln")
                gv = self.f32(1024); Tgv = T("gv")
                wsT = self.bf(8 * 128, (8, 128)); Tws = T("wsT")
                sv_top = self.top
                self.top = base_top
                wsf = self.f32(8 * 128, (8, 128))
                sbt = self.f32(8 * 128, (8, 128)); Tsbt = T("sbt")
                lng = self.f32(1024); lnb = self.f32(1024); Tln = T("ln")
                assert self.top <= kvi_top
                self.top = sv_top
                self.small = [(self.f32(16), T(f"sm{i}")) for i in range(4)]
                self.tmpf = [(self.f32(1024), T(f"tf{i}")) for i in range(3)]
                self.tmpb = [(self.bf(512), T(f"tb{i}")) for i in range(4)]
                self.dma('sync', wsf, wsT_d.rearrange("p (g t) -> p g t", g=8), 'c2', writes=[Tws])
                self.dma('sync', sbt, sbt_d.rearrange("p (g t) -> p g t", g=8), 'c2', writes=[Tsbt])
                self.dma('sync', lng, lng_d, 'c2', writes=[Tln])
                self.dma('sync', lnb, lnb_d, 'c2', writes=[Tln])
                S.op('gpsimd', lambda e: e.affine_select(out=wsf, in_=wsf, pattern=[[0, 8], [1, 128]], compare_op=ALU.is_ge,
                                                         fill=0.0, base=0, channel_multiplier=-1), [Tws], [Tws])
                self.copy(wsT, wsf, [Tws], [Tws])
                for ch in range(2):
                    w, Tw = self.wload(w_in, O_AU + ch * 512, 512)
                    for cc in range(4):
                        c = ch * 4 + cc
                        for half in range(2):
                            hs = slice(half * 512, (half + 1) * 512)
                            raw, Traw = self.pbank()
                            self.projT(w, Tw, cc * 128, 128, hT[:, :, hs], ThT, raw, Traw)
                            self.act(AT[:, c, hs], raw, AF.Gelu, [Traw], [TAT])
                wv0, Twv0 = self.wload(w_in, O_AV, 512)
                wv1, Twv1 = self.wload(w_in, O_AV + 512, 512)
                for i in range(NB):
                    for ch, (w, Tw) in enumerate(((wv0, Twv0), (wv1, Twv1))):
                        pb, Tp = self.pbank()
                        for k in range(16):
                            self.mm(pb, hT[:, k, i * 128:(i + 1) * 128], w[:, k, :], k == 0, k == 15, [ThT, Tw], [Tp])
                        self.act(gv[:, ch * 512:(ch + 1) * 512], pb, AF.Gelu, [Tp], [Tgv])
                    sm, Tsm = self.small[self.sidx % len(self.small)]
                    self.sidx += 1
                    S.op('vector', lambda e, sm=sm: e.bn_stats(out=sm[:, 0:6], in_=gv[:, 0:512]), [Tgv], [Tsm])
                    S.op('vector', lambda e, sm=sm: e.bn_stats(out=sm[:, 6:12], in_=gv[:, 512:1024]), [Tgv], [Tsm])
                    S.op('vector', lambda e, sm=sm: e.bn_aggr(out=sm[:, 12:14], in_=sm[:, 0:12].rearrange("p (a b) -> p a b", a=2)), [Tsm], [Tsm])
                    self.act(sm[:, 14:15], sm[:, 13:14], AF.Sqrt, [Tsm, Tconst], [Tsm], bias=cols[:, C_EPS:C_EPS + 1], scale=1.0)
                    self.recip(sm[:, 15:16], sm[:, 14:15], [Tsm], [Tsm])
                    t1, T1 = self.tmpf[i % 3]
                    self.ts(t1, gv, sm[:, 12:13], sm[:, 15:16], ALU.subtract, ALU.mult, [Tgv, Tsm], [T1])
                    self.tt(t1, t1, lng, ALU.mult, [T1, Tln], [T1])
                    self.tt(vln[:, i, :], t1, lnb, ALU.add, [T1, Tln], [Tvln])
                for i in range(NB):
                    qs = slice(i * 128, (i + 1) * 128)
                    for gh in range(2):
                        pb, Tp = self.pbank()
                        for gg in range(4):
                            g = gh * 4 + gg
                            self.mm(pb[:, gg * 128:(gg + 1) * 128], vln[:, i, g * 128:(g + 1) * 128], wsT[:, g, :], True, True, [Tvln, Tws], [Tp], inc=(gg == 3))
                        t1, T1 = self.tmpf[(i * 2 + gh) % 3]
                        self.tt(t1[:, 0:512], pb, sbt[:, gh * 4:(gh + 1) * 4, :].rearrange("p g t -> p (g t)"), ALU.add, [Tp, Tsbt], [T1])
                        self.tt(AT[:, gh * 4:(gh + 1) * 4, qs], t1[:, 0:512].rearrange("p (g t) -> p g t", g=4), AT[:, gh * 4:(gh + 1) * 4, qs],
                                ALU.mult, [T1, TAT], [TAT])
                self.tap("AT0", AT, TAT, [128, 8, NOWN])
                self.gate_mul(w_in, O_AZ, hT, ThT, AT, TAT)
                self.tap("AT", AT, TAT, [128, 8, NOWN])
                S.barrier()
                if self.stop == 'B3':
                    self._finish(y)
                    return nc

                self.top = b2_top
                mergedT = self.bf_at(base_top, 16 * NOWN, (16, NOWN)); Tmg = T("merged")
                accm = self.f32(4 * NOWN, (4, NOWN)); Tacc = T("accm")
                sgb = [(self.f32(512), T(f"sg{i}")) for i in range(3)]
                wbr = [(self.bf(8 * 512, (8, 512)), T(f"wbr{i}")) for i in range(2)]
                branches = ((O_GA, w_a, AT, TAT), (O_GB, w_b, BT, TBT), (O_GM, w_m, MT, TMT))
                nbr = 0
                for nq in range(4):
                    for bi, (og, wb_d, XT, TXT) in enumerate(branches):
                        wg, Twg = self.wload(w_in, og + nq * 512, 512)
                        wb_, Twb = wbr[nbr % 2]
                        nbr += 1
                        self.dma('gpsimd', wb_, wb_d.rearrange("(c p) n -> p c n", p=128)[:, :, nq * 512:(nq + 1) * 512], f'wb{nbr % 2}', writes=[Twb])
                        for nn in range(4):
                            hss = [slice(0, 512), slice(512, 1024)]
                            pgs = [self.pbank() for _ in range(2)]
                            for k in range(16):
                                for half in range(2):
                                    self.mm(pgs[half][0], wg[:, k, nn * 128:(nn + 1) * 128], hT[:, k, hss[half]], k == 0, k == 15,
                                            [Twg, ThT], [pgs[half][1]])
                            pys = [self.pbank() for _ in range(2)]
                            for c in range(8):
                                for half in range(2):
                                    self.mm(pys[half][0], wb_[:, c, nn * 128:(nn + 1) * 128], XT[:, c, hss[half]], c == 0, c == 7,
                                            [Twb, TXT], [pys[half][1]])
                            for half in range(2):
                                hs = hss[half]
                                pg, Tpg = pgs[half]
                                py, Tpy = pys[half]
                                sg, Tsg = sgb[(nn * 2 + half) % 3]
                                self.act(sg, pg, AF.Sigmoid, [Tpg], [Tsg])
                                if bi == 0:
                                    self.tt(accm[:, nn, hs], py, sg, ALU.mult, [Tpy, Tsg], [Tacc])
                                else:
                                    self.tt(sg, py, sg, ALU.mult, [Tpy, Tsg], [Tsg])
                                    if bi == 1:
                                        self.tt(accm[:, nn, hs], accm[:, nn, hs], sg, ALU.add, [Tacc, Tsg], [Tacc])
                                    else:
                                        self.tt(mergedT[:, nq * 4 + nn, hs], accm[:, nn, hs], sg, ALU.add, [Tacc, Tsg], [Tmg], eng='gpsimd')
                self.tap("merged", mergedT, Tmg, [128, 16, NOWN])
                S.barrier()
                if self.stop == 'B4':
                    self._finish(y)
                    return nc

                self.top = b2_top
                xo = [(self.f32(512), T(f"xo{i}")) for i in range(3)]
                ot = [(self.f32(512), T(f"ot{i}")) for i in range(3)]
                n_o = 0
                for dch in range(4):
                    ds_ = slice(dch * 512, (dch + 1) * 512)
                    w, Tw = self.wload(w_out, dch * 512, 512)
                    for i in range(NB):
                        rows = slice(i * 128, (i + 1) * 128)
                        xo_, Txo = xo[n_o % 3]
                        ot_, Tot = ot[n_o % 3]
                        self.dma('sync', xo_, xown[rows, ds_], f'xo{n_o % 3}', writes=[Txo])
                        pb, Tp = self.pbank()
                        for k in range(16):
                            self.mm(pb, mergedT[:, k, rows], w[:, k, :], k == 0, k == 15, [Tmg, Tw], [Tp])
                        self.tt(ot_, pb, xo_, ALU.add, [Tp, Txo], [Tot])
                        self.final_toks.append(self.dma('scalar', y[rows, ds_], ot_, f'yo{n_o % 3}', reads=[Tot]))
                        n_o += 1
                self._finish()
            except StopBuild:
                self.S.barrier()
                self._finish(y)
        return nc

    def _finish(self, y=None):
        S = self.S
        if y is not None:
            z, Tz = self.f32_at(0, 8), T("z")
            self.final_toks.append(self.dma('sync', y[0:128, 0:8], z, 'yo0', reads=[Tz]))
        last = {}
        for k, v in self.final_toks:
            last[k] = max(last.get(k, 0), v)
        S.streams['sync'].append((list(last.items()), None, None, 0))
        S.emit()

    def gate_mul(self, wd, col0, hT, ThT, XT, TXT):
        for ch in range(2):
            w, Tw = self.wload(wd, col0 + ch * 512, 512)
            for cc in range(4):
                c = ch * 4 + cc
                for half in range(2):
                    hs = slice(half * 512, (half + 1) * 512)
                    raw, Traw = self.pbank()
                    self.projT(w, Tw, cc * 128, 128, hT[:, :, hs], ThT, raw, Traw)
                    z, Tz = self.tmpb[self.bidx % len(self.tmpb)]
                    self.bidx += 1
                    self.act(z, raw, AF.Silu, [Traw], [Tz])
                    self.tt(XT[:, c, hs], XT[:, c, hs], z, ALU.mult, [TXT, Tz], [TXT])


def _host_consts():
    theta = 500000.0
    invk = np.zeros(128, np.float32)
    invk[:32] = (theta ** (-(np.arange(32) % 16).astype(np.float32) / 16.0)).astype(np.float32)
    invi = np.zeros(128, np.float32)
    for base in (0, 64):
        invi[base:base + 16] = (theta ** (-(np.arange(16) % 8).astype(np.float32) / 8.0)).astype(np.float32)
    Pk = np.zeros((128, 128), np.float32)
    for m in range(16):
        Pk[m, m + 16] = -1.0
        Pk[m + 16, m] = 1.0
    Pi = np.zeros((128, 128), np.float32)
    for base in (0, 64):
        for m in range(8):
            Pi[base + m, base + m + 8] = -1.0
            Pi[base + m + 8, base + m] = 1.0
    blk = np.zeros((128, 128), np.float32)
    blk[:64, :64] = 1.0 / 64
    blk[64:, 64:] = 1.0 / 64
    cmat = np.concatenate([Pk.T, Pi.T, blk], axis=1).astype(np.float32)
    pow2 = np.broadcast_to((2.0 ** -(np.arange(NIT + 2) + 1.0)).astype(np.float32), (128, NIT + 2)).copy()
    return invk, invi, cmat, pow2


_PROG = {}


def _get_prog(dbg=()):
    key = tuple(sorted(dbg))
    if key not in _PROG:
        b = Builder(dbg)
        nc = b.build()
        _PROG[key] = (nc, b)
    return _PROG[key]


def make_in_maps(x, mem, positions, norm_gain, w_in, gmlp_ln_gain, gmlp_ln_bias, spatial_w, spatial_b, w_branch_a,
                 q_norm_gain, k_norm_gain, idx_k_ln_gain, idx_k_ln_bias, w_branch_b, mem_norm_gain, w_mem_kv,
                 mem_q_norm_gain, mem_k_norm_gain, w_branch_m, w_out):
    f = lambda a: np.ascontiguousarray(np.asarray(a), dtype=np.float32)
    x = f(x); mem = f(mem)
    positions = np.ascontiguousarray(np.asarray(positions), dtype=np.int32)
    invk, invi, cmat, pow2 = _host_consts()
    cols = np.zeros((128, NCOLS), np.float32)
    cols[:, C_GQ] = f(q_norm_gain)[0]
    cols[:, C_GK] = f(k_norm_gain)[0]
    cols[:, C_GMQ0] = f(mem_q_norm_gain)[0][:128]
    cols[:, C_GMQ1] = f(mem_q_norm_gain)[0][128:]
    cols[:, C_GMK0] = f(mem_k_norm_gain)[0][:128]
    cols[:, C_GMK1] = f(mem_k_norm_gain)[0][128:]
    cols[:, C_IKG] = np.tile(f(idx_k_ln_gain)[0], 2)
    cols[:, C_IKB] = np.tile(f(idx_k_ln_bias)[0], 2)
    cols[:, C_INVK] = invk
    cols[:, C_INVI] = invi
    cols[:, C_EPS] = EPS
    cols[:, C_NPI] = -PI
    rep = lambda v, n: np.ascontiguousarray(np.broadcast_to(f(v).reshape(1, -1), (128, n)))
    shared = {
        "w_in": f(w_in)[0], "w_a": f(w_branch_a)[0], "w_b": f(w_branch_b)[0], "w_m": f(w_branch_m)[0],
        "w_out": f(w_out)[0], "w_mkv": f(w_mem_kv)[0],
        "gbc": rep(norm_gain[0], D), "gmembc": rep(mem_norm_gain[0], D), "cols": cols,
        "lng": rep(gmlp_ln_gain[0], 1024), "lnb": rep(gmlp_ln_bias[0], 1024),
        "wsT": np.ascontiguousarray(f(spatial_w)[0].transpose(2, 0, 1)).reshape(128, 1024),
        "sbt": rep(f(spatial_b)[0].reshape(-1), 1024),
        "cmat": cmat, "pow2": pow2,
    }
    in_maps = []
    tt = np.arange(128)
    for c in range(8):
        b, j = c // 4, c % 4
        own = np.concatenate([np.arange((j + 4 * i) * 128, (j + 4 * i + 1) * 128) for i in range(NB)])
        cm = np.zeros((128, 4, 128), np.float32)
        for ktl in range(4):
            if ktl > j:
                cm[:, ktl, :] = NEG
            elif ktl == j:
                cm[:, ktl, :] = np.where(tt[None, :] <= tt[:, None], 0.0, NEG)
        m = dict(shared)
        m["xall"] = x[b]
        m["xown"] = np.ascontiguousarray(x[b][own])
        m["posall"] = np.ascontiguousarray(np.broadcast_to(positions[b][None, :], (128, SEQ)))
        m["posown"] = np.ascontiguousarray(np.broadcast_to(positions[b][own][None, :], (128, NOWN)))
        m["memx"] = mem[b]
        m["cmneg"] = cm.reshape(128, 512)
        in_maps.append(m)
    return in_maps


def kernel(**inputs):
    nc, _ = _get_prog()
    in_maps = make_in_maps(**inputs)
    res = run_bass_kernel_spmd(nc, in_maps, core_ids=list(range(8)))
    out = np.zeros((2, SEQ, D), np.float32)
    for c in range(8):
        b, j = c // 4, c % 4
        yc = np.asarray(res.results[c]["y"])
        for i in range(NB):
            g = j + 4 * i
            out[b, g * 128:(g + 1) * 128, :] = yc[i * 128:(i + 1) * 128, :]
    return out
```
